# Optimizing a Trainium2 kernel written in Bass

```python
import math
import jax, jax.numpy as jnp
from jax import lax
import numpy as np

D_MODEL = 2048
BATCH = 4
SEQ = 4096
DEPTH = 2

CHUNK = 64
N_MIXERS = 2
N_ATTN_LAYERS = (DEPTH + 1) // 2
N_LRU_LAYERS = DEPTH // 2
DIFF_HEAD_DIM = 128
DIFF_HEADS = D_MODEL // (2 * DIFF_HEAD_DIM)
Q_BLOCK = 128
D_RNN = D_MODEL * 5 // 4
LRU_BLOCK_W = 256
LRU_BLOCKS = D_RNN // LRU_BLOCK_W
CONV_WIDTH = 4
LRU_C = 8.0
D_FF = D_MODEL * 11 // 4
ALPHA = (2 * DEPTH) ** 0.25
BETA = (8 * DEPTH) ** -0.25
LN_EPS = 1e-5
RMS_EPS = 1e-5

kernel_name = "hybrid_diffattn_rglru_macaron_deepnorm_adaln"


def layer_norm(x, g, b):
    x32 = x.astype(jnp.float32)
    mu = jnp.mean(x32, axis=-1, keepdims=True)
    var = jnp.mean(jnp.square(x32 - mu), axis=-1, keepdims=True)
    y = (x32 - mu) * lax.rsqrt(var + LN_EPS) * g.astype(jnp.float32) + b.astype(jnp.float32)
    return y.astype(x.dtype)


def modulate(x, shift, scale):
    return x * (1 + scale[:, None, :]) + shift[:, None, :]


def swiglu(h, w_in, w_out):
    a, u = jnp.split(h @ w_in, 2, axis=-1)
    return (jax.nn.silu(a) * u) @ w_out


def diff_attention(h, w_qkv, w_o, lq1, lk1, lq2, lk2, subln_g, lambda_init):
    B, S, _ = h.shape
    H, d = DIFF_HEADS, DIFF_HEAD_DIM
    q, k, v = jnp.split(h @ w_qkv, 3, axis=-1)
    q = q.reshape(B, S, H, 2, d)
    k = k.reshape(B, S, H, 2, d)
    v = v.reshape(B, S, H, 2 * d)
    lam = (jnp.exp(jnp.sum(lq1.astype(jnp.float32) * lk1.astype(jnp.float32)))
           - jnp.exp(jnp.sum(lq2.astype(jnp.float32) * lk2.astype(jnp.float32)))
           + lambda_init)
    scale = d ** -0.5
    k_chunk = jnp.arange(S) // CHUNK
    n_qb = S // Q_BLOCK
    q_blocks = q.reshape(B, n_qb, Q_BLOCK, H, 2, d).transpose(1, 0, 2, 3, 4, 5)
    starts = jnp.arange(n_qb, dtype=jnp.int32) * Q_BLOCK

    def one_block(args):
        q_blk, start = args
        s = jnp.einsum('bqhjd,bkhjd->bhjqk', q_blk, k).astype(jnp.float32) * scale
        q_chunk = (start + jnp.arange(Q_BLOCK)) // CHUNK
        mask = k_chunk[None, :] <= q_chunk[:, None]
        s = jnp.where(mask, s, -jnp.inf)
        p = jax.nn.softmax(s, axis=-1)
        a = p[:, :, 0] - lam * p[:, :, 1]
        return jnp.einsum('bhqk,bkhe->bqhe', a.astype(v.dtype), v)

    o = lax.map(one_block, (q_blocks, starts))
    o = o.transpose(1, 0, 2, 3, 4).reshape(B, S, H, 2 * d)
    o32 = o.astype(jnp.float32)
    o32 = o32 * lax.rsqrt(jnp.mean(jnp.square(o32), axis=-1, keepdims=True) + RMS_EPS)
    o32 = o32 * subln_g.astype(jnp.float32) * (1.0 - lambda_init)
    return o32.astype(h.dtype).reshape(B, S, H * 2 * d) @ w_o


def rglru_block(h, w_in, conv_w, conv_b, ga_w, ga_b, gx_w, gx_b, lam_param, w_out):
    B, S, _ = h.shape
    g_branch, x_branch = jnp.split(h @ w_in, 2, axis=-1)
    g_branch = jax.nn.gelu(g_branch)
    xc = lax.conv_general_dilated(
        x_branch, conv_w[:, None, :].astype(x_branch.dtype), window_strides=(1,),
        padding=[(CONV_WIDTH - 1, 0)], dimension_numbers=('NWC', 'WIO', 'NWC'),
        feature_group_count=D_RNN) + conv_b
    xb = xc.reshape(B, S, LRU_BLOCKS, LRU_BLOCK_W)
    r = jax.nn.sigmoid(jnp.einsum('bsnc,ncd->bsnd', xb, ga_w).reshape(B, S, D_RNN) + ga_b)
    i = jax.nn.sigmoid(jnp.einsum('bsnc,ncd->bsnd', xb, gx_w).reshape(B, S, D_RNN) + gx_b)
    log_a = -LRU_C * r.astype(jnp.float32) * jax.nn.softplus(-lam_param.astype(jnp.float32))
    a = jnp.exp(log_a)
    mult = jnp.sqrt(jnp.maximum(-jnp.expm1(2.0 * log_a), 0.0))
    bx = xc.astype(jnp.float32) * i.astype(jnp.float32) * mult

    def combine(e1, e2):
        a1, b1 = e1
        a2, b2 = e2
        return a1 * a2, a2 * b1 + b2

    _, hs = lax.associative_scan(combine, (a, bx), axis=1)
    return (hs.astype(h.dtype) * g_branch) @ w_out


def setup_inputs(seed: int = 0) -> dict:
    key = jax.random.key(seed)
    ks = jax.random.split(key, 24)
    nrm = jax.random.normal
    f32 = jnp.float32
    x = nrm(ks[0], (BATCH, SEQ, D_MODEL), f32)
    c = nrm(ks[1], (BATCH, D_MODEL), f32)
    ada_w = nrm(ks[2], (DEPTH, D_MODEL, 9 * D_MODEL), f32) * (0.1 * D_MODEL ** -0.5)
    ada_b = nrm(ks[3], (DEPTH, 9 * D_MODEL), f32) * 0.01
    ln_g = 1.0 + 0.02 * nrm(ks[4], (DEPTH, 3, D_MODEL), f32)
    ln_b = 0.02 * nrm(ks[5], (DEPTH, 3, D_MODEL), f32)
    ffn_w_in = nrm(ks[6], (DEPTH, 2, D_MODEL, 2 * D_FF), f32) * D_MODEL ** -0.5
    ffn_w_out = nrm(ks[7], (DEPTH, 2, D_FF, D_MODEL), f32) * (D_FF ** -0.5 * BETA)
    attn_w_qkv = nrm(ks[8], (N_ATTN_LAYERS, D_MODEL, 3 * D_MODEL), f32) * D_MODEL ** -0.5
    attn_w_o = nrm(ks[9], (N_ATTN_LAYERS, D_MODEL, D_MODEL), f32) * (D_MODEL ** -0.5 * BETA)
    attn_lambda_q1 = 0.1 * nrm(ks[10], (N_ATTN_LAYERS, DIFF_HEAD_DIM), f32)
    attn_lambda_k1 = 0.1 * nrm(ks[11], (N_ATTN_LAYERS, DIFF_HEAD_DIM), f32)
    attn_lambda_q2 = 0.1 * nrm(ks[12], (N_ATTN_LAYERS, DIFF_HEAD_DIM), f32)
    attn_lambda_k2 = 0.1 * nrm(ks[13], (N_ATTN_LAYERS, DIFF_HEAD_DIM), f32)
    attn_subln_g = 1.0 + 0.02 * nrm(ks[14], (N_ATTN_LAYERS, 2 * DIFF_HEAD_DIM), f32)
    lru_w_in = nrm(ks[15], (N_LRU_LAYERS, D_MODEL, 2 * D_RNN), f32) * D_MODEL ** -0.5
    lru_conv_w = nrm(ks[16], (N_LRU_LAYERS, CONV_WIDTH, D_RNN), f32) * CONV_WIDTH ** -0.5
    lru_conv_b = 0.01 * nrm(ks[17], (N_LRU_LAYERS, D_RNN), f32)
    lru_gate_a_w = nrm(ks[18], (N_LRU_LAYERS, LRU_BLOCKS, LRU_BLOCK_W, LRU_BLOCK_W), f32) * LRU_BLOCK_W ** -0.5
    lru_gate_a_b = 0.01 * nrm(ks[19], (N_LRU_LAYERS, D_RNN), f32)
    lru_gate_x_w = nrm(ks[20], (N_LRU_LAYERS, LRU_BLOCKS, LRU_BLOCK_W, LRU_BLOCK_W), f32) * LRU_BLOCK_W ** -0.5
    lru_gate_x_b = 0.01 * nrm(ks[21], (N_LRU_LAYERS, D_RNN), f32)
    a_c = jax.random.uniform(ks[22], (N_LRU_LAYERS, D_RNN), f32, 0.9, 0.999)
    a_base = a_c ** (1.0 / LRU_C)
    lru_lambda = jnp.log(a_base) - jnp.log1p(-a_base)
    lru_w_out = nrm(ks[23], (N_LRU_LAYERS, D_RNN, D_MODEL), f32) * (D_RNN ** -0.5 * BETA)
    return {"x": x, "c": c, "ada_w": ada_w, "ada_b": ada_b, "ln_g": ln_g, "ln_b": ln_b,
            "ffn_w_in": ffn_w_in, "ffn_w_out": ffn_w_out,
            "attn_w_qkv": attn_w_qkv, "attn_w_o": attn_w_o,
            "attn_lambda_q1": attn_lambda_q1, "attn_lambda_k1": attn_lambda_k1,
            "attn_lambda_q2": attn_lambda_q2, "attn_lambda_k2": attn_lambda_k2,
            "attn_subln_g": attn_subln_g,
            "lru_w_in": lru_w_in, "lru_conv_w": lru_conv_w, "lru_conv_b": lru_conv_b,
            "lru_gate_a_w": lru_gate_a_w, "lru_gate_a_b": lru_gate_a_b,
            "lru_gate_x_w": lru_gate_x_w, "lru_gate_x_b": lru_gate_x_b,
            "lru_lambda": lru_lambda, "lru_w_out": lru_w_out}


def reference(x, c, ada_w, ada_b, ln_g, ln_b, ffn_w_in, ffn_w_out,
              attn_w_qkv, attn_w_o, attn_lambda_q1, attn_lambda_k1, attn_lambda_q2,
              attn_lambda_k2, attn_subln_g,
              lru_w_in, lru_conv_w, lru_conv_b, lru_gate_a_w, lru_gate_a_b,
              lru_gate_x_w, lru_gate_x_b, lru_lambda, lru_w_out):
    c_act = jax.nn.silu(c)
    for i in range(DEPTH):
        mod = c_act @ ada_w[i] + ada_b[i]
        sh1, sc1, g1, sh2, sc2, g2, sh3, sc3, g3 = jnp.split(mod, 9, axis=-1)
        y = swiglu(modulate(x, sh1, sc1), ffn_w_in[i, 0], ffn_w_out[i, 0])
        x = layer_norm(ALPHA * x + 0.5 * (1 + g1)[:, None, :] * y, ln_g[i, 0], ln_b[i, 0])
        h = modulate(x, sh2, sc2)
        j = i // N_MIXERS
        if i % N_MIXERS == 0:
            lambda_init = 0.8 - 0.6 * math.exp(-0.3 * i)
            y = diff_attention(h, attn_w_qkv[j], attn_w_o[j], attn_lambda_q1[j], attn_lambda_k1[j],
                               attn_lambda_q2[j], attn_lambda_k2[j], attn_subln_g[j], lambda_init)
        else:
            y = rglru_block(h, lru_w_in[j], lru_conv_w[j], lru_conv_b[j], lru_gate_a_w[j],
                            lru_gate_a_b[j], lru_gate_x_w[j], lru_gate_x_b[j], lru_lambda[j],
                            lru_w_out[j])
        x = layer_norm(ALPHA * x + (1 + g2)[:, None, :] * y, ln_g[i, 1], ln_b[i, 1])
        y = swiglu(modulate(x, sh3, sc3), ffn_w_in[i, 1], ffn_w_out[i, 1])
        x = layer_norm(ALPHA * x + 0.5 * (1 + g3)[:, None, :] * y, ln_g[i, 2], ln_b[i, 2])
    return x
```

```python
import math
from contextlib import ExitStack

import numpy as np
import concourse.bass as bass
import concourse.mybir as mybir
from concourse.bass_utils import run_bass_kernel_spmd

F32 = mybir.dt.float32
BF16 = mybir.dt.bfloat16
AF = mybir.ActivationFunctionType
ALU = mybir.AluOpType

D = 2048
KD = 16
DFF = 5632
KF = 44
DRNN = 2560
KR = 20
DEPTH = 2
ALPHA = (2 * DEPTH) ** 0.25
LN_EPS = 1e-5
EPS_P = LN_EPS / (ALPHA * ALPHA)
RMS_EPS = 1e-5
TB = 1024
TT = 512
LRU_C = 8.0


class Res:
    __slots__ = ("name", "last_w", "readers")

    def __init__(self, name):
        self.name = name
        self.last_w = None
        self.readers = []


class Prog:
    ENGS = ("pe", "act", "dve", "pool", "sp")

    def __init__(self, nc, stack):
        self.nc = nc
        self.stack = stack
        self.e = dict(pe=nc.tensor, act=nc.scalar, dve=nc.vector, pool=nc.gpsimd, sp=nc.sync)
        self.q = {k: [] for k in self.ENGS}
        self.cnt = {k: 0 for k in self.ENGS}
        self.csem = {k: stack.enter_context(nc.semaphore("c_" + k)) for k in self.ENGS}
        self.seen = {k: {} for k in self.ENGS}
        self.dsem = {}
        self.dinc = {}

    def _dma_sem(self, key):
        if key not in self.dsem:
            self.dsem[key] = [self.stack.enter_context(self.nc.semaphore("d_" + key)), 0]
        return self.dsem[key]

    def op(self, eng, fn, reads=(), writes=(), dma_key=None, cc=False):
        deps = []
        for r in reads:
            if r.last_w is not None:
                deps.append(r.last_w)
        for w in writes:
            if w.last_w is not None:
                deps.append(w.last_w)
            deps.extend(w.readers)
        waits = {}
        for d in deps:
            if d[0] == "c":
                _, deng, idx = d
                if deng == eng and dma_key is None and eng in ("pe", "sp"):
                    continue
                sem, val, skey = self.csem[deng], idx, "c_" + deng
            else:
                _, key, c = d
                sem, val, skey = self.dsem[key][0], self.dinc.get(key, 16) * c, "d_" + key
            if self.seen[eng].get(skey, 0) >= val:
                continue
            if skey not in waits or waits[skey][1] < val:
                waits[skey] = (sem, val)
        for skey, (sem, val) in waits.items():
            self.seen[eng][skey] = val
        wl = list(waits.values())
        if dma_key is None:
            self.cnt[eng] += 1
            tok = ("c", eng, self.cnt[eng])
            mysem, inc = self.csem[eng], 1
        else:
            ds = self._dma_sem(dma_key)
            ds[1] += 1
            tok = ("d", dma_key, ds[1])
            mysem, inc = ds[0], 16
            if cc:
                self.dinc[dma_key] = 1
                inc = None
        engobj = self.e[eng]

        def emit():
            for sem, val in wl:
                engobj.wait_ge(sem, val)
            ins = fn(engobj)
            if inc is None:
                ins.then_inc(mysem)
            else:
                ins.then_inc(mysem, inc)

        self.q[eng].append(emit)
        for w in writes:
            w.last_w = tok
            w.readers = []
        for r in reads:
            if r not in writes:
                r.readers.append(tok)
        return tok

    def finish(self, final_res):
        wl = []
        for r in final_res:
            for d in ([r.last_w] if r.last_w else []) + list(r.readers):
                if d[0] == "c":
                    wl.append((self.csem[d[1]], d[2]))
                else:
                    wl.append((self.dsem[d[1]][0], self.dinc.get(d[1], 16) * d[2]))
        mx = {}
        for sem, val in wl:
            k = id(sem)
            if k not in mx or mx[k][1] < val:
                mx[k] = (sem, val)
        wl = list(mx.values())
        nc, q = self.nc, self.q
        with nc.Block() as block:
            @block.tensor
            def _(eng):
                for f in q["pe"]:
                    f()

            @block.scalar
            def _(eng):
                for f in q["act"]:
                    f()

            @block.vector
            def _(eng):
                for f in q["dve"]:
                    f()

            @block.gpsimd
            def _(eng):
                for f in q["pool"]:
                    f()

            @block.sync
            def _(eng):
                for f in q["sp"]:
                    f()
                for sem, val in wl:
                    eng.wait_ge(sem, val)


class Buf:
    __slots__ = ("ap", "r", "key")

    def __init__(self, ap, name):
        self.ap = ap
        self.r = Res(name)
        self.key = name


class MK:
    def __init__(self, NT, sub_list, paired=False):
        self.paired = paired
        self.NT = NT
        self.NB = NT // TB
        self.sub_list = sub_list
        self.nc = bass.Bass("TRN2", target_bir_lowering=False)
        self.st = ExitStack()
        self.P = Prog(self.nc, self.st)
        self._rr = {}
        self.in_names = []

    def din(self, name, shape, dt=F32):
        self.in_names.append(name)
        return self.nc.dram_tensor(name, list(shape), dt, kind="ExternalInput").ap()

    def dscr(self, name, shape, dt):
        return self.nc.dram_tensor(name, list(shape), dt, kind="Internal").ap()

    def sb(self, name, shape, dt):
        t = self.st.enter_context(self.nc.sbuf_tensor(name, list(shape), dt))
        return t

    def rot(self, name, n):
        i = self._rr.get(name, 0)
        self._rr[name] = i + 1
        return i % n

    def dres(self, name):
        if not hasattr(self, "_dres"):
            self._dres = {}
        if name not in self._dres:
            self._dres[name] = Res(name)
        return self._dres[name]

    def alloc(self):
        nc, st = self.nc, self.st
        NT = self.NT
        subs = set(self.sub_list)
        layers = sorted(set(l for l, _ in subs))
        self.xT = self.din("xT", [KD, 128, NT])
        self.cT = self.din("cT", [128, KD])
        self.ada_w = {l: self.din(f"ada_w_{l}", [D, 9 * D]) for l in layers}
        self.ada_bT = self.din("ada_bT", [128, DEPTH, 144])
        self.lnT = self.din("lnT", [128, DEPTH, 3, 2, KD])
        self.ffn_w_in, self.ffn_w_out = {}, {}
        for (l, sb_) in sorted(subs):
            if sb_ in (0, 2):
                wi = 0 if sb_ == 0 else 1
                self.ffn_w_in[(l, wi)] = self.din(f"ffn_w_in_{l}_{wi}", [D, 2 * DFF])
                self.ffn_w_out[(l, wi)] = self.din(f"ffn_w_out_{l}_{wi}", [DFF, D])
        if (0, 1) in subs:
            self.attn_w_qkv = self.din("attn_w_qkv", [D, 3 * D])
            self.attn_w_o = self.din("attn_w_o", [D, D])
            self.lamv = self.din("lamv", [128, 4, 128])
            self.sublnT = self.din("sublnT", [128, 2])
        if (1, 1) in subs:
            self.lru_w_in = self.din("lru_w_in", [D, 2 * DRNN])
            self.lru_w_out = self.din("lru_w_out", [DRNN, D])
            self.lru_ga = self.din("lru_ga", [10, 256, 256])
            self.lru_gx = self.din("lru_gx", [10, 256, 256])
            self.cwT = self.din("cwT", [128, KR, 4])
            self.lruv = self.din("lruv", [128, 4, KR])
        self.outT = nc.dram_tensor("outT", [KD, 128, NT], F32, kind="ExternalOutput").ap()
        self.XS = self.dscr("XS", [KD, 128, NT], F32)
        self.HS = self.dscr("HS", [KD, 128, NT], BF16)
        self.ZS = self.dscr("ZS", [KD, 128, TB], F32)
        self.QKS = self.dscr("QKS", [16, 128, NT], BF16)
        self.KIN = nc.dram_tensor("KIN", [D, NT], BF16).ap()
        self.VIN = nc.dram_tensor("VIN", [NT, D], BF16).ap()
        if self.paired:
            self.pmask = self.din("pmask", [128, 1])
            self.KOUT = [nc.dram_tensor(f"KOUT{i}", [1024, NT], BF16).ap() for i in range(D // 512)]
            self.VOUT = [nc.dram_tensor(f"VOUT{i}", [1024, D], BF16).ap() for i in range(NT // 512)]
            self.LIN = nc.dram_tensor("LIN", [128, 128], F32).ap()
            self.LOUT = nc.dram_tensor("LOUT", [256, 128], F32).ap()
        self.ONS = self.dscr("ONS", [KD, 128, NT], BF16)
        self.HT_t = self.sb("HT", [128, KD * TB], BF16)
        self.GT_t = self.sb("GT", [128, KF * TB], BF16)
        self.WS_t = [self.sb(f"WS{i}", [128, KF * 256], BF16) for i in range(2)]
        self.HT = Buf(self.HT_t[:].rearrange("p (k n) -> p k n", k=KD), "HT")
        self.GT = Buf(self.GT_t[:].rearrange("p (k n) -> p k n", k=KF), "GT")
        self.WS = [Buf(self.WS_t[i][:], f"WS{i}") for i in range(2)]
        self.WSB = [Res(f"WS{i}b") for i in range(2)]
        self.SA = [Buf(self.sb(f"SA{i}", [128, TT], F32)[:], f"SA{i}") for i in range(2)]
        self.MOD = Buf(self.sb("MOD", [128, DEPTH, 9, KD], F32)[:], "MOD")
        self.SC1P = Buf(self.sb("SC1P", [128, DEPTH, 3, KD], F32)[:], "SC1P")
        self.GATE = Buf(self.sb("GATE", [128, DEPTH, 3, KD], F32)[:], "GATE")
        self.LN = Buf(self.sb("LN", [128, DEPTH, 3, 2, KD], F32)[:], "LN")
        self.ADB = Buf(self.sb("ADB", [128, DEPTH, 144], F32)[:], "ADB")
        self.CT = Buf(self.sb("CTs", [128, KD], F32)[:], "CT")
        self.CA = Buf(self.sb("CA", [128, KD], BF16)[:], "CA")
        self.ONES = Buf(self.sb("ONES", [128, 128], F32)[:], "ONES")
        self.ONESB = Buf(self.sb("ONESB", [128, 128], BF16)[:], "ONESB")
        self.EPSP = Buf(self.sb("EPSP", [128, 1], F32)[:], "EPSP")
        self.RMSE = Buf(self.sb("RMSE", [128, 1], F32)[:], "RMSE")
        self.PS = []
        for i in range(8):
            t = st.enter_context(nc.psum_tensor(f"PS{i}", [128, TT], F32))
            self.PS.append(Buf(t[:], f"PS{i}"))
        self._ov_off = 0
        self.ov = {}

        def ovf32(name, n):
            a = self.HT_t[:, self._ov_off:self._ov_off + 2 * n].bitcast(F32)
            self._ov_off += 2 * n
            self.ov[name] = Buf(a, name)
            return self.ov[name]

        def ovbf(name, n):
            a = self.HT_t[:, self._ov_off:self._ov_off + n]
            self._ov_off += n
            self.ov[name] = Buf(a, name)
            return self.ov[name]

        for i in range(2):
            ovf32(f"XI{i}", TT)
            ovf32(f"ZO{i}", TT)
            ovf32(f"SQ{i}", TT)
            ovf32(f"T1{i}", TT)
            ovf32(f"XO{i}", TT)
            ovbf(f"HO{i}", TT)
        ovf32("S", TB)
        ovf32("Q", TB)
        for nm in ("RSTD", "NMR"):
            self.ov[nm] = Buf(self.sb(nm, [128, TB], F32)[:], nm)
        assert self._ov_off <= KD * TB, self._ov_off
        self.ov_fresh = set()

    def ovw(self, b):
        if b.key in self.ov_fresh:
            self.ov_fresh.discard(b.key)
            return [b.r, self.HT.r]
        return [b.r]

    def ht_load_writes(self):
        self.ov_fresh = set(self.ov.keys())
        return [self.HT.r] + [b.r for b in self.ov.values()]

    def stage_consts(self):
        P = self.P
        P.op("dve", lambda e: e.memset(self.ONES.ap, 1.0), writes=[self.ONES.r])
        P.op("dve", lambda e: e.memset(self.ONESB.ap, 1.0), writes=[self.ONESB.r])
        P.op("dve", lambda e: e.memset(self.EPSP.ap, EPS_P), writes=[self.EPSP.r])
        P.op("dve", lambda e: e.memset(self.RMSE.ap, RMS_EPS), writes=[self.RMSE.r])
        P.op("sp", lambda e: e.dma_start(out=self.CT.ap, in_=self.cT), writes=[self.CT.r], dma_key="CT")
        P.op("sp", lambda e: e.dma_start(out=self.ADB.ap, in_=self.ada_bT), writes=[self.ADB.r], dma_key="ADB")
        P.op("sp", lambda e: e.dma_start(out=self.LN.ap, in_=self.lnT), writes=[self.LN.r], dma_key="LN")
        P.op("act", lambda e: e.activation(out=self.CA.ap, in_=self.CT.ap, func=AF.Silu),
             reads=[self.CT.r], writes=[self.CA.r])

    def stage_ada(self, layer):
        P = self.P
        wv = self.ada_w[layer].rearrange("(kc p) n -> p kc n", p=128)
        ps = self.PS[7]
        for gq in range(36):
            s = self.rot("ws", 2)
            ws = self.WS[s]
            wsv = ws.ap[:, 0:KD * 512].rearrange("p (k n) -> p k n", k=KD)
            P.op("pool", lambda e, wsv=wsv, gq=gq: e.dma_start(out=wsv, in_=wv[:, :, 512 * gq:512 * gq + 512]),
                 writes=[ws.r, self.WSB[s]], dma_key=ws.key)

            def mm(e, wsv=wsv, gq=gq):
                ins = None
                for c4 in range(4):
                    j = 4 * gq + c4
                    for k in range(KD):
                        ins = e.matmul(ps.ap[:, j:j + 1], wsv[:, k, 128 * c4:128 * c4 + 128],
                                       self.CA.ap[:, k:k + 1], start=(k == 0), stop=(k == KD - 1))
                return ins
            P.op("pe", mm, reads=[ws.r, self.WSB[s], self.CA.r], writes=[ps.r])
        mod_l = self.MOD.ap[:, layer].rearrange("p a k -> p (a k)")
        P.op("dve", lambda e: e.tensor_tensor(out=mod_l, in0=ps.ap[:, 0:144], in1=self.ADB.ap[:, layer], op=ALU.add),
             reads=[ps.r, self.ADB.r], writes=[self.MOD.r])
        for s in range(3):
            w = 1.0 if s == 1 else 0.5
            P.op("dve", lambda e, s=s: e.tensor_scalar(out=self.SC1P.ap[:, layer, s], in0=self.MOD.ap[:, layer, 3 * s + 1],
                                                       scalar1=1.0, scalar2=None, op0=ALU.add),
                 reads=[self.MOD.r], writes=[self.SC1P.r])
            P.op("dve", lambda e, s=s, w=w: e.tensor_scalar(out=self.GATE.ap[:, layer, s], in0=self.MOD.ap[:, layer, 3 * s + 2],
                                                            scalar1=1.0, scalar2=w / ALPHA, op0=ALU.add, op1=ALU.mult),
                 reads=[self.MOD.r], writes=[self.GATE.r])

    def stage_prologue(self, layer, sub):
        P = self.P
        self.ht_load_writes()
        for b in range(self.NB):
            for m in range(KD):
                for t in range(TB // TT):
                    i = self.rot("pro", 2)
                    xi, ho = self.ov[f"XI{i}"], self.ov[f"HO{i}"]
                    c0 = b * TB + t * TT
                    P.op("sp", lambda e, xi=xi, m=m, c0=c0: e.dma_start(out=xi.ap, in_=self.xT[m, :, c0:c0 + TT]),
                         writes=self.ovw(xi), dma_key=xi.key)
                    P.op("act", lambda e, xi=xi, ho=ho, m=m: e.activation(
                        out=ho.ap, in_=xi.ap, func=AF.Identity,
                        scale=self.SC1P.ap[:, layer, sub, m:m + 1], bias=self.MOD.ap[:, layer, 3 * sub, m:m + 1]),
                        reads=[xi.r, self.SC1P.r, self.MOD.r], writes=self.ovw(ho))
                    P.op("sp", lambda e, ho=ho, m=m, c0=c0: e.dma_start(out=self.HS[m, :, c0:c0 + TT], in_=ho.ap),
                         reads=[ho.r], writes=[self.dres(f"HS{m}_{c0}")], dma_key=ho.key)

    def load_ht(self, b):
        P = self.P
        rl = [self.dres(f"HS{m}_{b * TB + t * TT}") for m in range(KD) for t in range(TB // TT)]
        P.op("sp", lambda e: e.dma_start(out=self.HT.ap, in_=self.HS[:, :, b * TB:(b + 1) * TB].rearrange("k p n -> p k n")),
             reads=rl, writes=self.ht_load_writes(), dma_key="HT")

    def out_proj(self, b, wv, KC, layer, sub, x_src, x_src_res, nxt, last):
        P = self.P
        S, Q, RSTD, NMR = self.ov["S"], self.ov["Q"], self.ov["RSTD"], self.ov["NMR"]
        P.op("dve", lambda e: e.memset(S.ap, 0.0), writes=self.ovw(S))
        P.op("dve", lambda e: e.memset(Q.ap, 0.0), writes=self.ovw(Q))
        steps = [(gq, t, mm) for gq in range(8) for t in range(TB // TT) for mm in range(2)]
        def load_x(step):
            gq, t, mm = step
            m = 2 * gq + mm
            i = self.rot("xi", 2)
            xi = self.ov[f"XI{i}"]
            c0 = b * TB + t * TT
            P.op("sp", lambda e: e.dma_start(out=xi.ap, in_=x_src[m, :, c0:c0 + TT]),
                 reads=[x_src_res(m, c0)], writes=self.ovw(xi), dma_key=xi.key)
            return xi
        xi_next = load_x(steps[0])
        ws = None
        for si, (gq, t, mm) in enumerate(steps):
            m = 2 * gq + mm
            if t == 0 and mm == 0:
                s = self.rot("ws", 2)
                ws = self.WS[s]
                wsv = ws.ap[:, 0:KC * 256].rearrange("p (k n) -> p k n", k=KC)
                wsr = [ws.r, self.WSB[s]]
                P.op("pool", lambda e, wsv=wsv, gq=gq: e.dma_start(out=wsv, in_=wv[:, :, 256 * gq:256 * gq + 256]),
                     writes=wsr, dma_key=ws.key)
            xi = xi_next
            if si + 1 < len(steps):
                xi_next = load_x(steps[si + 1])
            ps = self.PS[self.rot("ps_o", 2)]

            def mmf(e, wsv=wsv, mm=mm, t=t, ps=ps):
                ins = None
                for k in range(KC):
                    ins = e.matmul(ps.ap, wsv[:, k, 128 * mm:128 * mm + 128], self.GT.ap[:, k, t * TT:(t + 1) * TT],
                                   start=(k == 0), stop=(k == KC - 1))
                return ins
            P.op("pe", mmf, reads=wsr + [self.GT.r], writes=[ps.r])
            i = self.rot("zo", 2)
            zo, sq = self.ov[f"ZO{i}"], self.ov[f"SQ{i}"]
            P.op("dve", lambda e, ps=ps, zo=zo, xi=xi, m=m: e.scalar_tensor_tensor(
                out=zo.ap, in0=ps.ap, scalar=self.GATE.ap[:, layer, sub, m:m + 1], in1=xi.ap, op0=ALU.mult, op1=ALU.add),
                reads=[ps.r, xi.r, self.GATE.r], writes=self.ovw(zo))
            P.op("sp", lambda e, zo=zo, m=m, t=t: e.dma_start(out=self.ZS[m, :, t * TT:(t + 1) * TT], in_=zo.ap),
                 reads=[zo.r], writes=[self.dres(f"ZS{m}_{t}")], dma_key=zo.key)
            P.op("act", lambda e, zo=zo, sq=sq: e.activation(out=sq.ap, in_=zo.ap, func=AF.Square),
                 reads=[zo.r], writes=self.ovw(sq))
            P.op("dve", lambda e, zo=zo, t=t: e.tensor_tensor(out=S.ap[:, t * TT:(t + 1) * TT], in0=S.ap[:, t * TT:(t + 1) * TT],
                                                             in1=zo.ap, op=ALU.add),
                 reads=[zo.r], writes=[S.r])
            P.op("dve", lambda e, sq=sq, t=t: e.tensor_tensor(out=Q.ap[:, t * TT:(t + 1) * TT], in0=Q.ap[:, t * TT:(t + 1) * TT],
                                                             in1=sq.ap, op=ALU.add),
                 reads=[sq.r], writes=[Q.r])
        for t in range(TB // TT):
            pa, pb = self.PS[2], self.PS[3]
            sl = slice(t * TT, (t + 1) * TT)
            P.op("pe", lambda e, sl=sl: e.matmul(pa.ap, self.ONES.ap, S.ap[:, sl], start=True, stop=True),
                 reads=[S.r, self.ONES.r], writes=[pa.r])
            P.op("pe", lambda e, sl=sl: e.matmul(pb.ap, self.ONES.ap, Q.ap[:, sl], start=True, stop=True),
                 reads=[Q.r, self.ONES.r], writes=[pb.r])
            t1 = self.ov["T10"]
            P.op("act", lambda e, sl=sl: e.activation(out=NMR.ap[:, sl], in_=pa.ap, func=AF.Copy, scale=1.0 / D),
                 reads=[pa.r], writes=self.ovw(NMR))
            P.op("dve", lambda e, sl=sl: e.tensor_tensor(out=t1.ap, in0=NMR.ap[:, sl], in1=NMR.ap[:, sl], op=ALU.mult),
                 reads=[NMR.r], writes=self.ovw(t1))
            P.op("dve", lambda e, sl=sl: e.scalar_tensor_tensor(out=RSTD.ap[:, sl], in0=pb.ap, scalar=1.0 / D, in1=t1.ap,
                                                               op0=ALU.mult, op1=ALU.subtract),
                 reads=[pb.r, t1.r], writes=self.ovw(RSTD))
            P.op("act", lambda e, sl=sl: e.activation(out=RSTD.ap[:, sl], in_=RSTD.ap[:, sl], func=AF.Sqrt,
                                                     bias=self.EPSP.ap[:, 0:1], scale=1.0),
                 reads=[self.EPSP.r], writes=[RSTD.r])
            P.op("dve", lambda e, sl=sl: e.reciprocal(out=RSTD.ap[:, sl], in_=RSTD.ap[:, sl]), writes=[RSTD.r])
            P.op("dve", lambda e, sl=sl: e.scalar_tensor_tensor(out=NMR.ap[:, sl], in0=NMR.ap[:, sl], scalar=-1.0,
                                                               in1=RSTD.ap[:, sl], op0=ALU.mult, op1=ALU.mult),
                 reads=[RSTD.r], writes=[NMR.r])
        nsteps = [(m, t) for m in range(KD) for t in range(TB // TT)]

        def load_z(step):
            m, t = step
            i = self.rot("xi", 2)
            zi = self.ov[f"XI{i}"]
            P.op("sp", lambda e: e.dma_start(out=zi.ap, in_=self.ZS[m, :, t * TT:(t + 1) * TT]),
                 reads=[self.dres(f"ZS{m}_{t}")], writes=self.ovw(zi), dma_key=zi.key)
            return zi
        zi_next = load_z(nsteps[0])
        for si, (m, t) in enumerate(nsteps):
            zi = zi_next
            if si + 1 < len(nsteps):
                zi_next = load_z(nsteps[si + 1])
            sl = slice(t * TT, (t + 1) * TT)
            i = self.rot("no", 2)
            t1, xo, ho = self.ov[f"T1{i}"], self.ov[f"XO{i}"], self.ov[f"HO{i}"]
            c0 = b * TB + t * TT
            P.op("dve", lambda e, zi=zi, t1=t1, sl=sl: e.tensor_tensor(out=t1.ap, in0=zi.ap, in1=RSTD.ap[:, sl], op=ALU.mult),
                 reads=[zi.r, RSTD.r], writes=self.ovw(t1))
            P.op("dve", lambda e, t1=t1, sl=sl: e.tensor_tensor(out=t1.ap, in0=t1.ap, in1=NMR.ap[:, sl], op=ALU.add),
                 reads=[NMR.r], writes=[t1.r])
            P.op("act", lambda e, t1=t1, xo=xo, m=m: e.activation(
                out=xo.ap, in_=t1.ap, func=AF.Identity, scale=self.LN.ap[:, layer, sub, 0, m:m + 1],
                bias=self.LN.ap[:, layer, sub, 1, m:m + 1]),
                reads=[t1.r, self.LN.r], writes=self.ovw(xo))
            if last:
                P.op("sp", lambda e, xo=xo, m=m, c0=c0: e.dma_start(out=self.outT[m, :, c0:c0 + TT], in_=xo.ap),
                     reads=[xo.r], writes=[self.dres(f"OUT{m}_{c0}")], dma_key=xo.key)
            else:
                P.op("sp", lambda e, xo=xo, m=m, c0=c0: e.dma_start(out=self.XS[m, :, c0:c0 + TT], in_=xo.ap),
                     reads=[xo.r], writes=[self.dres(f"XS{m}_{c0}")], dma_key=xo.key)
                nl, ns = nxt
                P.op("act", lambda e, xo=xo, ho=ho, m=m: e.activation(
                    out=ho.ap, in_=xo.ap, func=AF.Identity, scale=self.SC1P.ap[:, nl, ns, m:m + 1],
                    bias=self.MOD.ap[:, nl, 3 * ns, m:m + 1]),
                    reads=[xo.r, self.SC1P.r, self.MOD.r], writes=self.ovw(ho))
                P.op("sp", lambda e, ho=ho, m=m, c0=c0: e.dma_start(out=self.HS[m, :, c0:c0 + TT], in_=ho.ap),
                     reads=[ho.r], writes=[self.dres(f"HS{m}_{c0}")], dma_key=ho.key)

    def stage_ffn(self, layer, sub, x_src, x_res_fn, nxt, last):
        P = self.P
        wi = 0 if sub == 0 else 1
        w_in = self.ffn_w_in[(layer, wi)].rearrange("(kc p) n -> p kc n", p=128)
        w_out = self.ffn_w_out[(layer, wi)].rearrange("(kc p) n -> p kc n", p=128)
        for b in range(self.NB):
            self.load_ht(b)
            for gq in range(DFF // 256):
                s = self.rot("ws", 2)
                ws = self.WS[s]
                wsa = ws.ap[:, 0:KD * 256].rearrange("p (k f) -> p k f", k=KD)
                wsu = ws.ap[:, KD * 256:KD * 512].rearrange("p (k f) -> p k f", k=KD)
                wsr = [ws.r, self.WSB[s]]
                P.op("pool", lambda e, wsa=wsa, gq=gq: e.dma_start(out=wsa, in_=w_in[:, :, 256 * gq:256 * gq + 256]),
                     writes=[ws.r], dma_key=ws.key)
                P.op("pool", lambda e, wsu=wsu, gq=gq: e.dma_start(out=wsu, in_=w_in[:, :, DFF + 256 * gq:DFF + 256 * gq + 256]),
                     writes=[self.WSB[s]], dma_key=ws.key + "b")
                for t in range(TB // TT):
                    for sb_ in range(2):
                        j = 2 * gq + sb_
                        pp = self.rot("ps_f", 2)
                        pa, pu = self.PS[4 + 2 * pp], self.PS[5 + 2 * pp]

                        def mmf(e, wsa=wsa, wsu=wsu, sb_=sb_, t=t, pa=pa, pu=pu):
                            ins = None
                            for k in range(KD):
                                ins = e.matmul(pa.ap, wsa[:, k, 128 * sb_:128 * sb_ + 128],
                                               self.HT.ap[:, k, t * TT:(t + 1) * TT], start=(k == 0), stop=(k == KD - 1))
                            for k in range(KD):
                                ins = e.matmul(pu.ap, wsu[:, k, 128 * sb_:128 * sb_ + 128],
                                               self.HT.ap[:, k, t * TT:(t + 1) * TT], start=(k == 0), stop=(k == KD - 1))
                            return ins
                        P.op("pe", mmf, reads=wsr + [self.HT.r], writes=[pa.r, pu.r])
                        sa = self.SA[self.rot("sa", 2)]
                        P.op("act", lambda e, sa=sa, pa=pa: e.activation(out=sa.ap, in_=pa.ap, func=AF.Silu),
                             reads=[pa.r], writes=[sa.r])
                        P.op("dve", lambda e, sa=sa, pu=pu, j=j, t=t: e.tensor_tensor(
                            out=self.GT.ap[:, j, t * TT:(t + 1) * TT], in0=sa.ap, in1=pu.ap, op=ALU.mult),
                            reads=[sa.r, pu.r], writes=[self.GT.r])
            self.out_proj(b, w_out, KF, layer, sub, x_src, x_res_fn, nxt, last)

    def fence(self, old, new):
        if not hasattr(self, "FD"):
            self.FD = Buf(self.sb("FD", [128, 2], F32)[:], "FD")
        self.P.op("dve", lambda e: e.memset(self.FD.ap, 0.0), writes=[self.FD.r] + list(old) + list(new))

    def carve(self, base_t, off, name, shape, dt):
        n = int(np.prod(shape[1:]))
        units = n * (2 if dt == F32 else 1)
        ap = base_t[:, off:off + units]
        if dt == F32:
            ap = ap.bitcast(F32)
        if len(shape) == 3:
            ap = ap.rearrange("p (a b) -> p a b", a=shape[1])
        return Buf(ap, name), off + units

    def stage_attn(self, layer, sub, x_src, x_res_fn, nxt, last):
        P, NT, NB = self.P, self.NT, self.NB
        lam_init = 0.8 - 0.6 * math.exp(-0.3 * layer)
        SCALE = 128 ** -0.5
        wq = self.attn_w_qkv.rearrange("(kc p) n -> p kc n", p=128)
        wo = self.attn_w_o.rearrange("(kc p) n -> p kc n", p=128)
        NQT = NT // TT
        NKT = NT // 128
        LAMV = Buf(self.sb("LAMV", [128, 4, 128], F32)[:], "LAMV")
        ATC = Buf(self.sb("ATC", [128, 8], F32)[:], "ATC")
        SUBG = Buf(self.sb("SUBG", [128, 2], F32)[:], "SUBG")
        QS = [Buf(self.sb(f"QS{i}", [128, TT], BF16)[:], f"QS{i}") for i in range(2)]
        P.op("sp", lambda e: e.dma_start(out=LAMV.ap, in_=self.lamv), writes=[LAMV.r], dma_key="LAMV")
        P.op("sp", lambda e: e.dma_start(out=SUBG.ap, in_=self.sublnT), writes=[SUBG.r], dma_key="SUBG")
        P.op("dve", lambda e: e.tensor_scalar(out=SUBG.ap, in0=SUBG.ap, scalar1=(1.0 - lam_init), scalar2=None, op0=ALU.mult),
             writes=[SUBG.r])
        for q in range(2):
            P.op("dve", lambda e, q=q: e.tensor_tensor(out=LAMV.ap[:, 2 * q], in0=LAMV.ap[:, 2 * q], in1=LAMV.ap[:, 2 * q + 1],
                                                      op=ALU.mult), writes=[LAMV.r])
            P.op("dve", lambda e, q=q: e.reduce_sum(out=ATC.ap[:, q:q + 1], in_=LAMV.ap[:, 2 * q], axis=mybir.AxisListType.X),
                 reads=[LAMV.r], writes=[ATC.r])
        P.op("act", lambda e: e.activation(out=ATC.ap[:, 2:4], in_=ATC.ap[:, 0:2], func=AF.Exp), writes=[ATC.r])
        P.op("dve", lambda e: e.tensor_tensor(out=ATC.ap[:, 4:5], in0=ATC.ap[:, 3:4], in1=ATC.ap[:, 2:3], op=ALU.subtract),
             writes=[ATC.r])
        P.op("dve", lambda e: e.tensor_scalar(out=ATC.ap[:, 5:6], in0=ATC.ap[:, 4:5], scalar1=-lam_init, scalar2=None, op0=ALU.add),
             writes=[ATC.r])
        NEGLAM = ATC.ap[:, 5:6]

        for b in range(NB):
            self.load_ht(b)
            for g in range(16):
                s = self.rot("ws", 2)
                ws = self.WS[s]
                wsr = [ws.r, self.WSB[s]]
                wsv = ws.ap[:, 0:KD * 256].rearrange("p (k f) -> p k f", k=KD)
                P.op("pool", lambda e, wsv=wsv, g=g: e.dma_start(out=wsv, in_=wq[:, :, 256 * g:256 * g + 256]),
                     writes=wsr, dma_key=ws.key)
                for t in range(TB // TT):
                    for sb_ in range(2):
                        ps = self.PS[4 + self.rot("ps_a", 4)]

                        def mmf(e, wsv=wsv, sb_=sb_, t=t, ps=ps):
                            ins = None
                            for k in range(KD):
                                ins = e.matmul(ps.ap, wsv[:, k, 128 * sb_:128 * sb_ + 128], self.HT.ap[:, k, t * TT:(t + 1) * TT],
                                               start=(k == 0), stop=(k == KD - 1))
                            return ins
                        P.op("pe", mmf, reads=wsr + [self.HT.r], writes=[ps.r])
                        qi = self.rot("qs", 2)
                        qs = QS[qi]
                        if qi == 0:
                            P.op("act", lambda e, qs=qs, ps=ps: e.activation(out=qs.ap, in_=ps.ap, func=AF.Copy),
                                 reads=[ps.r], writes=[qs.r])
                        else:
                            P.op("dve", lambda e, qs=qs, ps=ps: e.tensor_copy(out=qs.ap, in_=ps.ap), reads=[ps.r], writes=[qs.r])
                        ch = 2 * g + sb_
                        c0 = b * TB + t * TT
                        if ch < 16:
                            dst = self.QKS[ch, :, c0:c0 + TT]
                        else:
                            dst = self.KIN[(ch - 16) * 128:(ch - 15) * 128, c0:c0 + TT]
                        P.op("sp", lambda e, qs=qs, dst=dst: e.dma_start(out=dst, in_=qs.ap),
                             reads=[qs.r], writes=[self.dres(f"QK{ch}_{c0}")], dma_key=qs.key)
            for gv in range(4):
                s = self.rot("ws", 2)
                ws = self.WS[s]
                wsr = [ws.r, self.WSB[s]]
                wsv = ws.ap[:, 0:KD * 512].rearrange("p (k f) -> p k f", k=KD)
                P.op("pool", lambda e, wsv=wsv, gv=gv: e.dma_start(out=wsv, in_=wq[:, :, 2 * D + 512 * gv:2 * D + 512 * gv + 512]),
                     writes=wsr, dma_key=ws.key)
                for tt in range(TB // 128):
                    ps = self.PS[4 + self.rot("ps_a", 4)]

                    def mmv(e, wsv=wsv, tt=tt, ps=ps):
                        ins = None
                        for k in range(KD):
                            ins = e.matmul(ps.ap, self.HT.ap[:, k, 128 * tt:128 * tt + 128], wsv[:, k, :],
                                           start=(k == 0), stop=(k == KD - 1))
                        return ins
                    P.op("pe", mmv, reads=wsr + [self.HT.r], writes=[ps.r])
                    qi = self.rot("qs", 2)
                    qs = QS[qi]
                    if qi == 0:
                        P.op("act", lambda e, qs=qs, ps=ps: e.activation(out=qs.ap, in_=ps.ap, func=AF.Copy),
                             reads=[ps.r], writes=[qs.r])
                    else:
                        P.op("dve", lambda e, qs=qs, ps=ps: e.tensor_copy(out=qs.ap, in_=ps.ap), reads=[ps.r], writes=[qs.r])
                    tk = b * (TB // 128) + tt
                    P.op("sp", lambda e, qs=qs, tk=tk, gv=gv: e.dma_start(out=self.VIN[tk * 128:(tk + 1) * 128, 512 * gv:512 * gv + 512], in_=qs.ap),
                         reads=[qs.r], writes=[self.dres(f"V{tk}_{gv}")], dma_key=qs.key)

        NPREV = NT // 128 if self.paired else 0
        if self.paired:
            NEGB = Buf(self.sb("NEGB", [128, 1], F32)[:], "NEGB")
            PM = Buf(self.sb("PM", [128, 1], F32)[:], "PM")
            P.op("sp", lambda e: e.dma_start(out=PM.ap, in_=self.pmask), writes=[PM.r], dma_key="PM")
            P.op("dve", lambda e: e.tensor_scalar(out=NEGB.ap, in0=PM.ap, scalar1=-1.0, scalar2=30000.0, op0=ALU.add, op1=ALU.mult),
                 reads=[PM.r], writes=[NEGB.r])
            grp = [[0, 1], [2, 3], [4, 5], [6, 7]]
            for i in range(D // 512):
                rk = [self.dres(f"QK{16 + 4 * i + cc_}_{c0}") for cc_ in range(4) for c0 in range(0, NT, TT)]
                P.op("pool", lambda e, i=i: e.collective_compute("AllGather", ALU.bypass, replica_groups=grp,
                                                                 ins=[self.KIN[512 * i:512 * i + 512, :]], outs=[self.KOUT[i]]),
                     reads=rk, writes=[self.dres(f"KOUT{i}")], dma_key=f"ccK{i}", cc=True)
            for i in range(NT // 512):
                rv = [self.dres(f"V{tk}_{gv}") for tk in range(4 * i, 4 * i + 4) for gv in range(4)]
                P.op("pool", lambda e, i=i: e.collective_compute("AllGather", ALU.bypass, replica_groups=grp,
                                                                 ins=[self.VIN[512 * i:512 * i + 512, :]], outs=[self.VOUT[i]]),
                     reads=rv, writes=[self.dres(f"VOUT{i}")], dma_key=f"ccV{i}", cc=True)

        NKEY = NPREV + NKT
        KH, VH, QT, PT, OST = [], [], [], [], []
        off = 0
        for i in range(2):
            bf, off = self.carve(self.HT_t, off, f"KH{i}", [128, 2, NKEY * 128], BF16)
            KH.append(bf)
        assert off <= KD * TB
        off = 0
        VP = NKEY // 4
        for i in range(2):
            parts = []
            for pp in range(VP):
                bf, off = self.carve(self.GT_t, off, f"VH{i}_{pp}", [128, 4, 256], BF16)
                parts.append(bf)
            VH.append(parts)
        for i in range(2):
            bf, off = self.carve(self.GT_t, off, f"QT{i}", [128, 2, TT], BF16)
            QT.append(bf)
        for i in range(4):
            bf, off = self.carve(self.GT_t, off, f"PT{i}", [128, TT], BF16)
            PT.append(bf)
        for i in range(2):
            bf, off = self.carve(self.GT_t, off, f"OST{i}", [128, 2, TT], BF16)
            OST.append(bf)
        RL, off = self.carve(self.GT_t, off, "RL", [128, TT], F32)
        ONJ = []
        for i in range(2):
            bf, off = self.carve(self.GT_t, off, f"ONJ{i}", [128, 2 * TT], F32)
            ONJ.append(bf)
        OD, off = self.carve(self.GT_t, off, "OD", [128, 2 * TT], F32)
        SQO, off = self.carve(self.GT_t, off, "SQO", [128, 2 * TT], F32)
        RS, off = self.carve(self.GT_t, off, "RS", [128, TT], F32)
        assert off <= KF * TB, off
        scratch = KH + [p_ for v_ in VH for p_ in v_] + QT + PT + OST + [RL, OD, SQO, RS] + ONJ
        old = [self.HT.r, self.GT.r] + [bb.r for bb in self.ov.values()]
        self.fence(old, [x.r for x in scratch])
        for h in range(8):
            i = h % 2
            kh, vh = KH[i], VH[i]
            rk = [self.dres(f"QK{16 + 2 * h + j}_{c0}") for j in range(2) for c0 in range(0, NT, TT)]
            if self.paired:
                r0 = 256 * (h % 2)
                P.op("sp", lambda e, kh=kh, h=h, r0=r0: e.dma_start(
                    out=kh.ap[:, :, 0:NT], in_=self.KOUT[h // 2][r0:r0 + 256, :].rearrange("(j p) n -> p j n", p=128)),
                    reads=[self.dres(f"KOUT{h // 2}")], writes=[kh.r], dma_key=kh.key + "p")
            P.op("sp", lambda e, kh=kh, h=h: e.dma_start(
                out=kh.ap[:, :, NPREV * 128:NPREV * 128 + NT],
                in_=self.KIN[256 * h:256 * h + 256, :].rearrange("(j p) n -> p j n", p=128)),
                reads=rk, writes=[kh.r], dma_key=kh.key)
            for pp in range(VP):
                vp = vh[pp]
                if pp * 4 < NPREV:
                    P.op("sp", lambda e, vp=vp, h=h, pp=pp: e.dma_start(
                        out=vp.ap, in_=self.VOUT[pp][0:512, 256 * h:256 * h + 256].rearrange("(t p) e -> p t e", p=128)),
                        reads=[self.dres(f"VOUT{pp}")], writes=[vp.r], dma_key=vp.key)
                else:
                    po_ = pp - NPREV // 4
                    rv = [self.dres(f"V{tk}_{h // 2}") for tk in range(po_ * 4, po_ * 4 + 4)]
                    P.op("sp", lambda e, vp=vp, h=h, po_=po_: e.dma_start(
                        out=vp.ap, in_=self.VIN[po_ * 512:(po_ + 1) * 512, 256 * h:256 * h + 256].rearrange("(t p) e -> p t e", p=128)),
                        reads=rv, writes=[vp.r], dma_key=vp.key)
            for t in range(NQT):
                qt = QT[self.rot("qt", 2)]
                rq = [self.dres(f"QK{2 * h + j}_{t * TT}") for j in range(2)]
                P.op("sp", lambda e, qt=qt, h=h, t=t: e.dma_start(
                    out=qt.ap, in_=self.QKS[2 * h:2 * h + 2, :, t * TT:(t + 1) * TT].rearrange("j p n -> p j n")),
                    reads=rq, writes=[qt.r], dma_key=qt.key)
                nkt = NPREV + 4 * (t + 1)
                for j in range(2):
                    po = [self.PS[2 + 3 * j], self.PS[3 + 3 * j]]
                    pl = self.PS[4 + 3 * j]
                    pend = []

                    def emit_pv(kt, c0, pt, po=po, pl=pl, vh=vh, nkt=nkt):
                        vp = vh[kt // 4]

                        def pv(e, kt=kt, c0=c0, pt=pt, po=po, pl=pl, vp=vp, nkt=nkt):
                            ins = None
                            for c in range(2):
                                ins = e.matmul(po[c].ap[:, c0:TT], vp.ap[:, kt % 4, 128 * c:128 * c + 128], pt.ap[:, c0:TT],
                                               start=(kt == 0), stop=(kt == nkt - 1))
                            ins = e.matmul(pl.ap[:, c0:TT], self.ONESB.ap, pt.ap[:, c0:TT], start=(kt == 0), stop=(kt == nkt - 1))
                            return ins
                        P.op("pe", pv, reads=[vp.r, pt.r, self.ONESB.r], writes=[po[0].r, po[1].r, pl.r])
                    for kt in range(nkt):
                        dpos = kt - (NPREV + 4 * t)
                        c0 = 128 * dpos if dpos > 0 else 0
                        pss = self.PS[self.rot("ps_s", 2)]
                        pt = PT[self.rot("pt", 4)]
                        P.op("pe", lambda e, pss=pss, kt=kt, c0=c0, j=j, kh=kh, qt=qt: e.matmul(
                            pss.ap[:, c0:TT], kh.ap[:, j, 128 * kt:128 * kt + 128], qt.ap[:, j, c0:TT], start=True, stop=True),
                            reads=[kh.r, qt.r], writes=[pss.r])
                        if kt < NPREV:
                            P.op("act", lambda e, pss=pss, pt=pt: e.activation(
                                out=pt.ap, in_=pss.ap, func=AF.Exp, scale=SCALE, bias=NEGB.ap[:, 0:1]),
                                reads=[pss.r, NEGB.r], writes=[pt.r])
                        else:
                            P.op("act", lambda e, pss=pss, pt=pt, c0=c0: e.activation(
                                out=pt.ap[:, c0:TT], in_=pss.ap[:, c0:TT], func=AF.Exp, scale=SCALE),
                                reads=[pss.r], writes=[pt.r])
                        if dpos >= 0:
                            P.op("pool", lambda e, pt=pt, c0=c0: e.memset(pt.ap[64:128, c0:c0 + 64], 0.0), writes=[pt.r])
                        pend.append((kt, c0, pt))
                        if len(pend) > 1:
                            emit_pv(*pend.pop(0))
                    while pend:
                        emit_pv(*pend.pop(0))
                    P.op("dve", lambda e, pl=pl: e.reciprocal(out=RL.ap, in_=pl.ap), reads=[pl.r], writes=[RL.r])
                    for c in range(2):
                        P.op("dve", lambda e, c=c, j=j, po=po: e.tensor_tensor(out=ONJ[j].ap[:, c * TT:(c + 1) * TT], in0=po[c].ap,
                                                                              in1=RL.ap, op=ALU.mult),
                             reads=[po[c].r, RL.r], writes=[ONJ[j].r])
                P.op("dve", lambda e: e.scalar_tensor_tensor(out=OD.ap, in0=ONJ[1].ap, scalar=NEGLAM, in1=ONJ[0].ap,
                                                             op0=ALU.mult, op1=ALU.add),
                     reads=[ONJ[0].r, ONJ[1].r, ATC.r], writes=[OD.r])
                P.op("act", lambda e: e.activation(out=SQO.ap, in_=OD.ap, func=AF.Square), reads=[OD.r], writes=[SQO.r])
                pss = self.PS[self.rot("ps_s", 2)]

                def msf(e, pss=pss):
                    e.matmul(pss.ap, self.ONES.ap, SQO.ap[:, 0:TT], start=True, stop=False)
                    return e.matmul(pss.ap, self.ONES.ap, SQO.ap[:, TT:2 * TT], start=False, stop=True)
                P.op("pe", msf, reads=[SQO.r, self.ONES.r], writes=[pss.r])
                P.op("act", lambda e, pss=pss: e.activation(out=RS.ap, in_=pss.ap, func=AF.Sqrt, scale=1.0 / 256.0, bias=self.RMSE.ap[:, 0:1]),
                     reads=[pss.r, self.RMSE.r], writes=[RS.r])
                P.op("dve", lambda e: e.reciprocal(out=RS.ap, in_=RS.ap), writes=[RS.r])
                ost = OST[self.rot("ost", 2)]
                for c in range(2):
                    P.op("dve", lambda e, c=c: e.tensor_tensor(out=OD.ap[:, c * TT:(c + 1) * TT], in0=OD.ap[:, c * TT:(c + 1) * TT],
                                                              in1=RS.ap, op=ALU.mult), reads=[RS.r], writes=[OD.r])
                    P.op("act", lambda e, c=c, ost=ost: e.activation(out=ost.ap[:, c], in_=OD.ap[:, c * TT:(c + 1) * TT],
                                                                    func=AF.Identity, scale=SUBG.ap[:, c:c + 1], bias=0.0),
                         reads=[OD.r, SUBG.r], writes=[ost.r])
                P.op("sp", lambda e, ost=ost, h=h, t=t: e.dma_start(
                    out=self.ONS[2 * h:2 * h + 2, :, t * TT:(t + 1) * TT].rearrange("c p n -> p c n"), in_=ost.ap),
                    reads=[ost.r], writes=[self.dres(f"ON{h}_{t}")], dma_key=ost.key)
        self.fence([x.r for x in scratch], [self.HT.r, self.GT.r] + [bb.r for bb in self.ov.values()])
        for b in range(NB):
            rl = [self.dres(f"ON{h}_{b * (TB // TT) + t}") for h in range(8) for t in range(TB // TT)]
            P.op("sp", lambda e, b=b: e.dma_start(out=self.GT.ap[:, 0:KD, :],
                                                 in_=self.ONS[:, :, b * TB:(b + 1) * TB].rearrange("k p n -> p k n")),
                 reads=rl, writes=[self.GT.r], dma_key="GT")
            self.out_proj(b, wo, KD, layer, sub, x_src, x_res_fn, nxt, last)

    def stage_lru(self, layer, sub, x_src, x_res_fn, nxt, last):
        P, NT, NB = self.P, self.NT, self.NB
        w_in = self.lru_w_in.rearrange("(kc p) n -> p kc n", p=128)
        w_out = self.lru_w_out.rearrange("(kc p) n -> p kc n", p=128)
        gav = self.lru_ga.rearrange("n c d -> (n c) d").rearrange("(q p) d -> p q d", p=128)
        gxv = self.lru_gx.rearrange("n c d -> (n c) d").rearrange("(q p) d -> p q d", p=128)
        CW = Buf(self.sb("CW", [128, KR, 4], F32)[:], "CW")
        LV = Buf(self.sb("LV", [128, 4, KR], F32)[:], "LV")
        LC = Buf(self.sb("LC", [128, 4, KR], F32)[:], "LC")
        HALO = Buf(self.sb("HALO", [128, KR, 4], F32)[:], "HALO")
        STATE = Buf(self.sb("STATE", [128, KR], F32)[:], "STATE")
        P.op("sp", lambda e: e.dma_start(out=CW.ap, in_=self.cwT), writes=[CW.r], dma_key="CW")
        P.op("sp", lambda e: e.dma_start(out=LV.ap, in_=self.lruv), writes=[LV.r], dma_key="LV")
        P.op("dve", lambda e: e.memset(HALO.ap, 0.0), writes=[HALO.r])
        P.op("dve", lambda e: e.memset(STATE.ap, 0.0), writes=[STATE.r])
        lam = LV.ap[:, 3]
        P.op("act", lambda e: e.activation(out=LC.ap[:, 0], in_=lam, func=AF.Abs), reads=[LV.r], writes=[LC.r])
        P.op("act", lambda e: e.activation(out=LC.ap[:, 0], in_=LC.ap[:, 0], func=AF.Exp, scale=-1.0), writes=[LC.r])
        P.op("act", lambda e: e.activation(out=LC.ap[:, 0], in_=LC.ap[:, 0], func=AF.Ln, bias=1.0, scale=1.0), writes=[LC.r])
        P.op("dve", lambda e: e.tensor_scalar(out=LC.ap[:, 1], in0=lam, scalar1=-1.0, scalar2=0.0, op0=ALU.mult, op1=ALU.max),
             reads=[LV.r], writes=[LC.r])
        P.op("dve", lambda e: e.tensor_tensor(out=LC.ap[:, 1], in0=LC.ap[:, 1], in1=LC.ap[:, 0], op=ALU.add), writes=[LC.r])
        P.op("dve", lambda e: e.tensor_scalar(out=LC.ap[:, 2], in0=LC.ap[:, 1], scalar1=-LRU_C, scalar2=None, op0=ALU.mult), writes=[LC.r])
        P.op("dve", lambda e: e.tensor_scalar(out=LC.ap[:, 3], in0=LC.ap[:, 1], scalar1=-2.0 * LRU_C, scalar2=None, op0=ALU.mult), writes=[LC.r])
        off = KR * TB
        XB, XC, XCB, GG, RT, IT, GAW, GXW = [], [], [], [], [], [], [], []
        for i in range(2):
            bf, off = self.carve(self.GT_t, off, f"XB{i}", [128, TB + 4], F32); XB.append(bf)
            bf, off = self.carve(self.GT_t, off, f"XC{i}", [128, TB], F32); XC.append(bf)
            bf, off = self.carve(self.GT_t, off, f"XCB{i}", [128, TB], BF16); XCB.append(bf)
            bf, off = self.carve(self.GT_t, off, f"GG{i}", [128, TB], BF16); GG.append(bf)
            bf, off = self.carve(self.GT_t, off, f"RT{i}", [128, TT], F32); RT.append(bf)
            bf, off = self.carve(self.GT_t, off, f"IT{i}", [128, TT], F32); IT.append(bf)
            bf, off = self.carve(self.GT_t, off, f"GAW{i}", [128, 2, 256], BF16); GAW.append(bf)
            bf, off = self.carve(self.GT_t, off, f"GXW{i}", [128, 2, 256], BF16); GXW.append(bf)
        A, off = self.carve(self.GT_t, off, "A", [128, TB], F32)
        HSO, off = self.carve(self.GT_t, off, "HSO", [128, TB], F32)
        assert off <= KF * TB, off
        scratch = XB + XC + XCB + GG + RT + IT + GAW + GXW + [A, HSO]
        self.fence([self.GT.r], [x.r for x in scratch] + [self.GT.r])
        GELU_K = 2.0 * math.sqrt(2.0 / math.pi)

        def lru_block(b, state_only):
            self.load_ht(b)
            for n in range(10):
                s = self.rot("ws", 2)
                ws = self.WS[s]
                wsr = [ws.r, self.WSB[s]]
                wsa = ws.ap[:, 0:KD * 256].rearrange("p (k f) -> p k f", k=KD)
                wsu = ws.ap[:, KD * 256:KD * 512].rearrange("p (k f) -> p k f", k=KD)
                if not state_only:
                    P.op("pool", lambda e, wsa=wsa, n=n: e.dma_start(out=wsa, in_=w_in[:, :, 256 * n:256 * n + 256]),
                         writes=[ws.r], dma_key=ws.key)
                P.op("pool", lambda e, wsu=wsu, n=n: e.dma_start(out=wsu, in_=w_in[:, :, DRNN + 256 * n:DRNN + 256 * n + 256]),
                     writes=[self.WSB[s]], dma_key=ws.key + "b")
                gi = self.rot("gw", 2)
                gaw, gxw = GAW[gi], GXW[gi]
                P.op("pool", lambda e, gaw=gaw, n=n: e.dma_start(out=gaw.ap, in_=gav[:, 2 * n:2 * n + 2, :]), writes=[gaw.r], dma_key=gaw.key)
                P.op("pool", lambda e, gxw=gxw, n=n: e.dma_start(out=gxw.ap, in_=gxv[:, 2 * n:2 * n + 2, :]), writes=[gxw.r], dma_key=gxw.key)
                for sb_ in range(2):
                    c = 2 * n + sb_
                    P.op("dve", lambda e, sb_=sb_, c=c: e.tensor_copy(out=XB[sb_].ap[:, 0:3], in_=HALO.ap[:, c, 0:3]),
                         reads=[HALO.r], writes=[XB[sb_].r])
                for t in range(TB // TT):
                    for sb_ in range(2):
                        pp = self.rot("ps_f", 2)
                        pg, px = self.PS[4 + 2 * pp], self.PS[5 + 2 * pp]

                        def mmf(e, wsa=wsa, wsu=wsu, sb_=sb_, t=t, pg=pg, px=px, state_only=state_only):
                            ins = None
                            for k in range(KD if not state_only else 0):
                                ins = e.matmul(pg.ap, wsa[:, k, 128 * sb_:128 * sb_ + 128], self.HT.ap[:, k, t * TT:(t + 1) * TT],
                                               start=(k == 0), stop=(k == KD - 1))
                            for k in range(KD):
                                ins = e.matmul(px.ap, wsu[:, k, 128 * sb_:128 * sb_ + 128], self.HT.ap[:, k, t * TT:(t + 1) * TT],
                                               start=(k == 0), stop=(k == KD - 1))
                            return ins
                        P.op("pe", mmf, reads=wsr + [self.HT.r], writes=[pg.r, px.r])
                        sa = self.SA[self.rot("sa", 2)]
                        sl = slice(t * TT, (t + 1) * TT)
                        P.op("act", lambda e, px=px, sb_=sb_, t=t: e.activation(out=XB[sb_].ap[:, 3 + t * TT:3 + (t + 1) * TT], in_=px.ap, func=AF.Copy),
                             reads=[px.r], writes=[XB[sb_].r])
                        if state_only:
                            continue
                        P.op("act", lambda e, sa=sa, pg=pg: e.activation(out=sa.ap, in_=pg.ap, func=AF.Square), reads=[pg.r], writes=[sa.r])
                        P.op("dve", lambda e, sa=sa: e.tensor_scalar(out=sa.ap, in0=sa.ap, scalar1=0.044715, scalar2=1.0,
                                                                    op0=ALU.mult, op1=ALU.add), writes=[sa.r])
                        P.op("dve", lambda e, sa=sa, pg=pg: e.tensor_tensor(out=sa.ap, in0=sa.ap, in1=pg.ap, op=ALU.mult),
                             reads=[pg.r], writes=[sa.r])
                        P.op("act", lambda e, sa=sa: e.activation(out=sa.ap, in_=sa.ap, func=AF.Sigmoid, scale=GELU_K), writes=[sa.r])
                        P.op("dve", lambda e, sa=sa, pg=pg, sb_=sb_, sl=sl: e.tensor_tensor(out=GG[sb_].ap[:, sl], in0=sa.ap, in1=pg.ap, op=ALU.mult),
                             reads=[sa.r, pg.r], writes=[GG[sb_].r])
                for sb_ in range(2):
                    c = 2 * n + sb_
                    xb, xc = XB[sb_], XC[sb_]
                    P.op("dve", lambda e, xb=xb, xc=xc, c=c: e.tensor_scalar(out=xc.ap, in0=xb.ap[:, 3:3 + TB], scalar1=CW.ap[:, c, 3:4],
                                                                             scalar2=LV.ap[:, 0, c:c + 1], op0=ALU.mult, op1=ALU.add),
                         reads=[xb.r, CW.r, LV.r], writes=[xc.r])
                    for kk in (2, 1, 0):
                        P.op("dve", lambda e, xb=xb, xc=xc, c=c, kk=kk: e.scalar_tensor_tensor(
                            out=xc.ap, in0=xb.ap[:, kk:kk + TB], scalar=CW.ap[:, c, kk:kk + 1], in1=xc.ap, op0=ALU.mult, op1=ALU.add),
                            reads=[xb.r, CW.r], writes=[xc.r])
                    P.op("dve", lambda e, xb=xb, c=c: e.tensor_copy(out=HALO.ap[:, c, 0:3], in_=xb.ap[:, TB:TB + 3]),
                         reads=[xb.r], writes=[HALO.r])
                    P.op("act", lambda e, xc=xc, sb_=sb_: e.activation(out=XCB[sb_].ap, in_=xc.ap, func=AF.Copy),
                         reads=[xc.r], writes=[XCB[sb_].r])
                for ds_ in range(2):
                    d = 2 * n + ds_
                    for t in range(TB // TT):
                        sl = slice(t * TT, (t + 1) * TT)
                        pr, pi = self.PS[0], self.PS[1]

                        def gmm(e, ds_=ds_, sl=sl, gaw=gaw, gxw=gxw, pr=pr, pi=pi):
                            e.matmul(pr.ap, gaw.ap[:, 0, 128 * ds_:128 * ds_ + 128], XCB[0].ap[:, sl], start=True, stop=False)
                            e.matmul(pr.ap, gaw.ap[:, 1, 128 * ds_:128 * ds_ + 128], XCB[1].ap[:, sl], start=False, stop=True)
                            e.matmul(pi.ap, gxw.ap[:, 0, 128 * ds_:128 * ds_ + 128], XCB[0].ap[:, sl], start=True, stop=False)
                            return e.matmul(pi.ap, gxw.ap[:, 1, 128 * ds_:128 * ds_ + 128], XCB[1].ap[:, sl], start=False, stop=True)
                        P.op("pe", gmm, reads=[gaw.r, gxw.r, XCB[0].r, XCB[1].r], writes=[pr.r, pi.r])
                        ri = self.rot("rt", 2)
                        rt, it = RT[ri], IT[ri]
                        P.op("act", lambda e, rt=rt, d=d, pr=pr: e.activation(out=rt.ap, in_=pr.ap, func=AF.Sigmoid, bias=LV.ap[:, 1, d:d + 1], scale=1.0),
                             reads=[pr.r, LV.r], writes=[rt.r])
                        P.op("act", lambda e, it=it, d=d, pi=pi: e.activation(out=it.ap, in_=pi.ap, func=AF.Sigmoid, bias=LV.ap[:, 2, d:d + 1], scale=1.0),
                             reads=[pi.r, LV.r], writes=[it.r])
                        P.op("act", lambda e, rt=rt, d=d, sl=sl: e.activation(out=A.ap[:, sl], in_=rt.ap, func=AF.Exp, scale=LC.ap[:, 2, d:d + 1]),
                             reads=[rt.r, LC.r], writes=[A.r])
                        P.op("act", lambda e, rt=rt, d=d: e.activation(out=rt.ap, in_=rt.ap, func=AF.Exp, scale=LC.ap[:, 3, d:d + 1]),
                             reads=[LC.r], writes=[rt.r])
                        P.op("act", lambda e, rt=rt: e.activation(out=rt.ap, in_=rt.ap, func=AF.Sqrt, scale=-1.0, bias=1.0), writes=[rt.r])
                        P.op("dve", lambda e, it=it, ds_=ds_, sl=sl: e.tensor_tensor(out=XC[ds_].ap[:, sl], in0=XC[ds_].ap[:, sl], in1=it.ap, op=ALU.mult),
                             reads=[it.r], writes=[XC[ds_].r])
                        P.op("dve", lambda e, rt=rt, ds_=ds_, sl=sl: e.tensor_tensor(out=XC[ds_].ap[:, sl], in0=XC[ds_].ap[:, sl], in1=rt.ap, op=ALU.mult),
                             reads=[rt.r], writes=[XC[ds_].r])
                    P.op("dve", lambda e, ds_=ds_, d=d: e.tensor_tensor_scan(out=HSO.ap, data0=A.ap, data1=XC[ds_].ap, initial=STATE.ap[:, d:d + 1],
                                                                             op0=ALU.mult, op1=ALU.add),
                         reads=[A.r, XC[ds_].r, STATE.r], writes=[HSO.r])
                    P.op("dve", lambda e, d=d: e.tensor_copy(out=STATE.ap[:, d:d + 1], in_=HSO.ap[:, TB - 1:TB]), reads=[HSO.r], writes=[STATE.r])
                    if not state_only:
                        P.op("dve", lambda e, ds_=ds_, d=d: e.tensor_tensor(out=self.GT.ap[:, d, :], in0=HSO.ap, in1=GG[ds_].ap, op=ALU.mult),
                             reads=[HSO.r, GG[ds_].r], writes=[self.GT.r])

        if self.paired:
            for b in range(NB):
                lru_block(b, True)
            LX = Buf(self.sb("LX", [128, 128], F32)[:], "LX")
            PM2 = Buf(self.sb("PM2", [128, 1], F32)[:], "PM2")
            halo_flat = HALO.ap.rearrange("p a b -> p (a b)")
            P.op("sp", lambda e: e.dma_start(out=PM2.ap, in_=self.pmask), writes=[PM2.r], dma_key="PM2")
            P.op("dve", lambda e: e.memset(LX.ap, 0.0), writes=[LX.r])
            P.op("dve", lambda e: e.tensor_copy(out=LX.ap[:, 0:KR], in_=STATE.ap), reads=[STATE.r], writes=[LX.r])
            P.op("dve", lambda e: e.tensor_copy(out=LX.ap[:, KR:KR + 4 * KR], in_=halo_flat), reads=[HALO.r], writes=[LX.r])
            P.op("sp", lambda e: e.dma_start(out=self.LIN, in_=LX.ap), reads=[LX.r], writes=[self.dres("LIN")], dma_key="LX")
            P.op("pool", lambda e: e.collective_compute("AllGather", ALU.bypass, replica_groups=[[0, 1], [2, 3], [4, 5], [6, 7]],
                                                        ins=[self.LIN], outs=[self.LOUT]),
                 reads=[self.dres("LIN")], writes=[self.dres("LOUT")], dma_key="ccL", cc=True)
            P.op("sp", lambda e: e.dma_start(out=LX.ap, in_=self.LOUT[0:128, :]), reads=[self.dres("LOUT")], writes=[LX.r], dma_key="LX")
            P.op("dve", lambda e: e.tensor_scalar(out=STATE.ap, in0=LX.ap[:, 0:KR], scalar1=PM2.ap[:, 0:1], scalar2=None, op0=ALU.mult),
                 reads=[LX.r, PM2.r], writes=[STATE.r])
            P.op("dve", lambda e: e.tensor_scalar(out=halo_flat, in0=LX.ap[:, KR:KR + 4 * KR], scalar1=PM2.ap[:, 0:1], scalar2=None, op0=ALU.mult),
                 reads=[LX.r, PM2.r], writes=[HALO.r])
        for b in range(NB):
            lru_block(b, False)
            self.out_proj(b, w_out, KR, layer, sub, x_src, x_res_fn, nxt, last)
        self.fence([x.r for x in scratch] + [self.GT.r], [self.GT.r])

    def build(self):
        self.alloc()
        self.stage_consts()
        layers = sorted(set(l for l, s in self.sub_list))
        for l in layers:
            self.stage_ada(l)
        l0, s0 = self.sub_list[0]
        self.stage_prologue(l0, s0)
        for idx, (l, s) in enumerate(self.sub_list):
            first = idx == 0
            last = idx == len(self.sub_list) - 1
            nxt = None if last else self.sub_list[idx + 1]
            x_src = self.xT if first else self.XS
            x_res_fn = (lambda m, c0: self.dres("XIN")) if first else (lambda m, c0: self.dres(f"XS{m}_{c0}"))
            if s in (0, 2):
                self.stage_ffn(l, s, x_src, x_res_fn, nxt, last)
            elif l % 2 == 0:
                self.stage_attn(l, s, x_src, x_res_fn, nxt, last)
            else:
                self.stage_lru(l, s, x_src, x_res_fn, nxt, last)
        self.P.finish([r for n, r in self._dres.items() if n.startswith("OUT")])
        self.st.close()
        return self.nc


def prep_inputs(inputs, b, t0, NT, half=0):
    f = np.float32
    x = np.asarray(inputs["x"], f)[b, t0:t0 + NT]
    m = {}
    m["xT"] = np.ascontiguousarray(x.T.reshape(KD, 128, NT))
    m["pmask"] = np.full((128, 1), float(half), f)
    m["cT"] = np.ascontiguousarray(np.asarray(inputs["c"], f)[b].reshape(KD, 128).T)
    for l in range(DEPTH):
        m[f"ada_w_{l}"] = np.asarray(inputs["ada_w"], f)[l]
    m["ada_bT"] = np.ascontiguousarray(np.asarray(inputs["ada_b"], f).reshape(DEPTH, 144, 128).transpose(2, 0, 1))
    ln = np.stack([np.asarray(inputs["ln_g"], f), np.asarray(inputs["ln_b"], f)], axis=2)
    m["lnT"] = np.ascontiguousarray(ln.reshape(DEPTH, 3, 2, KD, 128).transpose(4, 0, 1, 2, 3))
    for l in range(DEPTH):
        for wi in range(2):
            m[f"ffn_w_in_{l}_{wi}"] = np.asarray(inputs["ffn_w_in"], f)[l, wi]
            m[f"ffn_w_out_{l}_{wi}"] = np.asarray(inputs["ffn_w_out"], f)[l, wi]
    m["attn_w_qkv"] = np.asarray(inputs["attn_w_qkv"], f)[0]
    m["attn_w_o"] = np.asarray(inputs["attn_w_o"], f)[0]
    lam = np.stack([np.asarray(inputs[k], f)[0] for k in
                    ("attn_lambda_q1", "attn_lambda_k1", "attn_lambda_q2", "attn_lambda_k2")])
    m["lamv"] = np.ascontiguousarray(np.broadcast_to(lam[None], (128, 4, 128)))
    m["sublnT"] = np.ascontiguousarray(np.asarray(inputs["attn_subln_g"], f)[0].reshape(2, 128).T)
    m["lru_w_in"] = np.asarray(inputs["lru_w_in"], f)[0]
    m["lru_w_out"] = np.asarray(inputs["lru_w_out"], f)[0]
    m["lru_ga"] = np.asarray(inputs["lru_gate_a_w"], f)[0]
    m["lru_gx"] = np.asarray(inputs["lru_gate_x_w"], f)[0]
    m["cwT"] = np.ascontiguousarray(np.asarray(inputs["lru_conv_w"], f)[0].reshape(4, KR, 128).transpose(2, 1, 0))
    lv = np.stack([np.asarray(inputs[k], f)[0] for k in
                   ("lru_conv_b", "lru_gate_a_b", "lru_gate_x_b", "lru_lambda")])
    m["lruv"] = np.ascontiguousarray(lv.reshape(4, KR, 128).transpose(2, 0, 1))
    return m


FULL_SUBS = [(0, 0), (0, 1), (0, 2), (1, 0), (1, 1), (1, 2)]


def run(inputs, sub_list=FULL_SUBS, NT=2048, n_cores=8):
    mk = MK(NT, sub_list, paired=(NT == 2048))
    nc = mk.build()
    B, S = 4, 4096
    per_seq = S // NT
    in_maps = []
    for c in range(n_cores):
        b, h = c // per_seq, c % per_seq
        full = prep_inputs(inputs, b, h * NT, NT, half=h)
        in_maps.append({k: full[k] for k in mk.in_names})
    res = run_bass_kernel_spmd(nc, in_maps, core_ids=list(range(n_cores)))
    out = np.empty((B, S, D), np.float32)
    for c in range(n_cores):
        b, h = c // per_seq, c % per_seq
        o = res.results[c]["outT"].reshape(D, NT)
        out[b, h * NT:(h + 1) * NT] = o.T
    return out


def kernel(**inputs):
    return run(inputs)
```

```python
import math
from contextlib import ExitStack

import numpy as np
import concourse.bass as bass
import concourse.mybir as mybir
from concourse.bass_utils import run_bass_kernel_spmd

F32 = mybir.dt.float32
BF16 = mybir.dt.bfloat16
AF = mybir.ActivationFunctionType
ALU = mybir.AluOpType

D = 2048
KD = 16
DFF = 5632
KF = 44
DRNN = 2560
KR = 20
DEPTH = 2
ALPHA = (2 * DEPTH) ** 0.25
LN_EPS = 1e-5
EPS_P = LN_EPS / (ALPHA * ALPHA)
RMS_EPS = 1e-5
TB = 1024
TT = 512
LRU_C = 8.0


class Res:
    __slots__ = ("name", "last_w", "readers")

    def __init__(self, name):
        self.name = name
        self.last_w = None
        self.readers = []


class Prog:
    ENGS = ("pe", "act", "dve", "pool", "sp")

    def __init__(self, nc, stack):
        self.nc = nc
        self.stack = stack
        self.e = dict(pe=nc.tensor, act=nc.scalar, dve=nc.vector, pool=nc.gpsimd, sp=nc.sync)
        self.q = {k: [] for k in self.ENGS}
        self.cnt = {k: 0 for k in self.ENGS}
        self.csem = {k: stack.enter_context(nc.semaphore("c_" + k)) for k in self.ENGS}
        self.seen = {k: {} for k in self.ENGS}
        self.dsem = {}
        self.dinc = {}

    def _dma_sem(self, key):
        if key not in self.dsem:
            self.dsem[key] = [self.stack.enter_context(self.nc.semaphore("d_" + key)), 0]
        return self.dsem[key]

    def op(self, eng, fn, reads=(), writes=(), dma_key=None, cc=False):
        deps = []
        for r in reads:
            if r.last_w is not None:
                deps.append(r.last_w)
        for w in writes:
            if w.last_w is not None:
                deps.append(w.last_w)
            deps.extend(w.readers)
        waits = {}
        for d in deps:
            if d[0] == "c":
                _, deng, idx = d
                if deng == eng and dma_key is None and eng in ("pe", "sp"):
                    continue
                sem, val, skey = self.csem[deng], idx, "c_" + deng
            else:
                _, key, c = d
                sem, val, skey = self.dsem[key][0], self.dinc.get(key, 16) * c, "d_" + key
            if self.seen[eng].get(skey, 0) >= val:
                continue
            if skey not in waits or waits[skey][1] < val:
                waits[skey] = (sem, val)
        for skey, (sem, val) in waits.items():
            self.seen[eng][skey] = val
        wl = list(waits.values())
        if dma_key is None:
            self.cnt[eng] += 1
            tok = ("c", eng, self.cnt[eng])
            mysem, inc = self.csem[eng], 1
        else:
            ds = self._dma_sem(dma_key)
            ds[1] += 1
            tok = ("d", dma_key, ds[1])
            mysem, inc = ds[0], 16
            if cc:
                self.dinc[dma_key] = 1
                inc = None
        engobj = self.e[eng]

        def emit():
            for sem, val in wl:
                engobj.wait_ge(sem, val)
            ins = fn(engobj)
            if inc is None:
                ins.then_inc(mysem)
            else:
                ins.then_inc(mysem, inc)

        self.q[eng].append(emit)
        for w in writes:
            w.last_w = tok
            w.readers = []
        for r in reads:
            if r not in writes:
                r.readers.append(tok)
        return tok

    def finish(self, final_res):
        wl = []
        for r in final_res:
            for d in ([r.last_w] if r.last_w else []) + list(r.readers):
                if d[0] == "c":
                    wl.append((self.csem[d[1]], d[2]))
                else:
                    wl.append((self.dsem[d[1]][0], self.dinc.get(d[1], 16) * d[2]))
        mx = {}
        for sem, val in wl:
            k = id(sem)
            if k not in mx or mx[k][1] < val:
                mx[k] = (sem, val)
        wl = list(mx.values())
        nc, q = self.nc, self.q
        with nc.Block() as block:
            @block.tensor
            def _(eng):
                for f in q["pe"]:
                    f()

            @block.scalar
            def _(eng):
                for f in q["act"]:
                    f()

            @block.vector
            def _(eng):
                for f in q["dve"]:
                    f()

            @block.gpsimd
            def _(eng):
                for f in q["pool"]:
                    f()

            @block.sync
            def _(eng):
                for f in q["sp"]:
                    f()
                for sem, val in wl:
                    eng.wait_ge(sem, val)


class Buf:
    __slots__ = ("ap", "r", "key")

    def __init__(self, ap, name):
        self.ap = ap
        self.r = Res(name)
        self.key = name


class MK:
    def __init__(self, NT, sub_list, paired=False):
        self.paired = paired
        self.NT = NT
        self.NB = NT // TB
        self.sub_list = sub_list
        self.nc = bass.Bass("TRN2", target_bir_lowering=False)
        self.st = ExitStack()
        self.P = Prog(self.nc, self.st)
        self._rr = {}
        self.in_names = []

    def din(self, name, shape, dt=F32):
        self.in_names.append(name)
        return self.nc.dram_tensor(name, list(shape), dt, kind="ExternalInput").ap()

    def dscr(self, name, shape, dt):
        return self.nc.dram_tensor(name, list(shape), dt, kind="Internal").ap()

    def sb(self, name, shape, dt):
        t = self.st.enter_context(self.nc.sbuf_tensor(name, list(shape), dt))
        return t

    def rot(self, name, n):
        i = self._rr.get(name, 0)
        self._rr[name] = i + 1
        return i % n

    def dres(self, name):
        if not hasattr(self, "_dres"):
            self._dres = {}
        if name not in self._dres:
            self._dres[name] = Res(name)
        return self._dres[name]

    def alloc(self):
        nc, st = self.nc, self.st
        NT = self.NT
        subs = set(self.sub_list)
        layers = sorted(set(l for l, _ in subs))
        self.xT = self.din("xT", [KD, 128, NT])
        self.cT = self.din("cT", [128, KD])
        self.NADA = 72 if self.paired else 144
        self.ada_w = {l: self.din(f"ada_w_{l}", [D, self.NADA * 128]) for l in layers}
        self.ada_bT = self.din("ada_bT", [128, DEPTH, self.NADA])
        if self.paired:
            self.MIN = {l: nc.dram_tensor(f"MIN{l}", [128, 72], F32).ap() for l in layers}
            self.MOUT = {l: nc.dram_tensor(f"MOUT{l}", [256, 72], F32).ap() for l in layers}
        self.lnT = self.din("lnT", [128, DEPTH, 3, 2, KD])
        self.ffn_w_in, self.ffn_w_out = {}, {}
        for (l, sb_) in sorted(subs):
            if sb_ in (0, 2):
                wi = 0 if sb_ == 0 else 1
                self.ffn_w_in[(l, wi)] = self.din(f"ffn_w_in_{l}_{wi}", [D, 2 * DFF])
                self.ffn_w_out[(l, wi)] = self.din(f"ffn_w_out_{l}_{wi}", [DFF, D])
        if (0, 1) in subs:
            self.attn_w_qkv = self.din("attn_w_qkv", [D, 3 * D])
            self.attn_w_o = self.din("attn_w_o", [D, D])
            self.lamv = self.din("lamv", [128, 4, 128])
            self.sublnT = self.din("sublnT", [128, 2])
        if (1, 1) in subs:
            self.lru_w_in = self.din("lru_w_in", [D, 2 * DRNN])
            self.lru_w_out = self.din("lru_w_out", [DRNN, D])
            self.lru_ga = self.din("lru_ga", [10, 256, 256])
            self.lru_gx = self.din("lru_gx", [10, 256, 256])
            self.cwT = self.din("cwT", [128, KR, 4])
            self.lruv = self.din("lruv", [128, 4, KR])
        self.outT = nc.dram_tensor("outT", [KD, 128, NT], F32, kind="ExternalOutput").ap()
        self.XS = self.dscr("XS", [KD, 128, NT], F32)
        self.HS = self.dscr("HS", [KD, 128, NT], BF16)
        self.ZS = self.dscr("ZS", [KD, 128, TB], F32)
        self.QKS = self.dscr("QKS", [16, 128, NT], BF16)
        self.KIN = nc.dram_tensor("KIN", [D, NT], BF16).ap()
        self.VIN = nc.dram_tensor("VIN", [NT, D], BF16).ap()
        if self.paired:
            self.pmask = self.din("pmask", [128, 1])
            self.KOUT = [nc.dram_tensor(f"KOUT{i}", [1024, NT], BF16).ap() for i in range(D // 512)]
            self.VOUT = [nc.dram_tensor(f"VOUT{i}", [1024, D], BF16).ap() for i in range(NT // 512)]
            self.LIN = nc.dram_tensor("LIN", [128, 128], F32).ap()
            self.LOUT = nc.dram_tensor("LOUT", [256, 128], F32).ap()
        self.ONS = self.dscr("ONS", [KD, 128, NT], BF16)
        self.HT_t = self.sb("HT", [128, KD * TB], BF16)
        self.GT_t = self.sb("GT", [128, KF * TB], BF16)
        self.WS_t = [self.sb(f"WS{i}", [128, KF * 256], BF16) for i in range(2)]
        self.HT = Buf(self.HT_t[:].rearrange("p (k n) -> p k n", k=KD), "HT")
        self.GT = Buf(self.GT_t[:].rearrange("p (k n) -> p k n", k=KF), "GT")
        self.WS = [Buf(self.WS_t[i][:], f"WS{i}") for i in range(2)]
        self.WSB = [Res(f"WS{i}b") for i in range(2)]
        self.SA = [Buf(self.sb(f"SA{i}", [128, TT], F32)[:], f"SA{i}") for i in range(2)]
        self.MOD = Buf(self.sb("MOD", [128, DEPTH, 9, KD], F32)[:], "MOD")
        self.SC1P = Buf(self.sb("SC1P", [128, DEPTH, 3, KD], F32)[:], "SC1P")
        self.GATE = Buf(self.sb("GATE", [128, DEPTH, 3, KD], F32)[:], "GATE")
        self.LN = Buf(self.sb("LN", [128, DEPTH, 3, 2, KD], F32)[:], "LN")
        self.ADB = Buf(self.sb("ADB", [128, DEPTH, self.NADA], F32)[:], "ADB")
        self.CT = Buf(self.sb("CTs", [128, KD], F32)[:], "CT")
        self.CA = Buf(self.sb("CA", [128, KD], BF16)[:], "CA")
        self.ONES = Buf(self.sb("ONES", [128, 128], F32)[:], "ONES")
        self.ONESB = Buf(self.sb("ONESB", [128, 128], BF16)[:], "ONESB")
        self.EPSP = Buf(self.sb("EPSP", [128, 1], F32)[:], "EPSP")
        self.RMSE = Buf(self.sb("RMSE", [128, 1], F32)[:], "RMSE")
        self.PS = []
        for i in range(8):
            t = st.enter_context(nc.psum_tensor(f"PS{i}", [128, TT], F32))
            self.PS.append(Buf(t[:], f"PS{i}"))
        self.ov = {}
        for i in range(3):
            self.ov[f"XI{i}"] = Buf(self.sb(f"XI{i}", [128, TT], F32)[:], f"XI{i}")
        for i in range(2):
            self.ov[f"HO{i}"] = Buf(self.sb(f"HO{i}", [128, TT], BF16)[:], f"HO{i}")
        for nm in ("S", "Q", "RSTD", "NMR"):
            self.ov[nm] = Buf(self.sb(nm, [128, TB], F32)[:], nm)
        self.ov_fresh = set()

    def ovw(self, b):
        return [b.r]

    def ht_load_writes(self):
        return [self.HT.r]

    def stage_consts(self):
        P = self.P
        P.op("dve", lambda e: e.memset(self.ONES.ap, 1.0), writes=[self.ONES.r])
        P.op("dve", lambda e: e.memset(self.ONESB.ap, 1.0), writes=[self.ONESB.r])
        P.op("dve", lambda e: e.memset(self.EPSP.ap, EPS_P), writes=[self.EPSP.r])
        P.op("dve", lambda e: e.memset(self.RMSE.ap, RMS_EPS), writes=[self.RMSE.r])
        P.op("sp", lambda e: e.dma_start(out=self.CT.ap, in_=self.cT), writes=[self.CT.r], dma_key="CT")
        P.op("sp", lambda e: e.dma_start(out=self.ADB.ap, in_=self.ada_bT), writes=[self.ADB.r], dma_key="ADB")
        P.op("sp", lambda e: e.dma_start(out=self.LN.ap, in_=self.lnT), writes=[self.LN.r], dma_key="LN")
        P.op("act", lambda e: e.activation(out=self.CA.ap, in_=self.CT.ap, func=AF.Silu),
             reads=[self.CT.r], writes=[self.CA.r])

    def stage_ada(self, layer):
        P = self.P
        wv = self.ada_w[layer].rearrange("(kc p) n -> p kc n", p=128)
        ps = self.PS[7]
        for gq in range(self.NADA // 4):
            s = self.rot("ws", 2)
            ws = self.WS[s]
            wsv = ws.ap[:, 0:KD * 512].rearrange("p (k n) -> p k n", k=KD)
            P.op("pool", lambda e, wsv=wsv, gq=gq: e.dma_start(out=wsv, in_=wv[:, :, 512 * gq:512 * gq + 512]),
                 writes=[ws.r, self.WSB[s]], dma_key=ws.key)

            def mm(e, wsv=wsv, gq=gq):
                ins = None
                for c4 in range(4):
                    j = 4 * gq + c4
                    for k in range(KD):
                        ins = e.matmul(ps.ap[:, j:j + 1], wsv[:, k, 128 * c4:128 * c4 + 128],
                                       self.CA.ap[:, k:k + 1], start=(k == 0), stop=(k == KD - 1))
                return ins
            P.op("pe", mm, reads=[ws.r, self.WSB[s], self.CA.r], writes=[ps.r])
        mod_l = self.MOD.ap[:, layer].rearrange("p a k -> p (a k)")
        NA = self.NADA
        P.op("dve", lambda e: e.tensor_tensor(out=mod_l[:, 0:NA], in0=ps.ap[:, 0:NA], in1=self.ADB.ap[:, layer], op=ALU.add),
             reads=[ps.r, self.ADB.r], writes=[self.MOD.r])
        if self.paired:
            P.op("sp", lambda e: e.dma_start(out=self.MIN[layer], in_=mod_l[:, 0:NA]), reads=[self.MOD.r],
                 writes=[self.dres(f"MIN{layer}")], dma_key="MODs")
            P.op("pool", lambda e: e.collective_compute("AllGather", ALU.bypass, replica_groups=[[0, 1], [2, 3], [4, 5], [6, 7]],
                                                        ins=[self.MIN[layer]], outs=[self.MOUT[layer]]),
                 reads=[self.dres(f"MIN{layer}")], writes=[self.dres(f"MOUT{layer}")], dma_key=f"ccM{layer}", cc=True)
            P.op("sp", lambda e: e.dma_start(out=mod_l.rearrange("p (r c) -> p r c", r=2),
                                             in_=self.MOUT[layer].rearrange("(r p) c -> p r c", p=128)),
                 reads=[self.dres(f"MOUT{layer}")], writes=[self.MOD.r], dma_key="MODl")
        for s in range(3):
            w = 1.0 if s == 1 else 0.5
            P.op("dve", lambda e, s=s: e.tensor_scalar(out=self.SC1P.ap[:, layer, s], in0=self.MOD.ap[:, layer, 3 * s + 1],
                                                       scalar1=1.0, scalar2=None, op0=ALU.add),
                 reads=[self.MOD.r], writes=[self.SC1P.r])
            P.op("dve", lambda e, s=s, w=w: e.tensor_scalar(out=self.GATE.ap[:, layer, s], in0=self.MOD.ap[:, layer, 3 * s + 2],
                                                            scalar1=1.0, scalar2=w / ALPHA, op0=ALU.add, op1=ALU.mult),
                 reads=[self.MOD.r], writes=[self.GATE.r])

    def stage_prologue(self, layer, sub):
        P = self.P
        self.ht_load_writes()
        for b in range(self.NB):
            for m in range(KD):
                for t in range(TB // TT):
                    xi, ho = self.ov[f"XI{self.rot('xi', 3)}"], self.ov[f"HO{self.rot('ho', 2)}"]
                    c0 = b * TB + t * TT
                    P.op("sp", lambda e, xi=xi, m=m, c0=c0: e.dma_start(out=xi.ap, in_=self.xT[m, :, c0:c0 + TT]),
                         writes=self.ovw(xi), dma_key=xi.key)
                    P.op("act", lambda e, xi=xi, ho=ho, m=m: e.activation(
                        out=ho.ap, in_=xi.ap, func=AF.Identity,
                        scale=self.SC1P.ap[:, layer, sub, m:m + 1], bias=self.MOD.ap[:, layer, 3 * sub, m:m + 1]),
                        reads=[xi.r, self.SC1P.r, self.MOD.r], writes=self.ovw(ho))
                    P.op("sp", lambda e, ho=ho, m=m, c0=c0: e.dma_start(out=self.HS[m, :, c0:c0 + TT], in_=ho.ap),
                         reads=[ho.r], writes=[self.dres(f"HS{m}_{c0}")], dma_key=ho.key)

    def load_ht(self, b):
        idx, pb = self.ht_plan[self.ht_pos]
        assert pb == b and self.hs_ver[b] == idx - 1, (self.ht_plan, self.ht_pos, b, self.hs_ver)
        if self.ht_loaded <= self.ht_pos:
            self._load_ht(b)
            self.ht_loaded = self.ht_pos + 1
        self.ht_pos += 1

    def prefetch_ht(self):
        if self.ht_loaded == self.ht_pos and self.ht_pos < len(self.ht_plan):
            idx, b = self.ht_plan[self.ht_pos]
            if self.hs_ver[b] != idx - 1:
                return
            self._load_ht(b)
            self.ht_loaded = self.ht_pos + 1

    def _load_ht(self, b):
        P = self.P
        rl = [self.dres(f"HS{m}_{b * TB + t * TT}") for m in range(KD) for t in range(TB // TT)]
        P.op("sp", lambda e: e.dma_start(out=self.HT.ap, in_=self.HS[:, :, b * TB:(b + 1) * TB].rearrange("k p n -> p k n")),
             reads=rl, writes=self.ht_load_writes(), dma_key="HT")

    def out_proj(self, b, wv, KC, layer, sub, x_src, x_src_res, nxt, last):
        P = self.P
        S, Q, RSTD, NMR = self.ov["S"], self.ov["Q"], self.ov["RSTD"], self.ov["NMR"]
        P.op("dve", lambda e: e.memset(S.ap, 0.0), writes=self.ovw(S))
        P.op("dve", lambda e: e.memset(Q.ap, 0.0), writes=self.ovw(Q))
        steps = [(gq, t, mm) for gq in range(8) for t in range(TB // TT) for mm in range(2)]
        def load_x(step):
            gq, t, mm = step
            m = 2 * gq + mm
            i = self.rot("xi", 3)
            xi = self.ov[f"XI{i}"]
            c0 = b * TB + t * TT
            P.op("sp", lambda e: e.dma_start(out=xi.ap, in_=x_src[m, :, c0:c0 + TT]),
                 reads=[x_src_res(m, c0)], writes=self.ovw(xi), dma_key=xi.key)
            return xi
        xi_next = load_x(steps[0])
        ws = None
        for si, (gq, t, mm) in enumerate(steps):
            m = 2 * gq + mm
            if t == 0 and mm == 0:
                s = self.rot("ws", 2)
                ws = self.WS[s]
                wsv = ws.ap[:, 0:KC * 256].rearrange("p (k n) -> p k n", k=KC)
                wsr = [ws.r, self.WSB[s]]
                P.op("pool", lambda e, wsv=wsv, gq=gq: e.dma_start(out=wsv, in_=wv[:, :, 256 * gq:256 * gq + 256]),
                     writes=wsr, dma_key=ws.key)
            xi = xi_next
            if si + 1 < len(steps):
                xi_next = load_x(steps[si + 1])
            ps = self.PS[self.rot("ps_o", 2)]

            def mmf(e, wsv=wsv, mm=mm, t=t, ps=ps):
                ins = None
                for k in range(KC):
                    ins = e.matmul(ps.ap, wsv[:, k, 128 * mm:128 * mm + 128], self.GT.ap[:, k, t * TT:(t + 1) * TT],
                                   start=(k == 0), stop=(k == KC - 1))
                return ins
            P.op("pe", mmf, reads=wsr + [self.GT.r], writes=[ps.r])
            zo, sq = xi, self.SA[self.rot("sa", 2)]
            P.op("dve", lambda e, ps=ps, zo=zo, xi=xi, m=m: e.scalar_tensor_tensor(
                out=zo.ap, in0=ps.ap, scalar=self.GATE.ap[:, layer, sub, m:m + 1], in1=xi.ap, op0=ALU.mult, op1=ALU.add),
                reads=[ps.r, xi.r, self.GATE.r], writes=self.ovw(zo))
            P.op("sp", lambda e, zo=zo, m=m, t=t: e.dma_start(out=self.ZS[m, :, t * TT:(t + 1) * TT], in_=zo.ap),
                 reads=[zo.r], writes=[self.dres(f"ZS{m}_{t}")], dma_key=zo.key)
            P.op("act", lambda e, zo=zo, sq=sq: e.activation(out=sq.ap, in_=zo.ap, func=AF.Square),
                 reads=[zo.r], writes=self.ovw(sq))
            P.op("dve", lambda e, zo=zo, t=t: e.tensor_tensor(out=S.ap[:, t * TT:(t + 1) * TT], in0=S.ap[:, t * TT:(t + 1) * TT],
                                                             in1=zo.ap, op=ALU.add),
                 reads=[zo.r], writes=[S.r])
            P.op("dve", lambda e, sq=sq, t=t: e.tensor_tensor(out=Q.ap[:, t * TT:(t + 1) * TT], in0=Q.ap[:, t * TT:(t + 1) * TT],
                                                             in1=sq.ap, op=ALU.add),
                 reads=[sq.r], writes=[Q.r])
        for t in range(TB // TT):
            pa, pb = self.PS[2], self.PS[3]
            sl = slice(t * TT, (t + 1) * TT)
            P.op("pe", lambda e, sl=sl: e.matmul(pa.ap, self.ONES.ap, S.ap[:, sl], start=True, stop=True),
                 reads=[S.r, self.ONES.r], writes=[pa.r])
            P.op("pe", lambda e, sl=sl: e.matmul(pb.ap, self.ONES.ap, Q.ap[:, sl], start=True, stop=True),
                 reads=[Q.r, self.ONES.r], writes=[pb.r])
            t1 = self.SA[self.rot("sa", 2)]
            P.op("act", lambda e, sl=sl: e.activation(out=NMR.ap[:, sl], in_=pa.ap, func=AF.Copy, scale=1.0 / D),
                 reads=[pa.r], writes=self.ovw(NMR))
            P.op("dve", lambda e, sl=sl, t1=t1: e.tensor_tensor(out=t1.ap, in0=NMR.ap[:, sl], in1=NMR.ap[:, sl], op=ALU.mult),
                 reads=[NMR.r], writes=self.ovw(t1))
            P.op("dve", lambda e, sl=sl, t1=t1: e.scalar_tensor_tensor(out=RSTD.ap[:, sl], in0=pb.ap, scalar=1.0 / D, in1=t1.ap,
                                                               op0=ALU.mult, op1=ALU.subtract),
                 reads=[pb.r, t1.r], writes=self.ovw(RSTD))
            P.op("act", lambda e, sl=sl: e.activation(out=RSTD.ap[:, sl], in_=RSTD.ap[:, sl], func=AF.Sqrt,
                                                     bias=self.EPSP.ap[:, 0:1], scale=1.0),
                 reads=[self.EPSP.r], writes=[RSTD.r])
            P.op("dve", lambda e, sl=sl: e.reciprocal(out=RSTD.ap[:, sl], in_=RSTD.ap[:, sl]), writes=[RSTD.r])
            P.op("dve", lambda e, sl=sl: e.scalar_tensor_tensor(out=NMR.ap[:, sl], in0=NMR.ap[:, sl], scalar=-1.0,
                                                               in1=RSTD.ap[:, sl], op0=ALU.mult, op1=ALU.mult),
                 reads=[RSTD.r], writes=[NMR.r])
        nsteps = [(m, t) for m in range(KD) for t in range(TB // TT)]

        def load_z(step):
            m, t = step
            i = self.rot("xi", 3)
            zi = self.ov[f"XI{i}"]
            P.op("sp", lambda e: e.dma_start(out=zi.ap, in_=self.ZS[m, :, t * TT:(t + 1) * TT]),
                 reads=[self.dres(f"ZS{m}_{t}")], writes=self.ovw(zi), dma_key=zi.key)
            return zi
        zi_next = load_z(nsteps[0])
        for si, (m, t) in enumerate(nsteps):
            zi = zi_next
            if si + 1 < len(nsteps):
                zi_next = load_z(nsteps[si + 1])
            sl = slice(t * TT, (t + 1) * TT)
            t1, xo, ho = zi, zi, self.ov[f"HO{self.rot('ho', 2)}"]
            c0 = b * TB + t * TT
            P.op("dve", lambda e, zi=zi, t1=t1, sl=sl: e.tensor_tensor(out=t1.ap, in0=zi.ap, in1=RSTD.ap[:, sl], op=ALU.mult),
                 reads=[zi.r, RSTD.r], writes=self.ovw(t1))
            P.op("dve", lambda e, t1=t1, sl=sl: e.tensor_tensor(out=t1.ap, in0=t1.ap, in1=NMR.ap[:, sl], op=ALU.add),
                 reads=[NMR.r], writes=[t1.r])
            P.op("act", lambda e, t1=t1, xo=xo, m=m: e.activation(
                out=xo.ap, in_=t1.ap, func=AF.Identity, scale=self.LN.ap[:, layer, sub, 0, m:m + 1],
                bias=self.LN.ap[:, layer, sub, 1, m:m + 1]),
                reads=[t1.r, self.LN.r], writes=self.ovw(xo))
            if last:
                P.op("sp", lambda e, xo=xo, m=m, c0=c0: e.dma_start(out=self.outT[m, :, c0:c0 + TT], in_=xo.ap),
                     reads=[xo.r], writes=[self.dres(f"OUT{m}_{c0}")], dma_key=xo.key)
            else:
                P.op("sp", lambda e, xo=xo, m=m, c0=c0: e.dma_start(out=self.XS[m, :, c0:c0 + TT], in_=xo.ap),
                     reads=[xo.r], writes=[self.dres(f"XS{m}_{c0}")], dma_key=xo.key)
                nl, ns = nxt
                P.op("act", lambda e, xo=xo, ho=ho, m=m: e.activation(
                    out=ho.ap, in_=xo.ap, func=AF.Identity, scale=self.SC1P.ap[:, nl, ns, m:m + 1],
                    bias=self.MOD.ap[:, nl, 3 * ns, m:m + 1]),
                    reads=[xo.r, self.SC1P.r, self.MOD.r], writes=self.ovw(ho))
                P.op("sp", lambda e, ho=ho, m=m, c0=c0: e.dma_start(out=self.HS[m, :, c0:c0 + TT], in_=ho.ap),
                     reads=[ho.r], writes=[self.dres(f"HS{m}_{c0}")], dma_key=ho.key)

        self.hs_ver[b] = self.cur_idx

    def stage_ffn(self, layer, sub, x_src, x_res_fn, nxt, last):
        P = self.P
        wi = 0 if sub == 0 else 1
        w_in = self.ffn_w_in[(layer, wi)].rearrange("(kc p) n -> p kc n", p=128)
        w_out = self.ffn_w_out[(layer, wi)].rearrange("(kc p) n -> p kc n", p=128)
        for b in range(self.NB):
            self.load_ht(b)
            for gq in range(DFF // 256):
                s = self.rot("ws", 2)
                ws = self.WS[s]
                wsa = ws.ap[:, 0:KD * 256].rearrange("p (k f) -> p k f", k=KD)
                wsu = ws.ap[:, KD * 256:KD * 512].rearrange("p (k f) -> p k f", k=KD)
                wsr = [ws.r, self.WSB[s]]
                P.op("pool", lambda e, wsa=wsa, gq=gq: e.dma_start(out=wsa, in_=w_in[:, :, 256 * gq:256 * gq + 256]),
                     writes=[ws.r], dma_key=ws.key)
                P.op("pool", lambda e, wsu=wsu, gq=gq: e.dma_start(out=wsu, in_=w_in[:, :, DFF + 256 * gq:DFF + 256 * gq + 256]),
                     writes=[self.WSB[s]], dma_key=ws.key + "b")
                for t in range(TB // TT):
                    for sb_ in range(2):
                        j = 2 * gq + sb_
                        pp = self.rot("ps_f", 2)
                        pa, pu = self.PS[4 + 2 * pp], self.PS[5 + 2 * pp]

                        def mmf(e, wsa=wsa, wsu=wsu, sb_=sb_, t=t, pa=pa, pu=pu):
                            ins = None
                            for k in range(KD):
                                ins = e.matmul(pa.ap, wsa[:, k, 128 * sb_:128 * sb_ + 128],
                                               self.HT.ap[:, k, t * TT:(t + 1) * TT], start=(k == 0), stop=(k == KD - 1))
                            for k in range(KD):
                                ins = e.matmul(pu.ap, wsu[:, k, 128 * sb_:128 * sb_ + 128],
                                               self.HT.ap[:, k, t * TT:(t + 1) * TT], start=(k == 0), stop=(k == KD - 1))
                            return ins
                        P.op("pe", mmf, reads=wsr + [self.HT.r], writes=[pa.r, pu.r])
                        sa = self.SA[self.rot("sa", 2)]
                        P.op("act", lambda e, sa=sa, pa=pa: e.activation(out=sa.ap, in_=pa.ap, func=AF.Silu),
                             reads=[pa.r], writes=[sa.r])
                        P.op("dve", lambda e, sa=sa, pu=pu, j=j, t=t: e.tensor_tensor(
                            out=self.GT.ap[:, j, t * TT:(t + 1) * TT], in0=sa.ap, in1=pu.ap, op=ALU.mult),
                            reads=[sa.r, pu.r], writes=[self.GT.r])
            self.prefetch_ht()
            self.out_proj(b, w_out, KF, layer, sub, x_src, x_res_fn, nxt, last)

    def fence(self, old, new):
        if not hasattr(self, "FD"):
            self.FD = Buf(self.sb("FD", [128, 2], F32)[:], "FD")
        self.P.op("dve", lambda e: e.memset(self.FD.ap, 0.0), writes=[self.FD.r] + list(old) + list(new))

    def carve(self, base_t, off, name, shape, dt):
        n = int(np.prod(shape[1:]))
        units = n * (2 if dt == F32 else 1)
        ap = base_t[:, off:off + units]
        if dt == F32:
            ap = ap.bitcast(F32)
        if len(shape) == 3:
            ap = ap.rearrange("p (a b) -> p a b", a=shape[1])
        return Buf(ap, name), off + units

    def stage_attn(self, layer, sub, x_src, x_res_fn, nxt, last):
        P, NT, NB = self.P, self.NT, self.NB
        lam_init = 0.8 - 0.6 * math.exp(-0.3 * layer)
        SCALE = 128 ** -0.5
        wq = self.attn_w_qkv.rearrange("(kc p) n -> p kc n", p=128)
        wo = self.attn_w_o.rearrange("(kc p) n -> p kc n", p=128)
        NQT = NT // TT
        NKT = NT // 128
        LAMV = Buf(self.sb("LAMV", [128, 4, 128], F32)[:], "LAMV")
        ATC = Buf(self.sb("ATC", [128, 8], F32)[:], "ATC")
        SUBG = Buf(self.sb("SUBG", [128, 2], F32)[:], "SUBG")
        QS = [Buf(self.sb(f"QS{i}", [128, TT], BF16)[:], f"QS{i}") for i in range(2)]
        P.op("sp", lambda e: e.dma_start(out=LAMV.ap, in_=self.lamv), writes=[LAMV.r], dma_key="LAMV")
        P.op("sp", lambda e: e.dma_start(out=SUBG.ap, in_=self.sublnT), writes=[SUBG.r], dma_key="SUBG")
        P.op("dve", lambda e: e.tensor_scalar(out=SUBG.ap, in0=SUBG.ap, scalar1=(1.0 - lam_init), scalar2=None, op0=ALU.mult),
             writes=[SUBG.r])
        for q in range(2):
            P.op("dve", lambda e, q=q: e.tensor_tensor(out=LAMV.ap[:, 2 * q], in0=LAMV.ap[:, 2 * q], in1=LAMV.ap[:, 2 * q + 1],
                                                      op=ALU.mult), writes=[LAMV.r])
            P.op("dve", lambda e, q=q: e.reduce_sum(out=ATC.ap[:, q:q + 1], in_=LAMV.ap[:, 2 * q], axis=mybir.AxisListType.X),
                 reads=[LAMV.r], writes=[ATC.r])
        P.op("act", lambda e: e.activation(out=ATC.ap[:, 2:4], in_=ATC.ap[:, 0:2], func=AF.Exp), writes=[ATC.r])
        P.op("dve", lambda e: e.tensor_tensor(out=ATC.ap[:, 4:5], in0=ATC.ap[:, 3:4], in1=ATC.ap[:, 2:3], op=ALU.subtract),
             writes=[ATC.r])
        P.op("dve", lambda e: e.tensor_scalar(out=ATC.ap[:, 5:6], in0=ATC.ap[:, 4:5], scalar1=-lam_init, scalar2=None, op0=ALU.add),
             writes=[ATC.r])
        NEGLAM = ATC.ap[:, 5:6]

        for b in range(NB):
            self.load_ht(b)
            for g in range(16):
                s = self.rot("ws", 2)
                ws = self.WS[s]
                wsr = [ws.r, self.WSB[s]]
                wsv = ws.ap[:, 0:KD * 256].rearrange("p (k f) -> p k f", k=KD)
                P.op("pool", lambda e, wsv=wsv, g=g: e.dma_start(out=wsv, in_=wq[:, :, 256 * g:256 * g + 256]),
                     writes=wsr, dma_key=ws.key)
                for t in range(TB // TT):
                    for sb_ in range(2):
                        ps = self.PS[4 + self.rot("ps_a", 4)]

                        def mmf(e, wsv=wsv, sb_=sb_, t=t, ps=ps):
                            ins = None
                            for k in range(KD):
                                ins = e.matmul(ps.ap, wsv[:, k, 128 * sb_:128 * sb_ + 128], self.HT.ap[:, k, t * TT:(t + 1) * TT],
                                               start=(k == 0), stop=(k == KD - 1))
                            return ins
                        P.op("pe", mmf, reads=wsr + [self.HT.r], writes=[ps.r])
                        qi = self.rot("qs", 2)
                        qs = QS[qi]
                        if qi == 0:
                            P.op("act", lambda e, qs=qs, ps=ps: e.activation(out=qs.ap, in_=ps.ap, func=AF.Copy),
                                 reads=[ps.r], writes=[qs.r])
                        else:
                            P.op("dve", lambda e, qs=qs, ps=ps: e.tensor_copy(out=qs.ap, in_=ps.ap), reads=[ps.r], writes=[qs.r])
                        ch = 2 * g + sb_
                        c0 = b * TB + t * TT
                        if ch < 16:
                            dst = self.QKS[ch, :, c0:c0 + TT]
                        else:
                            dst = self.KIN[(ch - 16) * 128:(ch - 15) * 128, c0:c0 + TT]
                        P.op("sp", lambda e, qs=qs, dst=dst: e.dma_start(out=dst, in_=qs.ap),
                             reads=[qs.r], writes=[self.dres(f"QK{ch}_{c0}")], dma_key=qs.key)
            for gv in range(4):
                s = self.rot("ws", 2)
                ws = self.WS[s]
                wsr = [ws.r, self.WSB[s]]
                wsv = ws.ap[:, 0:KD * 512].rearrange("p (k f) -> p k f", k=KD)
                P.op("pool", lambda e, wsv=wsv, gv=gv: e.dma_start(out=wsv, in_=wq[:, :, 2 * D + 512 * gv:2 * D + 512 * gv + 512]),
                     writes=wsr, dma_key=ws.key)
                for tt in range(TB // 128):
                    ps = self.PS[4 + self.rot("ps_a", 4)]

                    def mmv(e, wsv=wsv, tt=tt, ps=ps):
                        ins = None
                        for k in range(KD):
                            ins = e.matmul(ps.ap, self.HT.ap[:, k, 128 * tt:128 * tt + 128], wsv[:, k, :],
                                           start=(k == 0), stop=(k == KD - 1))
                        return ins
                    P.op("pe", mmv, reads=wsr + [self.HT.r], writes=[ps.r])
                    qi = self.rot("qs", 2)
                    qs = QS[qi]
                    if qi == 0:
                        P.op("act", lambda e, qs=qs, ps=ps: e.activation(out=qs.ap, in_=ps.ap, func=AF.Copy),
                             reads=[ps.r], writes=[qs.r])
                    else:
                        P.op("dve", lambda e, qs=qs, ps=ps: e.tensor_copy(out=qs.ap, in_=ps.ap), reads=[ps.r], writes=[qs.r])
                    tk = b * (TB // 128) + tt
                    P.op("sp", lambda e, qs=qs, tk=tk, gv=gv: e.dma_start(out=self.VIN[tk * 128:(tk + 1) * 128, 512 * gv:512 * gv + 512], in_=qs.ap),
                         reads=[qs.r], writes=[self.dres(f"V{tk}_{gv}")], dma_key=qs.key)
            if b + 1 < NB:
                self.prefetch_ht()

        NPREV = NT // 128 if self.paired else 0
        if self.paired:
            NEGB = Buf(self.sb("NEGB", [128, 1], F32)[:], "NEGB")
            PM = Buf(self.sb("PM", [128, 1], F32)[:], "PM")
            P.op("sp", lambda e: e.dma_start(out=PM.ap, in_=self.pmask), writes=[PM.r], dma_key="PM")
            P.op("dve", lambda e: e.tensor_scalar(out=NEGB.ap, in0=PM.ap, scalar1=-1.0, scalar2=30000.0, op0=ALU.add, op1=ALU.mult),
                 reads=[PM.r], writes=[NEGB.r])
            grp = [[0, 1], [2, 3], [4, 5], [6, 7]]
            for i in range(D // 512):
                rk = [self.dres(f"QK{16 + 4 * i + cc_}_{c0}") for cc_ in range(4) for c0 in range(0, NT, TT)]
                P.op("pool", lambda e, i=i: e.collective_compute("AllGather", ALU.bypass, replica_groups=grp,
                                                                 ins=[self.KIN[512 * i:512 * i + 512, :]], outs=[self.KOUT[i]]),
                     reads=rk, writes=[self.dres(f"KOUT{i}")], dma_key=f"ccK{i}", cc=True)
            for i in range(NT // 512):
                rv = [self.dres(f"V{tk}_{gv}") for tk in range(4 * i, 4 * i + 4) for gv in range(4)]
                P.op("pool", lambda e, i=i: e.collective_compute("AllGather", ALU.bypass, replica_groups=grp,
                                                                 ins=[self.VIN[512 * i:512 * i + 512, :]], outs=[self.VOUT[i]]),
                     reads=rv, writes=[self.dres(f"VOUT{i}")], dma_key=f"ccV{i}", cc=True)

        NKEY = NPREV + NKT
        KH, VH, QT, PT, OST = [], [], [], [], []
        off = 0
        for i in range(2):
            bf, off = self.carve(self.HT_t, off, f"KH{i}", [128, 2, NKEY * 128], BF16)
            KH.append(bf)
        assert off <= KD * TB
        off = 0
        VP = NKEY // 4
        for i in range(2):
            parts = []
            for pp in range(VP):
                bf, off = self.carve(self.GT_t, off, f"VH{i}_{pp}", [128, 4, 256], BF16)
                parts.append(bf)
            VH.append(parts)
        for i in range(2):
            bf, off = self.carve(self.GT_t, off, f"QT{i}", [128, 2, TT], BF16)
            QT.append(bf)
        for i in range(6):
            bf, off = self.carve(self.GT_t, off, f"PT{i}", [128, TT], BF16)
            PT.append(bf)
        ACC = []
        for i in range(4):
            bf, off = self.carve(self.GT_t, off, f"ACC{i}", [128, TT], F32)
            ACC.append(bf)
        for i in range(2):
            bf, off = self.carve(self.GT_t, off, f"OST{i}", [128, 2, TT], BF16)
            OST.append(bf)
        RL, off = self.carve(self.GT_t, off, "RL", [128, TT], F32)
        ONJ = []
        for i in range(2):
            bf, off = self.carve(self.GT_t, off, f"ONJ{i}", [128, 2 * TT], F32)
            ONJ.append(bf)
        OD, off = self.carve(self.GT_t, off, "OD", [128, 2 * TT], F32)
        SQO, off = self.carve(self.GT_t, off, "SQO", [128, 2 * TT], F32)
        RS, off = self.carve(self.GT_t, off, "RS", [128, TT], F32)
        assert off <= KF * TB, off
        scratch = KH + [p_ for v_ in VH for p_ in v_] + QT + PT + OST + [RL, OD, SQO, RS] + ONJ + ACC
        old = [self.HT.r, self.GT.r] + [bb.r for bb in self.ov.values()]
        self.fence(old, [x.r for x in scratch])
        for h in range(8):
            i = h % 2
            kh, vh = KH[i], VH[i]
            rk = [self.dres(f"QK{16 + 2 * h + j}_{c0}") for j in range(2) for c0 in range(0, NT, TT)]
            if self.paired:
                r0 = 256 * (h % 2)
                P.op("sp", lambda e, kh=kh, h=h, r0=r0: e.dma_start(
                    out=kh.ap[:, :, 0:NT], in_=self.KOUT[h // 2][r0:r0 + 256, :].rearrange("(j p) n -> p j n", p=128)),
                    reads=[self.dres(f"KOUT{h // 2}")], writes=[kh.r], dma_key=kh.key + "p")
            P.op("sp", lambda e, kh=kh, h=h: e.dma_start(
                out=kh.ap[:, :, NPREV * 128:NPREV * 128 + NT],
                in_=self.KIN[256 * h:256 * h + 256, :].rearrange("(j p) n -> p j n", p=128)),
                reads=rk, writes=[kh.r], dma_key=kh.key)
            for pp in range(VP):
                vp = vh[pp]
                if pp * 4 < NPREV:
                    P.op("sp", lambda e, vp=vp, h=h, pp=pp: e.dma_start(
                        out=vp.ap, in_=self.VOUT[pp][0:512, 256 * h:256 * h + 256].rearrange("(t p) e -> p t e", p=128)),
                        reads=[self.dres(f"VOUT{pp}")], writes=[vp.r], dma_key=vp.key)
                else:
                    po_ = pp - NPREV // 4
                    rv = [self.dres(f"V{tk}_{h // 2}") for tk in range(po_ * 4, po_ * 4 + 4)]
                    P.op("sp", lambda e, vp=vp, h=h, po_=po_: e.dma_start(
                        out=vp.ap, in_=self.VIN[po_ * 512:(po_ + 1) * 512, 256 * h:256 * h + 256].rearrange("(t p) e -> p t e", p=128)),
                        reads=rv, writes=[vp.r], dma_key=vp.key)
            for t in range(NQT):
                qt = QT[self.rot("qt", 2)]
                rq = [self.dres(f"QK{2 * h + j}_{t * TT}") for j in range(2)]
                P.op("sp", lambda e, qt=qt, h=h, t=t: e.dma_start(
                    out=qt.ap, in_=self.QKS[2 * h:2 * h + 2, :, t * TT:(t + 1) * TT].rearrange("j p n -> p j n")),
                    reads=rq, writes=[qt.r], dma_key=qt.key)
                nkt = NPREV + 4 * (t + 1)
                for j in range(2):
                    po = [self.PS[3 + 2 * j], self.PS[4 + 2 * j]]
                    pl = self.PS[7]
                    acc = [ACC[2 * j], ACC[2 * j + 1]]
                    P.op("dve", lambda e, a_=acc[0]: e.memset(a_.ap, 0.0), writes=[acc[0].r])
                    P.op("pool", lambda e, a_=acc[1]: e.memset(a_.ap, 0.0), writes=[acc[1].r])
                    pend = []

                    def emit_pv(kt, c0, pt, po=po, vh=vh, nkt=nkt, acc=acc):
                        vp = vh[kt // 4]

                        def pv(e, kt=kt, c0=c0, pt=pt, po=po, vp=vp, nkt=nkt):
                            ins = None
                            for c in range(2):
                                ins = e.matmul(po[c].ap[:, c0:TT], vp.ap[:, kt % 4, 128 * c:128 * c + 128], pt.ap[:, c0:TT],
                                               start=(kt == 0), stop=(kt == nkt - 1))
                            return ins
                        P.op("pe", pv, reads=[vp.r, pt.r], writes=[po[0].r, po[1].r])
                        a_ = acc[kt % 2]
                        eng_ = "dve" if kt % 2 == 0 else "pool"
                        P.op(eng_, lambda e, a_=a_, pt=pt, c0=c0: e.tensor_tensor(out=a_.ap[:, c0:TT], in0=a_.ap[:, c0:TT],
                                                                                 in1=pt.ap[:, c0:TT], op=ALU.add),
                             reads=[pt.r], writes=[a_.r])
                    for kt in range(nkt):
                        dpos = kt - (NPREV + 4 * t)
                        c0 = 128 * dpos if dpos > 0 else 0
                        pss = self.PS[self.rot("ps_s", 3)]
                        pt = PT[self.rot("pt", 6)]
                        P.op("pe", lambda e, pss=pss, kt=kt, c0=c0, j=j, kh=kh, qt=qt: e.matmul(
                            pss.ap[:, c0:TT], kh.ap[:, j, 128 * kt:128 * kt + 128], qt.ap[:, j, c0:TT], start=True, stop=True),
                            reads=[kh.r, qt.r], writes=[pss.r])
                        if kt < NPREV:
                            P.op("act", lambda e, pss=pss, pt=pt: e.activation(
                                out=pt.ap, in_=pss.ap, func=AF.Exp, scale=SCALE, bias=NEGB.ap[:, 0:1]),
                                reads=[pss.r, NEGB.r], writes=[pt.r])
                        else:
                            P.op("act", lambda e, pss=pss, pt=pt, c0=c0: e.activation(
                                out=pt.ap[:, c0:TT], in_=pss.ap[:, c0:TT], func=AF.Exp, scale=SCALE),
                                reads=[pss.r], writes=[pt.r])
                        if dpos >= 0:
                            P.op("pool", lambda e, pt=pt, c0=c0: e.memset(pt.ap[64:128, c0:c0 + 64], 0.0), writes=[pt.r])
                        pend.append((kt, c0, pt))
                        if len(pend) > 2:
                            emit_pv(*pend.pop(0))
                    while pend:
                        emit_pv(*pend.pop(0))

                    def lsum(e, pl=pl, acc=acc):
                        e.matmul(pl.ap, self.ONES.ap, acc[0].ap, start=True, stop=False)
                        return e.matmul(pl.ap, self.ONES.ap, acc[1].ap, start=False, stop=True)
                    P.op("pe", lsum, reads=[acc[0].r, acc[1].r, self.ONES.r], writes=[pl.r])
                    P.op("dve", lambda e, pl=pl: e.reciprocal(out=RL.ap, in_=pl.ap), reads=[pl.r], writes=[RL.r])
                    for c in range(2):
                        P.op("dve", lambda e, c=c, j=j, po=po: e.tensor_tensor(out=ONJ[j].ap[:, c * TT:(c + 1) * TT], in0=po[c].ap,
                                                                              in1=RL.ap, op=ALU.mult),
                             reads=[po[c].r, RL.r], writes=[ONJ[j].r])
                P.op("dve", lambda e: e.scalar_tensor_tensor(out=OD.ap, in0=ONJ[1].ap, scalar=NEGLAM, in1=ONJ[0].ap,
                                                             op0=ALU.mult, op1=ALU.add),
                     reads=[ONJ[0].r, ONJ[1].r, ATC.r], writes=[OD.r])
                P.op("act", lambda e: e.activation(out=SQO.ap, in_=OD.ap, func=AF.Square), reads=[OD.r], writes=[SQO.r])
                pss = self.PS[self.rot("ps_s", 3)]

                def msf(e, pss=pss):
                    e.matmul(pss.ap, self.ONES.ap, SQO.ap[:, 0:TT], start=True, stop=False)
                    return e.matmul(pss.ap, self.ONES.ap, SQO.ap[:, TT:2 * TT], start=False, stop=True)
                P.op("pe", msf, reads=[SQO.r, self.ONES.r], writes=[pss.r])
                P.op("act", lambda e, pss=pss: e.activation(out=RS.ap, in_=pss.ap, func=AF.Sqrt, scale=1.0 / 256.0, bias=self.RMSE.ap[:, 0:1]),
                     reads=[pss.r, self.RMSE.r], writes=[RS.r])
                P.op("dve", lambda e: e.reciprocal(out=RS.ap, in_=RS.ap), writes=[RS.r])
                ost = OST[self.rot("ost", 2)]
                for c in range(2):
                    P.op("dve", lambda e, c=c: e.tensor_tensor(out=OD.ap[:, c * TT:(c + 1) * TT], in0=OD.ap[:, c * TT:(c + 1) * TT],
                                                              in1=RS.ap, op=ALU.mult), reads=[RS.r], writes=[OD.r])
                    P.op("act", lambda e, c=c, ost=ost: e.activation(out=ost.ap[:, c], in_=OD.ap[:, c * TT:(c + 1) * TT],
                                                                    func=AF.Identity, scale=SUBG.ap[:, c:c + 1], bias=0.0),
                         reads=[OD.r, SUBG.r], writes=[ost.r])
                P.op("sp", lambda e, ost=ost, h=h, t=t: e.dma_start(
                    out=self.ONS[2 * h:2 * h + 2, :, t * TT:(t + 1) * TT].rearrange("c p n -> p c n"), in_=ost.ap),
                    reads=[ost.r], writes=[self.dres(f"ON{h}_{t}")], dma_key=ost.key)
        self.fence([x.r for x in scratch], [self.HT.r, self.GT.r] + [bb.r for bb in self.ov.values()])
        for b in range(NB):
            rl = [self.dres(f"ON{h}_{b * (TB // TT) + t}") for h in range(8) for t in range(TB // TT)]
            P.op("sp", lambda e, b=b: e.dma_start(out=self.GT.ap[:, 0:KD, :],
                                                 in_=self.ONS[:, :, b * TB:(b + 1) * TB].rearrange("k p n -> p k n")),
                 reads=rl, writes=[self.GT.r], dma_key="GT")
            self.out_proj(b, wo, KD, layer, sub, x_src, x_res_fn, nxt, last)
            self.prefetch_ht()

    def stage_lru(self, layer, sub, x_src, x_res_fn, nxt, last):
        P, NT, NB = self.P, self.NT, self.NB
        w_in = self.lru_w_in.rearrange("(kc p) n -> p kc n", p=128)
        w_out = self.lru_w_out.rearrange("(kc p) n -> p kc n", p=128)
        gav = self.lru_ga.rearrange("n c d -> (n c) d").rearrange("(q p) d -> p q d", p=128)
        gxv = self.lru_gx.rearrange("n c d -> (n c) d").rearrange("(q p) d -> p q d", p=128)
        CW = Buf(self.sb("CW", [128, KR, 4], F32)[:], "CW")
        LV = Buf(self.sb("LV", [128, 4, KR], F32)[:], "LV")
        LC = Buf(self.sb("LC", [128, 4, KR], F32)[:], "LC")
        HALO = Buf(self.sb("HALO", [128, KR, 4], F32)[:], "HALO")
        STATE = Buf(self.sb("STATE", [128, KR], F32)[:], "STATE")
        P.op("sp", lambda e: e.dma_start(out=CW.ap, in_=self.cwT), writes=[CW.r], dma_key="CW")
        P.op("sp", lambda e: e.dma_start(out=LV.ap, in_=self.lruv), writes=[LV.r], dma_key="LV")
        P.op("dve", lambda e: e.memset(HALO.ap, 0.0), writes=[HALO.r])
        P.op("dve", lambda e: e.memset(STATE.ap, 0.0), writes=[STATE.r])
        lam = LV.ap[:, 3]
        P.op("act", lambda e: e.activation(out=LC.ap[:, 0], in_=lam, func=AF.Abs), reads=[LV.r], writes=[LC.r])
        P.op("act", lambda e: e.activation(out=LC.ap[:, 0], in_=LC.ap[:, 0], func=AF.Exp, scale=-1.0), writes=[LC.r])
        P.op("act", lambda e: e.activation(out=LC.ap[:, 0], in_=LC.ap[:, 0], func=AF.Ln, bias=1.0, scale=1.0), writes=[LC.r])
        P.op("dve", lambda e: e.tensor_scalar(out=LC.ap[:, 1], in0=lam, scalar1=-1.0, scalar2=0.0, op0=ALU.mult, op1=ALU.max),
             reads=[LV.r], writes=[LC.r])
        P.op("dve", lambda e: e.tensor_tensor(out=LC.ap[:, 1], in0=LC.ap[:, 1], in1=LC.ap[:, 0], op=ALU.add), writes=[LC.r])
        P.op("dve", lambda e: e.tensor_scalar(out=LC.ap[:, 2], in0=LC.ap[:, 1], scalar1=-LRU_C, scalar2=None, op0=ALU.mult), writes=[LC.r])
        P.op("dve", lambda e: e.tensor_scalar(out=LC.ap[:, 3], in0=LC.ap[:, 1], scalar1=-2.0 * LRU_C, scalar2=None, op0=ALU.mult), writes=[LC.r])
        off = KR * TB
        XB, XC, XCB, GG, RT, IT, GAW, GXW = [], [], [], [], [], [], [], []
        for i in range(2):
            bf, off = self.carve(self.GT_t, off, f"XB{i}", [128, TB + 4], F32); XB.append(bf)
            bf, off = self.carve(self.GT_t, off, f"XC{i}", [128, TB], F32); XC.append(bf)
            bf, off = self.carve(self.GT_t, off, f"XCB{i}", [128, TB], BF16); XCB.append(bf)
            bf, off = self.carve(self.GT_t, off, f"GG{i}", [128, TB], BF16); GG.append(bf)
            bf, off = self.carve(self.GT_t, off, f"RT{i}", [128, TT], F32); RT.append(bf)
            bf, off = self.carve(self.GT_t, off, f"IT{i}", [128, TT], F32); IT.append(bf)
            bf, off = self.carve(self.GT_t, off, f"GAW{i}", [128, 2, 256], BF16); GAW.append(bf)
            bf, off = self.carve(self.GT_t, off, f"GXW{i}", [128, 2, 256], BF16); GXW.append(bf)
        A, off = self.carve(self.GT_t, off, "A", [128, TB], F32)
        HSO, off = self.carve(self.GT_t, off, "HSO", [128, TB], F32)
        assert off <= KF * TB, off
        scratch = XB + XC + XCB + GG + RT + IT + GAW + GXW + [A, HSO]
        self.fence([self.GT.r], [x.r for x in scratch] + [self.GT.r])
        GELU_K = 2.0 * math.sqrt(2.0 / math.pi)

        def lru_block(b, state_only):
            self.load_ht(b)
            for n in range(10):
                s = self.rot("ws", 2)
                ws = self.WS[s]
                wsr = [ws.r, self.WSB[s]]
                wsa = ws.ap[:, 0:KD * 256].rearrange("p (k f) -> p k f", k=KD)
                wsu = ws.ap[:, KD * 256:KD * 512].rearrange("p (k f) -> p k f", k=KD)
                if not state_only:
                    P.op("pool", lambda e, wsa=wsa, n=n: e.dma_start(out=wsa, in_=w_in[:, :, 256 * n:256 * n + 256]),
                         writes=[ws.r], dma_key=ws.key)
                P.op("pool", lambda e, wsu=wsu, n=n: e.dma_start(out=wsu, in_=w_in[:, :, DRNN + 256 * n:DRNN + 256 * n + 256]),
                     writes=[self.WSB[s]], dma_key=ws.key + "b")
                gi = self.rot("gw", 2)
                gaw, gxw = GAW[gi], GXW[gi]
                P.op("pool", lambda e, gaw=gaw, n=n: e.dma_start(out=gaw.ap, in_=gav[:, 2 * n:2 * n + 2, :]), writes=[gaw.r], dma_key=gaw.key)
                P.op("pool", lambda e, gxw=gxw, n=n: e.dma_start(out=gxw.ap, in_=gxv[:, 2 * n:2 * n + 2, :]), writes=[gxw.r], dma_key=gxw.key)
                for sb_ in range(2):
                    c = 2 * n + sb_
                    P.op("dve", lambda e, sb_=sb_, c=c: e.tensor_copy(out=XB[sb_].ap[:, 0:3], in_=HALO.ap[:, c, 0:3]),
                         reads=[HALO.r], writes=[XB[sb_].r])
                for t in range(TB // TT):
                    for sb_ in range(2):
                        pp = self.rot("ps_f", 2)
                        pg, px = self.PS[4 + 2 * pp], self.PS[5 + 2 * pp]

                        def mmf(e, wsa=wsa, wsu=wsu, sb_=sb_, t=t, pg=pg, px=px, state_only=state_only):
                            ins = None
                            for k in range(KD if not state_only else 0):
                                ins = e.matmul(pg.ap, wsa[:, k, 128 * sb_:128 * sb_ + 128], self.HT.ap[:, k, t * TT:(t + 1) * TT],
                                               start=(k == 0), stop=(k == KD - 1))
                            for k in range(KD):
                                ins = e.matmul(px.ap, wsu[:, k, 128 * sb_:128 * sb_ + 128], self.HT.ap[:, k, t * TT:(t + 1) * TT],
                                               start=(k == 0), stop=(k == KD - 1))
                            return ins
                        P.op("pe", mmf, reads=wsr + [self.HT.r], writes=[pg.r, px.r])
                        sa = self.SA[self.rot("sa", 2)]
                        sl = slice(t * TT, (t + 1) * TT)
                        P.op("act", lambda e, px=px, sb_=sb_, t=t: e.activation(out=XB[sb_].ap[:, 3 + t * TT:3 + (t + 1) * TT], in_=px.ap, func=AF.Copy),
                             reads=[px.r], writes=[XB[sb_].r])
                        if state_only:
                            continue
                        P.op("act", lambda e, sa=sa, pg=pg: e.activation(out=sa.ap, in_=pg.ap, func=AF.Square), reads=[pg.r], writes=[sa.r])
                        P.op("dve", lambda e, sa=sa: e.tensor_scalar(out=sa.ap, in0=sa.ap, scalar1=0.044715, scalar2=1.0,
                                                                    op0=ALU.mult, op1=ALU.add), writes=[sa.r])
                        P.op("dve", lambda e, sa=sa, pg=pg: e.tensor_tensor(out=sa.ap, in0=sa.ap, in1=pg.ap, op=ALU.mult),
                             reads=[pg.r], writes=[sa.r])
                        P.op("act", lambda e, sa=sa: e.activation(out=sa.ap, in_=sa.ap, func=AF.Sigmoid, scale=GELU_K), writes=[sa.r])
                        P.op("dve", lambda e, sa=sa, pg=pg, sb_=sb_, sl=sl: e.tensor_tensor(out=GG[sb_].ap[:, sl], in0=sa.ap, in1=pg.ap, op=ALU.mult),
                             reads=[sa.r, pg.r], writes=[GG[sb_].r])
                for sb_ in range(2):
                    c = 2 * n + sb_
                    xb, xc = XB[sb_], XC[sb_]
                    P.op("dve", lambda e, xb=xb, xc=xc, c=c: e.tensor_scalar(out=xc.ap, in0=xb.ap[:, 3:3 + TB], scalar1=CW.ap[:, c, 3:4],
                                                                             scalar2=LV.ap[:, 0, c:c + 1], op0=ALU.mult, op1=ALU.add),
                         reads=[xb.r, CW.r, LV.r], writes=[xc.r])
                    for kk in (2, 1, 0):
                        P.op("dve", lambda e, xb=xb, xc=xc, c=c, kk=kk: e.scalar_tensor_tensor(
                            out=xc.ap, in0=xb.ap[:, kk:kk + TB], scalar=CW.ap[:, c, kk:kk + 1], in1=xc.ap, op0=ALU.mult, op1=ALU.add),
                            reads=[xb.r, CW.r], writes=[xc.r])
                    P.op("dve", lambda e, xb=xb, c=c: e.tensor_copy(out=HALO.ap[:, c, 0:3], in_=xb.ap[:, TB:TB + 3]),
                         reads=[xb.r], writes=[HALO.r])
                    P.op("act", lambda e, xc=xc, sb_=sb_: e.activation(out=XCB[sb_].ap, in_=xc.ap, func=AF.Copy),
                         reads=[xc.r], writes=[XCB[sb_].r])
                for ds_ in range(2):
                    d = 2 * n + ds_
                    for t in range(TB // TT):
                        sl = slice(t * TT, (t + 1) * TT)
                        pr, pi = self.PS[0], self.PS[1]

                        def gmm(e, ds_=ds_, sl=sl, gaw=gaw, gxw=gxw, pr=pr, pi=pi):
                            e.matmul(pr.ap, gaw.ap[:, 0, 128 * ds_:128 * ds_ + 128], XCB[0].ap[:, sl], start=True, stop=False)
                            e.matmul(pr.ap, gaw.ap[:, 1, 128 * ds_:128 * ds_ + 128], XCB[1].ap[:, sl], start=False, stop=True)
                            e.matmul(pi.ap, gxw.ap[:, 0, 128 * ds_:128 * ds_ + 128], XCB[0].ap[:, sl], start=True, stop=False)
                            return e.matmul(pi.ap, gxw.ap[:, 1, 128 * ds_:128 * ds_ + 128], XCB[1].ap[:, sl], start=False, stop=True)
                        P.op("pe", gmm, reads=[gaw.r, gxw.r, XCB[0].r, XCB[1].r], writes=[pr.r, pi.r])
                        ri = self.rot("rt", 2)
                        rt, it = RT[ri], IT[ri]
                        P.op("act", lambda e, rt=rt, d=d, pr=pr: e.activation(out=rt.ap, in_=pr.ap, func=AF.Sigmoid, bias=LV.ap[:, 1, d:d + 1], scale=1.0),
                             reads=[pr.r, LV.r], writes=[rt.r])
                        P.op("act", lambda e, it=it, d=d, pi=pi: e.activation(out=it.ap, in_=pi.ap, func=AF.Sigmoid, bias=LV.ap[:, 2, d:d + 1], scale=1.0),
                             reads=[pi.r, LV.r], writes=[it.r])
                        P.op("act", lambda e, rt=rt, d=d, sl=sl: e.activation(out=A.ap[:, sl], in_=rt.ap, func=AF.Exp, scale=LC.ap[:, 2, d:d + 1]),
                             reads=[rt.r, LC.r], writes=[A.r])
                        P.op("act", lambda e, rt=rt, d=d: e.activation(out=rt.ap, in_=rt.ap, func=AF.Exp, scale=LC.ap[:, 3, d:d + 1]),
                             reads=[LC.r], writes=[rt.r])
                        P.op("act", lambda e, rt=rt: e.activation(out=rt.ap, in_=rt.ap, func=AF.Sqrt, scale=-1.0, bias=1.0), writes=[rt.r])
                        P.op("dve", lambda e, it=it, ds_=ds_, sl=sl: e.tensor_tensor(out=XC[ds_].ap[:, sl], in0=XC[ds_].ap[:, sl], in1=it.ap, op=ALU.mult),
                             reads=[it.r], writes=[XC[ds_].r])
                        P.op("dve", lambda e, rt=rt, ds_=ds_, sl=sl: e.tensor_tensor(out=XC[ds_].ap[:, sl], in0=XC[ds_].ap[:, sl], in1=rt.ap, op=ALU.mult),
                             reads=[rt.r], writes=[XC[ds_].r])
                    P.op("dve", lambda e, ds_=ds_, d=d: e.tensor_tensor_scan(out=HSO.ap, data0=A.ap, data1=XC[ds_].ap, initial=STATE.ap[:, d:d + 1],
                                                                             op0=ALU.mult, op1=ALU.add),
                         reads=[A.r, XC[ds_].r, STATE.r], writes=[HSO.r])
                    P.op("dve", lambda e, d=d: e.tensor_copy(out=STATE.ap[:, d:d + 1], in_=HSO.ap[:, TB - 1:TB]), reads=[HSO.r], writes=[STATE.r])
                    if not state_only:
                        P.op("dve", lambda e, ds_=ds_, d=d: e.tensor_tensor(out=self.GT.ap[:, d, :], in0=HSO.ap, in1=GG[ds_].ap, op=ALU.mult),
                             reads=[HSO.r, GG[ds_].r], writes=[self.GT.r])

        if self.paired:
            for b in range(NB):
                lru_block(b, True)
                self.prefetch_ht()
            LX = Buf(self.sb("LX", [128, 128], F32)[:], "LX")
            PM2 = Buf(self.sb("PM2", [128, 1], F32)[:], "PM2")
            halo_flat = HALO.ap.rearrange("p a b -> p (a b)")
            P.op("sp", lambda e: e.dma_start(out=PM2.ap, in_=self.pmask), writes=[PM2.r], dma_key="PM2")
            P.op("dve", lambda e: e.memset(LX.ap, 0.0), writes=[LX.r])
            P.op("dve", lambda e: e.tensor_copy(out=LX.ap[:, 0:KR], in_=STATE.ap), reads=[STATE.r], writes=[LX.r])
            P.op("dve", lambda e: e.tensor_copy(out=LX.ap[:, KR:KR + 4 * KR], in_=halo_flat), reads=[HALO.r], writes=[LX.r])
            P.op("sp", lambda e: e.dma_start(out=self.LIN, in_=LX.ap), reads=[LX.r], writes=[self.dres("LIN")], dma_key="LX")
            P.op("pool", lambda e: e.collective_compute("AllGather", ALU.bypass, replica_groups=[[0, 1], [2, 3], [4, 5], [6, 7]],
                                                        ins=[self.LIN], outs=[self.LOUT]),
                 reads=[self.dres("LIN")], writes=[self.dres("LOUT")], dma_key="ccL", cc=True)
            P.op("sp", lambda e: e.dma_start(out=LX.ap, in_=self.LOUT[0:128, :]), reads=[self.dres("LOUT")], writes=[LX.r], dma_key="LX")
            P.op("dve", lambda e: e.tensor_scalar(out=STATE.ap, in0=LX.ap[:, 0:KR], scalar1=PM2.ap[:, 0:1], scalar2=None, op0=ALU.mult),
                 reads=[LX.r, PM2.r], writes=[STATE.r])
            P.op("dve", lambda e: e.tensor_scalar(out=halo_flat, in0=LX.ap[:, KR:KR + 4 * KR], scalar1=PM2.ap[:, 0:1], scalar2=None, op0=ALU.mult),
                 reads=[LX.r, PM2.r], writes=[HALO.r])
        for b in range(NB):
            lru_block(b, False)
            self.prefetch_ht()
            self.out_proj(b, w_out, KR, layer, sub, x_src, x_res_fn, nxt, last)
        self.fence([x.r for x in scratch] + [self.GT.r], [self.GT.r])

    def build(self):
        self.alloc()
        self.stage_consts()
        layers = sorted(set(l for l, s in self.sub_list))
        for l in layers:
            self.stage_ada(l)
        l0, s0 = self.sub_list[0]
        self.stage_prologue(l0, s0)
        self.ht_plan = []
        for idx_, (l, s_) in enumerate(self.sub_list):
            rep = 2 if (s_ == 1 and l % 2 == 1 and self.paired) else 1
            self.ht_plan += [(idx_, b_) for b_ in range(self.NB)] * rep
        self.hs_ver = {b_: -1 for b_ in range(self.NB)}
        self.cur_idx = 0
        self.ht_pos = 0
        self.ht_loaded = 0
        for idx, (l, s) in enumerate(self.sub_list):
            self.cur_idx = idx
            first = idx == 0
            last = idx == len(self.sub_list) - 1
            nxt = None if last else self.sub_list[idx + 1]
            x_src = self.xT if first else self.XS
            x_res_fn = (lambda m, c0: self.dres("XIN")) if first else (lambda m, c0: self.dres(f"XS{m}_{c0}"))
            if s in (0, 2):
                self.stage_ffn(l, s, x_src, x_res_fn, nxt, last)
            elif l % 2 == 0:
                self.stage_attn(l, s, x_src, x_res_fn, nxt, last)
            else:
                self.stage_lru(l, s, x_src, x_res_fn, nxt, last)
        self.P.finish([r for n, r in self._dres.items() if n.startswith("OUT")])
        self.st.close()
        return self.nc


def prep_inputs(inputs, b, t0, NT, half=0, paired=False):
    f = np.float32
    x = np.asarray(inputs["x"], f)[b, t0:t0 + NT]
    m = {}
    m["xT"] = np.ascontiguousarray(x.T.reshape(KD, 128, NT))
    m["pmask"] = np.full((128, 1), float(half), f)
    m["cT"] = np.ascontiguousarray(np.asarray(inputs["c"], f)[b].reshape(KD, 128).T)
    adb = np.asarray(inputs["ada_b"], f).reshape(DEPTH, 144, 128).transpose(2, 0, 1)
    if paired:
        for l in range(DEPTH):
            m[f"ada_w_{l}"] = np.ascontiguousarray(np.asarray(inputs["ada_w"], f)[l][:, half * 9216:(half + 1) * 9216])
        m["ada_bT"] = np.ascontiguousarray(adb[:, :, half * 72:(half + 1) * 72])
    else:
        for l in range(DEPTH):
            m[f"ada_w_{l}"] = np.asarray(inputs["ada_w"], f)[l]
        m["ada_bT"] = np.ascontiguousarray(adb)
    ln = np.stack([np.asarray(inputs["ln_g"], f), np.asarray(inputs["ln_b"], f)], axis=2)
    m["lnT"] = np.ascontiguousarray(ln.reshape(DEPTH, 3, 2, KD, 128).transpose(4, 0, 1, 2, 3))
    for l in range(DEPTH):
        for wi in range(2):
            m[f"ffn_w_in_{l}_{wi}"] = np.asarray(inputs["ffn_w_in"], f)[l, wi]
            m[f"ffn_w_out_{l}_{wi}"] = np.asarray(inputs["ffn_w_out"], f)[l, wi]
    m["attn_w_qkv"] = np.asarray(inputs["attn_w_qkv"], f)[0]
    m["attn_w_o"] = np.asarray(inputs["attn_w_o"], f)[0]
    lam = np.stack([np.asarray(inputs[k], f)[0] for k in
                    ("attn_lambda_q1", "attn_lambda_k1", "attn_lambda_q2", "attn_lambda_k2")])
    m["lamv"] = np.ascontiguousarray(np.broadcast_to(lam[None], (128, 4, 128)))
    m["sublnT"] = np.ascontiguousarray(np.asarray(inputs["attn_subln_g"], f)[0].reshape(2, 128).T)
    m["lru_w_in"] = np.asarray(inputs["lru_w_in"], f)[0]
    m["lru_w_out"] = np.asarray(inputs["lru_w_out"], f)[0]
    m["lru_ga"] = np.asarray(inputs["lru_gate_a_w"], f)[0]
    m["lru_gx"] = np.asarray(inputs["lru_gate_x_w"], f)[0]
    m["cwT"] = np.ascontiguousarray(np.asarray(inputs["lru_conv_w"], f)[0].reshape(4, KR, 128).transpose(2, 1, 0))
    lv = np.stack([np.asarray(inputs[k], f)[0] for k in
                   ("lru_conv_b", "lru_gate_a_b", "lru_gate_x_b", "lru_lambda")])
    m["lruv"] = np.ascontiguousarray(lv.reshape(4, KR, 128).transpose(2, 0, 1))
    return m


FULL_SUBS = [(0, 0), (0, 1), (0, 2), (1, 0), (1, 1), (1, 2)]


def run(inputs, sub_list=FULL_SUBS, NT=2048, n_cores=8):
    mk = MK(NT, sub_list, paired=(NT == 2048))
    nc = mk.build()
    B, S = 4, 4096
    per_seq = S // NT
    in_maps = []
    for c in range(n_cores):
        b, h = c // per_seq, c % per_seq
        full = prep_inputs(inputs, b, h * NT, NT, half=h, paired=mk.paired)
        in_maps.append({k: full[k] for k in mk.in_names})
    res = run_bass_kernel_spmd(nc, in_maps, core_ids=list(range(n_cores)))
    out = np.empty((B, S, D), np.float32)
    for c in range(n_cores):
        b, h = c // per_seq, c % per_seq
        o = res.results[c]["outT"].reshape(D, NT)
        out[b, h * NT:(h + 1) * NT] = o.T
    return out


def kernel(**inputs):
    return run(inputs)
```

```python
import math
from contextlib import ExitStack

import numpy as np
import concourse.bass as bass
import concourse.mybir as mybir
from concourse.bass_utils import run_bass_kernel_spmd

F32 = mybir.dt.float32
BF16 = mybir.dt.bfloat16
AF = mybir.ActivationFunctionType
ALU = mybir.AluOpType

D = 2048
KD = 16
DFF = 5632
KF = 44
DRNN = 2560
KR = 20
DEPTH = 2
ALPHA = (2 * DEPTH) ** 0.25
LN_EPS = 1e-5
EPS_P = LN_EPS / (ALPHA * ALPHA)
RMS_EPS = 1e-5
TB = 1024
TT = 512
LRU_C = 8.0


class Res:
    __slots__ = ("name", "last_w", "readers")

    def __init__(self, name):
        self.name = name
        self.last_w = None
        self.readers = []


class Prog:
    ENGS = ("pe", "act", "dve", "pool", "sp")

    def __init__(self, nc, stack):
        self.nc = nc
        self.stack = stack
        self.e = dict(pe=nc.tensor, act=nc.scalar, dve=nc.vector, pool=nc.gpsimd, sp=nc.sync)
        self.q = {k: [] for k in self.ENGS}
        self.cnt = {k: 0 for k in self.ENGS}
        self.csem = {k: stack.enter_context(nc.semaphore("c_" + k)) for k in self.ENGS}
        self.seen = {k: {} for k in self.ENGS}
        self.dsem = {}
        self.dinc = {}
        self.ss = True

    def _dma_sem(self, key):
        if key not in self.dsem:
            self.dsem[key] = [self.stack.enter_context(self.nc.semaphore("d_" + key)), 0]
        return self.dsem[key]

    def op(self, eng, fn, reads=(), writes=(), dma_key=None, cc=False, ss=None):
        if ss is None:
            ss = self.ss
        deps = []
        for r in reads:
            if r.last_w is not None:
                deps.append(r.last_w)
        for w in writes:
            if w.last_w is not None:
                deps.append(w.last_w)
            deps.extend(w.readers)
        waits = {}
        for d in deps:
            if d[0] == "c":
                _, deng, idx = d
                if deng == eng and dma_key is None and (eng in ("pe", "sp") or not ss):
                    continue
                sem, val, skey = self.csem[deng], idx, "c_" + deng
            else:
                _, key, c = d
                sem, val, skey = self.dsem[key][0], self.dinc.get(key, 16) * c, "d_" + key
            if self.seen[eng].get(skey, 0) >= val:
                continue
            if skey not in waits or waits[skey][1] < val:
                waits[skey] = (sem, val)
        for skey, (sem, val) in waits.items():
            self.seen[eng][skey] = val
        wl = list(waits.values())
        if dma_key is None:
            self.cnt[eng] += 1
            tok = ("c", eng, self.cnt[eng])
            mysem, inc = self.csem[eng], 1
        else:
            ds = self._dma_sem(dma_key)
            ds[1] += 1
            tok = ("d", dma_key, ds[1])
            mysem, inc = ds[0], 16
            if cc:
                self.dinc[dma_key] = 1
                inc = None
        engobj = self.e[eng]

        def emit():
            for sem, val in wl:
                engobj.wait_ge(sem, val)
            ins = fn(engobj)
            if inc is None:
                ins.then_inc(mysem)
            else:
                ins.then_inc(mysem, inc)

        self.q[eng].append(emit)
        for w in writes:
            w.last_w = tok
            w.readers = []
        for r in reads:
            if r not in writes:
                r.readers.append(tok)
        return tok

    def finish(self, final_res):
        wl = []
        for r in final_res:
            for d in ([r.last_w] if r.last_w else []) + list(r.readers):
                if d[0] == "c":
                    wl.append((self.csem[d[1]], d[2]))
                else:
                    wl.append((self.dsem[d[1]][0], self.dinc.get(d[1], 16) * d[2]))
        mx = {}
        for sem, val in wl:
            k = id(sem)
            if k not in mx or mx[k][1] < val:
                mx[k] = (sem, val)
        wl = list(mx.values())
        nc, q = self.nc, self.q
        with nc.Block() as block:
            @block.tensor
            def _(eng):
                for f in q["pe"]:
                    f()

            @block.scalar
            def _(eng):
                for f in q["act"]:
                    f()

            @block.vector
            def _(eng):
                for f in q["dve"]:
                    f()

            @block.gpsimd
            def _(eng):
                for f in q["pool"]:
                    f()

            @block.sync
            def _(eng):
                for f in q["sp"]:
                    f()
                for sem, val in wl:
                    eng.wait_ge(sem, val)


class Buf:
    __slots__ = ("ap", "r", "key")

    def __init__(self, ap, name):
        self.ap = ap
        self.r = Res(name)
        self.key = name


class MK:
    def __init__(self, NT, sub_list, paired=False):
        self.paired = paired
        self.NT = NT
        self.NB = NT // TB
        self.sub_list = sub_list
        self.nc = bass.Bass("TRN2", target_bir_lowering=False)
        self.st = ExitStack()
        self.P = Prog(self.nc, self.st)
        self._rr = {}
        self.in_names = []

    def din(self, name, shape, dt=F32):
        self.in_names.append(name)
        return self.nc.dram_tensor(name, list(shape), dt, kind="ExternalInput").ap()

    def dscr(self, name, shape, dt):
        return self.nc.dram_tensor(name, list(shape), dt, kind="Internal").ap()

    def sb(self, name, shape, dt):
        t = self.st.enter_context(self.nc.sbuf_tensor(name, list(shape), dt))
        return t

    def rot(self, name, n):
        i = self._rr.get(name, 0)
        self._rr[name] = i + 1
        return i % n

    def dres(self, name):
        if not hasattr(self, "_dres"):
            self._dres = {}
        if name not in self._dres:
            self._dres[name] = Res(name)
        return self._dres[name]

    def alloc(self):
        nc, st = self.nc, self.st
        NT = self.NT
        subs = set(self.sub_list)
        layers = sorted(set(l for l, _ in subs))
        self.xT = self.din("xT", [KD, 128, NT])
        self.cT = self.din("cT", [128, KD])
        self.NADA = 72 if self.paired else 144
        self.ada_w = {l: self.din(f"ada_w_{l}", [D, self.NADA * 128]) for l in layers}
        self.ada_bT = self.din("ada_bT", [128, DEPTH, self.NADA])
        if self.paired:
            self.MIN = {l: nc.dram_tensor(f"MIN{l}", [128, 72], F32).ap() for l in layers}
            self.MOUT = {l: nc.dram_tensor(f"MOUT{l}", [256, 72], F32).ap() for l in layers}
        self.lnT = self.din("lnT", [128, DEPTH, 3, 2, KD])
        self.ffn_w_in, self.ffn_w_out = {}, {}
        for (l, sb_) in sorted(subs):
            if sb_ in (0, 2):
                wi = 0 if sb_ == 0 else 1
                self.ffn_w_in[(l, wi)] = self.din(f"ffn_w_in_{l}_{wi}", [D, 2 * DFF])
                self.ffn_w_out[(l, wi)] = self.din(f"ffn_w_out_{l}_{wi}", [DFF, D])
        if (0, 1) in subs:
            self.attn_w_qkv = self.din("attn_w_qkv", [D, 3 * D])
            self.attn_w_o = self.din("attn_w_o", [D, D])
            self.lamv = self.din("lamv", [128, 4, 128])
            self.sublnT = self.din("sublnT", [128, 2])
        if (1, 1) in subs:
            self.lru_w_in = self.din("lru_w_in", [D, 2 * DRNN])
            self.lru_w_out = self.din("lru_w_out", [DRNN, D])
            self.lru_ga = self.din("lru_ga", [10, 256, 256])
            self.lru_gx = self.din("lru_gx", [10, 256, 256])
            self.cwT = self.din("cwT", [128, KR, 4])
            self.lruv = self.din("lruv", [128, 4, KR])
        self.outT = nc.dram_tensor("outT", [KD, 128, NT], F32, kind="ExternalOutput").ap()
        self.XS = self.dscr("XS", [KD, 128, NT], F32)
        self.HS = self.dscr("HS", [KD, 128, NT], BF16)
        self.ZS = self.dscr("ZS", [KD, 128, TB], F32)
        self.QKS = self.dscr("QKS", [16, 128, NT], BF16)
        self.KIN = nc.dram_tensor("KIN", [D, NT], BF16).ap()
        self.VIN = nc.dram_tensor("VIN", [NT, D], BF16).ap()
        if self.paired:
            self.pmask = self.din("pmask", [128, 1])
            self.KOUT = [nc.dram_tensor(f"KOUT{i}", [1024, NT], BF16).ap() for i in range(D // 512)]
            self.VOUT = [nc.dram_tensor(f"VOUT{i}", [1024, D], BF16).ap() for i in range(NT // 512)]
            self.LIN = nc.dram_tensor("LIN", [128, 128], F32).ap()
            self.LOUT = nc.dram_tensor("LOUT", [256, 128], F32).ap()
        self.ONS = self.dscr("ONS", [KD, 128, NT], BF16)
        self.HT_t = self.sb("HT", [128, KD * TB], BF16)
        self.GT_t = self.sb("GT", [128, KF * TB], BF16)
        self.WS_t = [self.sb(f"WS{i}", [128, KF * 256], BF16) for i in range(2)]
        self.HT = Buf(self.HT_t[:].rearrange("p (k n) -> p k n", k=KD), "HT")
        self.GT = Buf(self.GT_t[:].rearrange("p (k n) -> p k n", k=KF), "GT")
        self.WS = [Buf(self.WS_t[i][:], f"WS{i}") for i in range(2)]
        self.WSB = [Res(f"WS{i}b") for i in range(2)]
        self.SA = [Buf(self.sb(f"SA{i}", [128, TT], F32)[:], f"SA{i}") for i in range(2)]
        self.MOD = Buf(self.sb("MOD", [128, DEPTH, 9, KD], F32)[:], "MOD")
        self.SC1P = Buf(self.sb("SC1P", [128, DEPTH, 3, KD], F32)[:], "SC1P")
        self.GATE = Buf(self.sb("GATE", [128, DEPTH, 3, KD], F32)[:], "GATE")
        self.LN = Buf(self.sb("LN", [128, DEPTH, 3, 2, KD], F32)[:], "LN")
        self.ADB = Buf(self.sb("ADB", [128, DEPTH, self.NADA], F32)[:], "ADB")
        self.CT = Buf(self.sb("CTs", [128, KD], F32)[:], "CT")
        self.CA = Buf(self.sb("CA", [128, KD], BF16)[:], "CA")
        self.ONES = Buf(self.sb("ONES", [128, 128], F32)[:], "ONES")
        self.ONESB = Buf(self.sb("ONESB", [128, 128], BF16)[:], "ONESB")
        self.EPSP = Buf(self.sb("EPSP", [128, 1], F32)[:], "EPSP")
        self.RMSE = Buf(self.sb("RMSE", [128, 1], F32)[:], "RMSE")
        self.PS = []
        for i in range(8):
            t = st.enter_context(nc.psum_tensor(f"PS{i}", [128, TT], F32))
            self.PS.append(Buf(t[:], f"PS{i}"))
        self.ov = {}
        for i in range(3):
            self.ov[f"XI{i}"] = Buf(self.sb(f"XI{i}", [128, TT], F32)[:], f"XI{i}")
        for i in range(2):
            self.ov[f"HO{i}"] = Buf(self.sb(f"HO{i}", [128, TT], BF16)[:], f"HO{i}")
        for nm in ("S", "Q", "RSTD", "NMR"):
            self.ov[nm] = Buf(self.sb(nm, [128, TB], F32)[:], nm)
        self.ov_fresh = set()

    def ovw(self, b):
        return [b.r]

    def ht_load_writes(self):
        return [self.HT.r]

    def stage_consts(self):
        P = self.P
        P.op("dve", lambda e: e.memset(self.ONES.ap, 1.0), writes=[self.ONES.r])
        P.op("dve", lambda e: e.memset(self.ONESB.ap, 1.0), writes=[self.ONESB.r])
        P.op("dve", lambda e: e.memset(self.EPSP.ap, EPS_P), writes=[self.EPSP.r])
        P.op("dve", lambda e: e.memset(self.RMSE.ap, RMS_EPS), writes=[self.RMSE.r])
        P.op("sp", lambda e: e.dma_start(out=self.CT.ap, in_=self.cT), writes=[self.CT.r], dma_key="CT")
        P.op("sp", lambda e: e.dma_start(out=self.ADB.ap, in_=self.ada_bT), writes=[self.ADB.r], dma_key="ADB")
        P.op("sp", lambda e: e.dma_start(out=self.LN.ap, in_=self.lnT), writes=[self.LN.r], dma_key="LN")
        P.op("act", lambda e: e.activation(out=self.CA.ap, in_=self.CT.ap, func=AF.Silu),
             reads=[self.CT.r], writes=[self.CA.r])

    def stage_ada(self, layer):
        P = self.P
        wv = self.ada_w[layer].rearrange("(kc p) n -> p kc n", p=128)
        ps = self.PS[7]
        for gq in range(self.NADA // 4):
            s = self.rot("ws", 2)
            ws = self.WS[s]
            wsv = ws.ap[:, 0:KD * 512].rearrange("p (k n) -> p k n", k=KD)
            P.op("pool", lambda e, wsv=wsv, gq=gq: e.dma_start(out=wsv, in_=wv[:, :, 512 * gq:512 * gq + 512]),
                 writes=[ws.r, self.WSB[s]], dma_key=ws.key)

            def mm(e, wsv=wsv, gq=gq):
                ins = None
                for c4 in range(4):
                    j = 4 * gq + c4
                    for k in range(KD):
                        ins = e.matmul(ps.ap[:, j:j + 1], wsv[:, k, 128 * c4:128 * c4 + 128],
                                       self.CA.ap[:, k:k + 1], start=(k == 0), stop=(k == KD - 1))
                return ins
            P.op("pe", mm, reads=[ws.r, self.WSB[s], self.CA.r], writes=[ps.r])
        mod_l = self.MOD.ap[:, layer].rearrange("p a k -> p (a k)")
        NA = self.NADA
        P.op("dve", lambda e: e.tensor_tensor(out=mod_l[:, 0:NA], in0=ps.ap[:, 0:NA], in1=self.ADB.ap[:, layer], op=ALU.add),
             reads=[ps.r, self.ADB.r], writes=[self.MOD.r])
        if self.paired:
            P.op("sp", lambda e: e.dma_start(out=self.MIN[layer], in_=mod_l[:, 0:NA]), reads=[self.MOD.r],
                 writes=[self.dres(f"MIN{layer}")], dma_key="MODs")
            P.op("pool", lambda e: e.collective_compute("AllGather", ALU.bypass, replica_groups=[[0, 1], [2, 3], [4, 5], [6, 7]],
                                                        ins=[self.MIN[layer]], outs=[self.MOUT[layer]]),
                 reads=[self.dres(f"MIN{layer}")], writes=[self.dres(f"MOUT{layer}")], dma_key=f"ccM{layer}", cc=True)
            P.op("sp", lambda e: e.dma_start(out=mod_l.rearrange("p (r c) -> p r c", r=2),
                                             in_=self.MOUT[layer].rearrange("(r p) c -> p r c", p=128)),
                 reads=[self.dres(f"MOUT{layer}")], writes=[self.MOD.r], dma_key="MODl")
        for s in range(3):
            w = 1.0 if s == 1 else 0.5
            P.op("dve", lambda e, s=s: e.tensor_scalar(out=self.SC1P.ap[:, layer, s], in0=self.MOD.ap[:, layer, 3 * s + 1],
                                                       scalar1=1.0, scalar2=None, op0=ALU.add),
                 reads=[self.MOD.r], writes=[self.SC1P.r])
            P.op("dve", lambda e, s=s, w=w: e.tensor_scalar(out=self.GATE.ap[:, layer, s], in0=self.MOD.ap[:, layer, 3 * s + 2],
                                                            scalar1=1.0, scalar2=w / ALPHA, op0=ALU.add, op1=ALU.mult),
                 reads=[self.MOD.r], writes=[self.GATE.r])

    def stage_prologue(self, layer, sub):
        P = self.P
        self.ht_load_writes()
        for b in range(self.NB):
            for m in range(KD):
                for t in range(TB // TT):
                    xi, ho = self.ov[f"XI{self.rot('xi', 3)}"], self.ov[f"HO{self.rot('ho', 2)}"]
                    c0 = b * TB + t * TT
                    P.op("sp", lambda e, xi=xi, m=m, c0=c0: e.dma_start(out=xi.ap, in_=self.xT[m, :, c0:c0 + TT]),
                         writes=self.ovw(xi), dma_key=xi.key)
                    P.op("act", lambda e, xi=xi, ho=ho, m=m: e.activation(
                        out=ho.ap, in_=xi.ap, func=AF.Identity,
                        scale=self.SC1P.ap[:, layer, sub, m:m + 1], bias=self.MOD.ap[:, layer, 3 * sub, m:m + 1]),
                        reads=[xi.r, self.SC1P.r, self.MOD.r], writes=self.ovw(ho))
                    P.op("sp", lambda e, ho=ho, m=m, c0=c0: e.dma_start(out=self.HS[m, :, c0:c0 + TT], in_=ho.ap),
                         reads=[ho.r], writes=[self.dres(f"HS{m}_{c0}")], dma_key=ho.key)

    def load_ht(self, b):
        idx, pb = self.ht_plan[self.ht_pos]
        assert pb == b and self.hs_ver[b] == idx - 1, (self.ht_plan, self.ht_pos, b, self.hs_ver)
        if self.ht_loaded <= self.ht_pos:
            self._load_ht(b)
            self.ht_loaded = self.ht_pos + 1
        self.ht_pos += 1

    def prefetch_ht(self):
        if self.ht_loaded == self.ht_pos and self.ht_pos < len(self.ht_plan):
            idx, b = self.ht_plan[self.ht_pos]
            if self.hs_ver[b] != idx - 1:
                return
            self._load_ht(b)
            self.ht_loaded = self.ht_pos + 1

    def _load_ht(self, b):
        P = self.P
        rl = [self.dres(f"HS{m}_{b * TB + t * TT}") for m in range(KD) for t in range(TB // TT)]
        P.op("sp", lambda e: e.dma_start(out=self.HT.ap, in_=self.HS[:, :, b * TB:(b + 1) * TB].rearrange("k p n -> p k n")),
             reads=rl, writes=self.ht_load_writes(), dma_key="HT")

    def out_proj(self, b, wv, KC, layer, sub, x_src, x_src_res, nxt, last):
        P = self.P
        S, Q, RSTD, NMR = self.ov["S"], self.ov["Q"], self.ov["RSTD"], self.ov["NMR"]
        P.op("dve", lambda e: e.memset(S.ap, 0.0), writes=self.ovw(S))
        P.op("dve", lambda e: e.memset(Q.ap, 0.0), writes=self.ovw(Q))
        steps = [(gq, t, mm) for gq in range(8) for t in range(TB // TT) for mm in range(2)]
        def load_x(step):
            gq, t, mm = step
            m = 2 * gq + mm
            i = self.rot("xi", 3)
            xi = self.ov[f"XI{i}"]
            c0 = b * TB + t * TT
            P.op("sp", lambda e: e.dma_start(out=xi.ap, in_=x_src[m, :, c0:c0 + TT]),
                 reads=[x_src_res(m, c0)], writes=self.ovw(xi), dma_key=xi.key)
            return xi
        xi_next = load_x(steps[0])
        ws = None
        for si, (gq, t, mm) in enumerate(steps):
            m = 2 * gq + mm
            if t == 0 and mm == 0:
                s = self.rot("ws", 2)
                ws = self.WS[s]
                wsv = ws.ap[:, 0:KC * 256].rearrange("p (k n) -> p k n", k=KC)
                wsr = [ws.r, self.WSB[s]]
                P.op("pool", lambda e, wsv=wsv, gq=gq: e.dma_start(out=wsv, in_=wv[:, :, 256 * gq:256 * gq + 256]),
                     writes=wsr, dma_key=ws.key)
            xi = xi_next
            if si + 1 < len(steps):
                xi_next = load_x(steps[si + 1])
            ps = self.PS[self.rot("ps_o", 2)]

            def mmf(e, wsv=wsv, mm=mm, t=t, ps=ps):
                ins = None
                for k in range(KC):
                    ins = e.matmul(ps.ap, wsv[:, k, 128 * mm:128 * mm + 128], self.GT.ap[:, k, t * TT:(t + 1) * TT],
                                   start=(k == 0), stop=(k == KC - 1))
                return ins
            P.op("pe", mmf, reads=wsr + [self.GT.r], writes=[ps.r])
            zo, sq = xi, self.SA[self.rot("sa", 2)]
            P.op("dve", lambda e, ps=ps, zo=zo, xi=xi, m=m: e.scalar_tensor_tensor(
                out=zo.ap, in0=ps.ap, scalar=self.GATE.ap[:, layer, sub, m:m + 1], in1=xi.ap, op0=ALU.mult, op1=ALU.add),
                reads=[ps.r, xi.r, self.GATE.r], writes=self.ovw(zo))
            P.op("sp", lambda e, zo=zo, m=m, t=t: e.dma_start(out=self.ZS[m, :, t * TT:(t + 1) * TT], in_=zo.ap),
                 reads=[zo.r], writes=[self.dres(f"ZS{m}_{t}")], dma_key=zo.key)
            P.op("act", lambda e, zo=zo, sq=sq: e.activation(out=sq.ap, in_=zo.ap, func=AF.Square),
                 reads=[zo.r], writes=self.ovw(sq))
            P.op("dve", lambda e, zo=zo, t=t: e.tensor_tensor(out=S.ap[:, t * TT:(t + 1) * TT], in0=S.ap[:, t * TT:(t + 1) * TT],
                                                             in1=zo.ap, op=ALU.add),
                 reads=[zo.r], writes=[S.r])
            P.op("dve", lambda e, sq=sq, t=t: e.tensor_tensor(out=Q.ap[:, t * TT:(t + 1) * TT], in0=Q.ap[:, t * TT:(t + 1) * TT],
                                                             in1=sq.ap, op=ALU.add),
                 reads=[sq.r], writes=[Q.r])
        P.ss = True
        for t in range(TB // TT):
            pa, pb = self.PS[2], self.PS[3]
            sl = slice(t * TT, (t + 1) * TT)
            P.op("pe", lambda e, sl=sl: e.matmul(pa.ap, self.ONES.ap, S.ap[:, sl], start=True, stop=True),
                 reads=[S.r, self.ONES.r], writes=[pa.r])
            P.op("pe", lambda e, sl=sl: e.matmul(pb.ap, self.ONES.ap, Q.ap[:, sl], start=True, stop=True),
                 reads=[Q.r, self.ONES.r], writes=[pb.r])
            t1 = self.SA[self.rot("sa", 2)]
            P.op("act", lambda e, sl=sl: e.activation(out=NMR.ap[:, sl], in_=pa.ap, func=AF.Copy, scale=1.0 / D),
                 reads=[pa.r], writes=self.ovw(NMR))
            P.op("dve", lambda e, sl=sl, t1=t1: e.tensor_tensor(out=t1.ap, in0=NMR.ap[:, sl], in1=NMR.ap[:, sl], op=ALU.mult),
                 reads=[NMR.r], writes=self.ovw(t1))
            P.op("dve", lambda e, sl=sl, t1=t1: e.scalar_tensor_tensor(out=RSTD.ap[:, sl], in0=pb.ap, scalar=1.0 / D, in1=t1.ap,
                                                               op0=ALU.mult, op1=ALU.subtract),
                 reads=[pb.r, t1.r], writes=self.ovw(RSTD))
            P.op("act", lambda e, sl=sl: e.activation(out=RSTD.ap[:, sl], in_=RSTD.ap[:, sl], func=AF.Sqrt,
                                                     bias=self.EPSP.ap[:, 0:1], scale=1.0),
                 reads=[self.EPSP.r], writes=[RSTD.r])
            P.op("dve", lambda e, sl=sl: e.reciprocal(out=RSTD.ap[:, sl], in_=RSTD.ap[:, sl]), writes=[RSTD.r])
            P.op("dve", lambda e, sl=sl: e.scalar_tensor_tensor(out=NMR.ap[:, sl], in0=NMR.ap[:, sl], scalar=-1.0,
                                                               in1=RSTD.ap[:, sl], op0=ALU.mult, op1=ALU.mult),
                 reads=[RSTD.r], writes=[NMR.r])
        nsteps = [(m, t) for m in range(KD) for t in range(TB // TT)]

        def load_z(step):
            m, t = step
            i = self.rot("xi", 3)
            zi = self.ov[f"XI{i}"]
            P.op("sp", lambda e: e.dma_start(out=zi.ap, in_=self.ZS[m, :, t * TT:(t + 1) * TT]),
                 reads=[self.dres(f"ZS{m}_{t}")], writes=self.ovw(zi), dma_key=zi.key)
            return zi
        zi_next = load_z(nsteps[0])
        for si, (m, t) in enumerate(nsteps):
            zi = zi_next
            if si + 1 < len(nsteps):
                zi_next = load_z(nsteps[si + 1])
            sl = slice(t * TT, (t + 1) * TT)
            t1, xo, ho = zi, zi, self.ov[f"HO{self.rot('ho', 2)}"]
            c0 = b * TB + t * TT
            P.op("dve", lambda e, zi=zi, t1=t1, sl=sl: e.tensor_tensor(out=t1.ap, in0=zi.ap, in1=RSTD.ap[:, sl], op=ALU.mult),
                 reads=[zi.r, RSTD.r], writes=self.ovw(t1))
            P.op("dve", lambda e, t1=t1, sl=sl: e.tensor_tensor(out=t1.ap, in0=t1.ap, in1=NMR.ap[:, sl], op=ALU.add),
                 reads=[NMR.r], writes=[t1.r])
            P.op("act", lambda e, t1=t1, xo=xo, m=m: e.activation(
                out=xo.ap, in_=t1.ap, func=AF.Identity, scale=self.LN.ap[:, layer, sub, 0, m:m + 1],
                bias=self.LN.ap[:, layer, sub, 1, m:m + 1]),
                reads=[t1.r, self.LN.r], writes=self.ovw(xo))
            if last:
                P.op("sp", lambda e, xo=xo, m=m, c0=c0: e.dma_start(out=self.outT[m, :, c0:c0 + TT], in_=xo.ap),
                     reads=[xo.r], writes=[self.dres(f"OUT{m}_{c0}")], dma_key=xo.key)
            else:
                P.op("sp", lambda e, xo=xo, m=m, c0=c0: e.dma_start(out=self.XS[m, :, c0:c0 + TT], in_=xo.ap),
                     reads=[xo.r], writes=[self.dres(f"XS{m}_{c0}")], dma_key=xo.key)
                nl, ns = nxt
                P.op("act", lambda e, xo=xo, ho=ho, m=m: e.activation(
                    out=ho.ap, in_=xo.ap, func=AF.Identity, scale=self.SC1P.ap[:, nl, ns, m:m + 1],
                    bias=self.MOD.ap[:, nl, 3 * ns, m:m + 1]),
                    reads=[xo.r, self.SC1P.r, self.MOD.r], writes=self.ovw(ho))
                P.op("sp", lambda e, ho=ho, m=m, c0=c0: e.dma_start(out=self.HS[m, :, c0:c0 + TT], in_=ho.ap),
                     reads=[ho.r], writes=[self.dres(f"HS{m}_{c0}")], dma_key=ho.key)

        self.hs_ver[b] = self.cur_idx

    def stage_ffn(self, layer, sub, x_src, x_res_fn, nxt, last):
        P = self.P
        P.ss = True
        wi = 0 if sub == 0 else 1
        w_in = self.ffn_w_in[(layer, wi)].rearrange("(kc p) n -> p kc n", p=128)
        w_out = self.ffn_w_out[(layer, wi)].rearrange("(kc p) n -> p kc n", p=128)
        for b in range(self.NB):
            self.load_ht(b)
            for gq in range(DFF // 256):
                s = self.rot("ws", 2)
                ws = self.WS[s]
                wsa = ws.ap[:, 0:KD * 256].rearrange("p (k f) -> p k f", k=KD)
                wsu = ws.ap[:, KD * 256:KD * 512].rearrange("p (k f) -> p k f", k=KD)
                wsr = [ws.r, self.WSB[s]]
                P.op("pool", lambda e, wsa=wsa, gq=gq: e.dma_start(out=wsa, in_=w_in[:, :, 256 * gq:256 * gq + 256]),
                     writes=[ws.r], dma_key=ws.key)
                P.op("pool", lambda e, wsu=wsu, gq=gq: e.dma_start(out=wsu, in_=w_in[:, :, DFF + 256 * gq:DFF + 256 * gq + 256]),
                     writes=[self.WSB[s]], dma_key=ws.key + "b")
                for t in range(TB // TT):
                    for sb_ in range(2):
                        j = 2 * gq + sb_
                        pp = self.rot("ps_f", 2)
                        pa, pu = self.PS[4 + 2 * pp], self.PS[5 + 2 * pp]

                        def mmf(e, wsa=wsa, wsu=wsu, sb_=sb_, t=t, pa=pa, pu=pu):
                            ins = None
                            for k in range(KD):
                                ins = e.matmul(pa.ap, wsa[:, k, 128 * sb_:128 * sb_ + 128],
                                               self.HT.ap[:, k, t * TT:(t + 1) * TT], start=(k == 0), stop=(k == KD - 1))
                            for k in range(KD):
                                ins = e.matmul(pu.ap, wsu[:, k, 128 * sb_:128 * sb_ + 128],
                                               self.HT.ap[:, k, t * TT:(t + 1) * TT], start=(k == 0), stop=(k == KD - 1))
                            return ins
                        P.op("pe", mmf, reads=wsr + [self.HT.r], writes=[pa.r, pu.r])
                        sa = self.SA[self.rot("sa", 2)]
                        P.op("act", lambda e, sa=sa, pa=pa: e.activation(out=sa.ap, in_=pa.ap, func=AF.Silu),
                             reads=[pa.r], writes=[sa.r])
                        P.op("dve", lambda e, sa=sa, pu=pu, j=j, t=t: e.tensor_tensor(
                            out=self.GT.ap[:, j, t * TT:(t + 1) * TT], in0=sa.ap, in1=pu.ap, op=ALU.mult),
                            reads=[sa.r, pu.r], writes=[self.GT.r])
            self.prefetch_ht()
            self.out_proj(b, w_out, KF, layer, sub, x_src, x_res_fn, nxt, last)

    def fence(self, old, new):
        if not hasattr(self, "FD"):
            self.FD = Buf(self.sb("FD", [128, 2], F32)[:], "FD")
        self.P.op("dve", lambda e: e.memset(self.FD.ap, 0.0), writes=[self.FD.r] + list(old) + list(new))

    def carve(self, base_t, off, name, shape, dt):
        n = int(np.prod(shape[1:]))
        units = n * (2 if dt == F32 else 1)
        ap = base_t[:, off:off + units]
        if dt == F32:
            ap = ap.bitcast(F32)
        if len(shape) == 3:
            ap = ap.rearrange("p (a b) -> p a b", a=shape[1])
        return Buf(ap, name), off + units

    def stage_attn(self, layer, sub, x_src, x_res_fn, nxt, last):
        P, NT, NB = self.P, self.NT, self.NB
        P.ss = True
        lam_init = 0.8 - 0.6 * math.exp(-0.3 * layer)
        SCALE = 128 ** -0.5
        wq = self.attn_w_qkv.rearrange("(kc p) n -> p kc n", p=128)
        wo = self.attn_w_o.rearrange("(kc p) n -> p kc n", p=128)
        NQT = NT // TT
        NKT = NT // 128
        LAMV = Buf(self.sb("LAMV", [128, 4, 128], F32)[:], "LAMV")
        ATC = Buf(self.sb("ATC", [128, 8], F32)[:], "ATC")
        SUBG = Buf(self.sb("SUBG", [128, 2], F32)[:], "SUBG")
        QS = [Buf(self.sb(f"QS{i}", [128, TT], BF16)[:], f"QS{i}") for i in range(2)]
        P.op("sp", lambda e: e.dma_start(out=LAMV.ap, in_=self.lamv), writes=[LAMV.r], dma_key="LAMV")
        P.op("sp", lambda e: e.dma_start(out=SUBG.ap, in_=self.sublnT), writes=[SUBG.r], dma_key="SUBG")
        P.op("dve", lambda e: e.tensor_scalar(out=SUBG.ap, in0=SUBG.ap, scalar1=(1.0 - lam_init), scalar2=None, op0=ALU.mult),
             writes=[SUBG.r])
        for q in range(2):
            P.op("dve", lambda e, q=q: e.tensor_tensor(out=LAMV.ap[:, 2 * q], in0=LAMV.ap[:, 2 * q], in1=LAMV.ap[:, 2 * q + 1],
                                                      op=ALU.mult), writes=[LAMV.r])
            P.op("dve", lambda e, q=q: e.reduce_sum(out=ATC.ap[:, q:q + 1], in_=LAMV.ap[:, 2 * q], axis=mybir.AxisListType.X),
                 reads=[LAMV.r], writes=[ATC.r])
        P.op("act", lambda e: e.activation(out=ATC.ap[:, 2:4], in_=ATC.ap[:, 0:2], func=AF.Exp), writes=[ATC.r])
        P.op("dve", lambda e: e.tensor_tensor(out=ATC.ap[:, 4:5], in0=ATC.ap[:, 3:4], in1=ATC.ap[:, 2:3], op=ALU.subtract),
             writes=[ATC.r])
        P.op("dve", lambda e: e.tensor_scalar(out=ATC.ap[:, 5:6], in0=ATC.ap[:, 4:5], scalar1=-lam_init, scalar2=None, op0=ALU.add),
             writes=[ATC.r])
        NEGLAM = ATC.ap[:, 5:6]

        for b in range(NB):
            self.load_ht(b)
            for g in range(16):
                s = self.rot("ws", 2)
                ws = self.WS[s]
                wsr = [ws.r, self.WSB[s]]
                wsv = ws.ap[:, 0:KD * 256].rearrange("p (k f) -> p k f", k=KD)
                P.op("pool", lambda e, wsv=wsv, g=g: e.dma_start(out=wsv, in_=wq[:, :, 256 * g:256 * g + 256]),
                     writes=wsr, dma_key=ws.key)
                for t in range(TB // TT):
                    for sb_ in range(2):
                        ps = self.PS[4 + self.rot("ps_a", 4)]

                        def mmf(e, wsv=wsv, sb_=sb_, t=t, ps=ps):
                            ins = None
                            for k in range(KD):
                                ins = e.matmul(ps.ap, wsv[:, k, 128 * sb_:128 * sb_ + 128], self.HT.ap[:, k, t * TT:(t + 1) * TT],
                                               start=(k == 0), stop=(k == KD - 1))
                            return ins
                        P.op("pe", mmf, reads=wsr + [self.HT.r], writes=[ps.r])
                        qi = self.rot("qs", 2)
                        qs = QS[qi]
                        if qi == 0:
                            P.op("act", lambda e, qs=qs, ps=ps: e.activation(out=qs.ap, in_=ps.ap, func=AF.Copy),
                                 reads=[ps.r], writes=[qs.r])
                        else:
                            P.op("dve", lambda e, qs=qs, ps=ps: e.tensor_copy(out=qs.ap, in_=ps.ap), reads=[ps.r], writes=[qs.r])
                        ch = 2 * g + sb_
                        c0 = b * TB + t * TT
                        if ch < 16:
                            dst = self.QKS[ch, :, c0:c0 + TT]
                        else:
                            dst = self.KIN[(ch - 16) * 128:(ch - 15) * 128, c0:c0 + TT]
                        P.op("sp", lambda e, qs=qs, dst=dst: e.dma_start(out=dst, in_=qs.ap),
                             reads=[qs.r], writes=[self.dres(f"QK{ch}_{c0}")], dma_key=qs.key)
            for gv in range(4):
                s = self.rot("ws", 2)
                ws = self.WS[s]
                wsr = [ws.r, self.WSB[s]]
                wsv = ws.ap[:, 0:KD * 512].rearrange("p (k f) -> p k f", k=KD)
                P.op("pool", lambda e, wsv=wsv, gv=gv: e.dma_start(out=wsv, in_=wq[:, :, 2 * D + 512 * gv:2 * D + 512 * gv + 512]),
                     writes=wsr, dma_key=ws.key)
                for tt in range(TB // 128):
                    ps = self.PS[4 + self.rot("ps_a", 4)]

                    def mmv(e, wsv=wsv, tt=tt, ps=ps):
                        ins = None
                        for k in range(KD):
                            ins = e.matmul(ps.ap, self.HT.ap[:, k, 128 * tt:128 * tt + 128], wsv[:, k, :],
                                           start=(k == 0), stop=(k == KD - 1))
                        return ins
                    P.op("pe", mmv, reads=wsr + [self.HT.r], writes=[ps.r])
                    qi = self.rot("qs", 2)
                    qs = QS[qi]
                    if qi == 0:
                        P.op("act", lambda e, qs=qs, ps=ps: e.activation(out=qs.ap, in_=ps.ap, func=AF.Copy),
                             reads=[ps.r], writes=[qs.r])
                    else:
                        P.op("dve", lambda e, qs=qs, ps=ps: e.tensor_copy(out=qs.ap, in_=ps.ap), reads=[ps.r], writes=[qs.r])
                    tk = b * (TB // 128) + tt
                    P.op("sp", lambda e, qs=qs, tk=tk, gv=gv: e.dma_start(out=self.VIN[tk * 128:(tk + 1) * 128, 512 * gv:512 * gv + 512], in_=qs.ap),
                         reads=[qs.r], writes=[self.dres(f"V{tk}_{gv}")], dma_key=qs.key)
            if b + 1 < NB:
                self.prefetch_ht()

        P.ss = True
        NPREV = NT // 128 if self.paired else 0
        if self.paired:
            NEGB = Buf(self.sb("NEGB", [128, 1], F32)[:], "NEGB")
            PM = Buf(self.sb("PM", [128, 1], F32)[:], "PM")
            P.op("sp", lambda e: e.dma_start(out=PM.ap, in_=self.pmask), writes=[PM.r], dma_key="PM")
            P.op("dve", lambda e: e.tensor_scalar(out=NEGB.ap, in0=PM.ap, scalar1=-1.0, scalar2=30000.0, op0=ALU.add, op1=ALU.mult),
                 reads=[PM.r], writes=[NEGB.r])
            grp = [[0, 1], [2, 3], [4, 5], [6, 7]]
            for i in range(D // 512):
                rk = [self.dres(f"QK{16 + 4 * i + cc_}_{c0}") for cc_ in range(4) for c0 in range(0, NT, TT)]
                P.op("pool", lambda e, i=i: e.collective_compute("AllGather", ALU.bypass, replica_groups=grp,
                                                                 ins=[self.KIN[512 * i:512 * i + 512, :]], outs=[self.KOUT[i]]),
                     reads=rk, writes=[self.dres(f"KOUT{i}")], dma_key=f"ccK{i}", cc=True)
            for i in range(NT // 512):
                rv = [self.dres(f"V{tk}_{gv}") for tk in range(4 * i, 4 * i + 4) for gv in range(4)]
                P.op("pool", lambda e, i=i: e.collective_compute("AllGather", ALU.bypass, replica_groups=grp,
                                                                 ins=[self.VIN[512 * i:512 * i + 512, :]], outs=[self.VOUT[i]]),
                     reads=rv, writes=[self.dres(f"VOUT{i}")], dma_key=f"ccV{i}", cc=True)

        NKEY = NPREV + NKT
        KH, VH, QT, PT, OST = [], [], [], [], []
        off = 0
        for i in range(2):
            bf, off = self.carve(self.HT_t, off, f"KH{i}", [128, 2, NKEY * 128], BF16)
            KH.append(bf)
        assert off <= KD * TB
        off = 0
        VP = NKEY // 4
        for i in range(2):
            parts = []
            for pp in range(VP):
                bf, off = self.carve(self.GT_t, off, f"VH{i}_{pp}", [128, 4, 256], BF16)
                parts.append(bf)
            VH.append(parts)
        for i in range(2):
            bf, off = self.carve(self.GT_t, off, f"QT{i}", [128, 2, TT], BF16)
            QT.append(bf)
        for i in range(6):
            bf, off = self.carve(self.GT_t, off, f"PT{i}", [128, TT], BF16)
            PT.append(bf)
        ACC = []
        for i in range(4):
            bf, off = self.carve(self.GT_t, off, f"ACC{i}", [128, TT], F32)
            ACC.append(bf)
        for i in range(2):
            bf, off = self.carve(self.GT_t, off, f"OST{i}", [128, 2, TT], BF16)
            OST.append(bf)
        RL, off = self.carve(self.GT_t, off, "RL", [128, TT], F32)
        ONJ = []
        for i in range(2):
            bf, off = self.carve(self.GT_t, off, f"ONJ{i}", [128, 2 * TT], F32)
            ONJ.append(bf)
        OD, off = self.carve(self.GT_t, off, "OD", [128, 2 * TT], F32)
        SQO, off = self.carve(self.GT_t, off, "SQO", [128, 2 * TT], F32)
        RS, off = self.carve(self.GT_t, off, "RS", [128, TT], F32)
        assert off <= KF * TB, off
        scratch = KH + [p_ for v_ in VH for p_ in v_] + QT + PT + OST + [RL, OD, SQO, RS] + ONJ + ACC
        old = [self.HT.r, self.GT.r] + [bb.r for bb in self.ov.values()]
        self.fence(old, [x.r for x in scratch])
        for h in range(8):
            i = h % 2
            kh, vh = KH[i], VH[i]
            rk = [self.dres(f"QK{16 + 2 * h + j}_{c0}") for j in range(2) for c0 in range(0, NT, TT)]
            if self.paired:
                r0 = 256 * (h % 2)
                P.op("sp", lambda e, kh=kh, h=h, r0=r0: e.dma_start(
                    out=kh.ap[:, :, 0:NT], in_=self.KOUT[h // 2][r0:r0 + 256, :].rearrange("(j p) n -> p j n", p=128)),
                    reads=[self.dres(f"KOUT{h // 2}")], writes=[kh.r], dma_key=kh.key + "p")
            P.op("sp", lambda e, kh=kh, h=h: e.dma_start(
                out=kh.ap[:, :, NPREV * 128:NPREV * 128 + NT],
                in_=self.KIN[256 * h:256 * h + 256, :].rearrange("(j p) n -> p j n", p=128)),
                reads=rk, writes=[kh.r], dma_key=kh.key)
            for pp in range(VP):
                vp = vh[pp]
                if pp * 4 < NPREV:
                    P.op("sp", lambda e, vp=vp, h=h, pp=pp: e.dma_start(
                        out=vp.ap, in_=self.VOUT[pp][0:512, 256 * h:256 * h + 256].rearrange("(t p) e -> p t e", p=128)),
                        reads=[self.dres(f"VOUT{pp}")], writes=[vp.r], dma_key=vp.key)
                else:
                    po_ = pp - NPREV // 4
                    rv = [self.dres(f"V{tk}_{h // 2}") for tk in range(po_ * 4, po_ * 4 + 4)]
                    P.op("sp", lambda e, vp=vp, h=h, po_=po_: e.dma_start(
                        out=vp.ap, in_=self.VIN[po_ * 512:(po_ + 1) * 512, 256 * h:256 * h + 256].rearrange("(t p) e -> p t e", p=128)),
                        reads=rv, writes=[vp.r], dma_key=vp.key)
            for t in range(NQT):
                qt = QT[self.rot("qt", 2)]
                rq = [self.dres(f"QK{2 * h + j}_{t * TT}") for j in range(2)]
                P.op("sp", lambda e, qt=qt, h=h, t=t: e.dma_start(
                    out=qt.ap, in_=self.QKS[2 * h:2 * h + 2, :, t * TT:(t + 1) * TT].rearrange("j p n -> p j n")),
                    reads=rq, writes=[qt.r], dma_key=qt.key)
                nkt = NPREV + 4 * (t + 1)
                for j in range(2):
                    po = [self.PS[3 + 2 * j], self.PS[4 + 2 * j]]
                    pl = self.PS[7]
                    acc = [ACC[2 * j], ACC[2 * j + 1]]
                    P.op("dve", lambda e, a_=acc[0]: e.memset(a_.ap, 0.0), writes=[acc[0].r])
                    pend = []

                    def emit_pv(kt, c0, pt, po=po, vh=vh, nkt=nkt, acc=acc):
                        vp = vh[kt // 4]

                        def pv(e, kt=kt, c0=c0, pt=pt, po=po, vp=vp, nkt=nkt):
                            ins = None
                            for c in range(2):
                                ins = e.matmul(po[c].ap[:, c0:TT], vp.ap[:, kt % 4, 128 * c:128 * c + 128], pt.ap[:, c0:TT],
                                               start=(kt == 0), stop=(kt == nkt - 1))
                            return ins
                        P.op("pe", pv, reads=[vp.r, pt.r], writes=[po[0].r, po[1].r])
                        a_ = acc[0]
                        eng_ = "dve"
                        P.op(eng_, lambda e, a_=a_, pt=pt, c0=c0: e.tensor_tensor(out=a_.ap[:, c0:TT], in0=a_.ap[:, c0:TT],
                                                                                 in1=pt.ap[:, c0:TT], op=ALU.add),
                             reads=[pt.r], writes=[a_.r])
                    for kt in range(nkt):
                        dpos = kt - (NPREV + 4 * t)
                        c0 = 128 * dpos if dpos > 0 else 0
                        pss = self.PS[self.rot("ps_s", 3)]
                        pt = PT[self.rot("pt", 6)]
                        P.op("pe", lambda e, pss=pss, kt=kt, c0=c0, j=j, kh=kh, qt=qt: e.matmul(
                            pss.ap[:, c0:TT], kh.ap[:, j, 128 * kt:128 * kt + 128], qt.ap[:, j, c0:TT], start=True, stop=True),
                            reads=[kh.r, qt.r], writes=[pss.r])
                        if kt < NPREV:
                            P.op("act", lambda e, pss=pss, pt=pt: e.activation(
                                out=pt.ap, in_=pss.ap, func=AF.Exp, scale=SCALE, bias=NEGB.ap[:, 0:1]),
                                reads=[pss.r, NEGB.r], writes=[pt.r])
                        else:
                            P.op("act", lambda e, pss=pss, pt=pt, c0=c0: e.activation(
                                out=pt.ap[:, c0:TT], in_=pss.ap[:, c0:TT], func=AF.Exp, scale=SCALE),
                                reads=[pss.r], writes=[pt.r])
                        if dpos >= 0:
                            P.op("pool", lambda e, pt=pt, c0=c0: e.memset(pt.ap[64:128, c0:c0 + 64], 0.0), writes=[pt.r])
                        pend.append((kt, c0, pt))
                        if len(pend) > 2:
                            emit_pv(*pend.pop(0))
                    while pend:
                        emit_pv(*pend.pop(0))

                    def lsum(e, pl=pl, acc=acc):
                        return e.matmul(pl.ap, self.ONES.ap, acc[0].ap, start=True, stop=True)
                    P.op("pe", lsum, reads=[acc[0].r, self.ONES.r], writes=[pl.r])
                    P.op("dve", lambda e, pl=pl: e.reciprocal(out=RL.ap, in_=pl.ap), reads=[pl.r], writes=[RL.r])
                    for c in range(2):
                        P.op("dve", lambda e, c=c, j=j, po=po: e.tensor_tensor(out=ONJ[j].ap[:, c * TT:(c + 1) * TT], in0=po[c].ap,
                                                                              in1=RL.ap, op=ALU.mult),
                             reads=[po[c].r, RL.r], writes=[ONJ[j].r])
                P.op("dve", lambda e: e.scalar_tensor_tensor(out=OD.ap, in0=ONJ[1].ap, scalar=NEGLAM, in1=ONJ[0].ap,
                                                             op0=ALU.mult, op1=ALU.add),
                     reads=[ONJ[0].r, ONJ[1].r, ATC.r], writes=[OD.r])
                P.op("act", lambda e: e.activation(out=SQO.ap, in_=OD.ap, func=AF.Square), reads=[OD.r], writes=[SQO.r])
                pss = self.PS[self.rot("ps_s", 3)]

                def msf(e, pss=pss):
                    e.matmul(pss.ap, self.ONES.ap, SQO.ap[:, 0:TT], start=True, stop=False)
                    return e.matmul(pss.ap, self.ONES.ap, SQO.ap[:, TT:2 * TT], start=False, stop=True)
                P.op("pe", msf, reads=[SQO.r, self.ONES.r], writes=[pss.r])
                P.op("act", lambda e, pss=pss: e.activation(out=RS.ap, in_=pss.ap, func=AF.Sqrt, scale=1.0 / 256.0, bias=self.RMSE.ap[:, 0:1]),
                     reads=[pss.r, self.RMSE.r], writes=[RS.r])
                P.op("dve", lambda e: e.reciprocal(out=RS.ap, in_=RS.ap), writes=[RS.r])
                ost = OST[self.rot("ost", 2)]
                for c in range(2):
                    P.op("dve", lambda e, c=c: e.tensor_tensor(out=OD.ap[:, c * TT:(c + 1) * TT], in0=OD.ap[:, c * TT:(c + 1) * TT],
                                                              in1=RS.ap, op=ALU.mult), reads=[RS.r], writes=[OD.r])
                    P.op("act", lambda e, c=c, ost=ost: e.activation(out=ost.ap[:, c], in_=OD.ap[:, c * TT:(c + 1) * TT],
                                                                    func=AF.Identity, scale=SUBG.ap[:, c:c + 1], bias=0.0),
                         reads=[OD.r, SUBG.r], writes=[ost.r])
                P.op("sp", lambda e, ost=ost, h=h, t=t: e.dma_start(
                    out=self.ONS[2 * h:2 * h + 2, :, t * TT:(t + 1) * TT].rearrange("c p n -> p c n"), in_=ost.ap),
                    reads=[ost.r], writes=[self.dres(f"ON{h}_{t}")], dma_key=ost.key)
        P.ss = True
        self.fence([x.r for x in scratch], [self.HT.r, self.GT.r] + [bb.r for bb in self.ov.values()])
        for b in range(NB):
            rl = [self.dres(f"ON{h}_{b * (TB // TT) + t}") for h in range(8) for t in range(TB // TT)]
            P.op("sp", lambda e, b=b: e.dma_start(out=self.GT.ap[:, 0:KD, :],
                                                 in_=self.ONS[:, :, b * TB:(b + 1) * TB].rearrange("k p n -> p k n")),
                 reads=rl, writes=[self.GT.r], dma_key="GT")
            self.out_proj(b, wo, KD, layer, sub, x_src, x_res_fn, nxt, last)
            self.prefetch_ht()

    def stage_lru(self, layer, sub, x_src, x_res_fn, nxt, last):
        P, NT, NB = self.P, self.NT, self.NB
        P.ss = True
        w_in = self.lru_w_in.rearrange("(kc p) n -> p kc n", p=128)
        w_out = self.lru_w_out.rearrange("(kc p) n -> p kc n", p=128)
        gav = self.lru_ga.rearrange("n c d -> (n c) d").rearrange("(q p) d -> p q d", p=128)
        gxv = self.lru_gx.rearrange("n c d -> (n c) d").rearrange("(q p) d -> p q d", p=128)
        CW = Buf(self.sb("CW", [128, KR, 4], F32)[:], "CW")
        LV = Buf(self.sb("LV", [128, 4, KR], F32)[:], "LV")
        LC = Buf(self.sb("LC", [128, 4, KR], F32)[:], "LC")
        HALO = Buf(self.sb("HALO", [128, KR, 4], F32)[:], "HALO")
        STATE = Buf(self.sb("STATE", [128, KR], F32)[:], "STATE")
        P.op("sp", lambda e: e.dma_start(out=CW.ap, in_=self.cwT), writes=[CW.r], dma_key="CW")
        P.op("sp", lambda e: e.dma_start(out=LV.ap, in_=self.lruv), writes=[LV.r], dma_key="LV")
        P.op("dve", lambda e: e.memset(HALO.ap, 0.0), writes=[HALO.r])
        P.op("dve", lambda e: e.memset(STATE.ap, 0.0), writes=[STATE.r])
        lam = LV.ap[:, 3]
        P.op("act", lambda e: e.activation(out=LC.ap[:, 0], in_=lam, func=AF.Abs), reads=[LV.r], writes=[LC.r])
        P.op("act", lambda e: e.activation(out=LC.ap[:, 0], in_=LC.ap[:, 0], func=AF.Exp, scale=-1.0), writes=[LC.r])
        P.op("act", lambda e: e.activation(out=LC.ap[:, 0], in_=LC.ap[:, 0], func=AF.Ln, bias=1.0, scale=1.0), writes=[LC.r])
        P.op("dve", lambda e: e.tensor_scalar(out=LC.ap[:, 1], in0=lam, scalar1=-1.0, scalar2=0.0, op0=ALU.mult, op1=ALU.max),
             reads=[LV.r], writes=[LC.r])
        P.op("dve", lambda e: e.tensor_tensor(out=LC.ap[:, 1], in0=LC.ap[:, 1], in1=LC.ap[:, 0], op=ALU.add), writes=[LC.r])
        P.op("dve", lambda e: e.tensor_scalar(out=LC.ap[:, 2], in0=LC.ap[:, 1], scalar1=-LRU_C, scalar2=None, op0=ALU.mult), writes=[LC.r])
        P.op("dve", lambda e: e.tensor_scalar(out=LC.ap[:, 3], in0=LC.ap[:, 1], scalar1=-2.0 * LRU_C, scalar2=None, op0=ALU.mult), writes=[LC.r])
        off = KR * TB
        XB, XC, XCB, GG, RT, IT, GAW, GXW, AA, HSOs = [], [], [], [], [], [], [], [], [], []
        for i in range(2):
            xb_, xc_, xcb_, gg_ = [], [], [], []
            for sb_ in range(2):
                bf, off = self.carve(self.GT_t, off, f"XB{i}{sb_}", [128, TT + 4], F32); xb_.append(bf)
                bf, off = self.carve(self.GT_t, off, f"XC{i}{sb_}", [128, TT], F32); xc_.append(bf)
                bf, off = self.carve(self.GT_t, off, f"XCB{i}{sb_}", [128, TT], BF16); xcb_.append(bf)
                bf, off = self.carve(self.GT_t, off, f"GG{i}{sb_}", [128, TT], BF16); gg_.append(bf)
            XB.append(xb_); XC.append(xc_); XCB.append(xcb_); GG.append(gg_)
            bf, off = self.carve(self.GT_t, off, f"RT{i}", [128, TT], F32); RT.append(bf)
            bf, off = self.carve(self.GT_t, off, f"IT{i}", [128, TT], F32); IT.append(bf)
            bf, off = self.carve(self.GT_t, off, f"GAW{i}", [128, 2, 256], BF16); GAW.append(bf)
            bf, off = self.carve(self.GT_t, off, f"GXW{i}", [128, 2, 256], BF16); GXW.append(bf)
            bf, off = self.carve(self.GT_t, off, f"AA{i}", [128, TT], F32); AA.append(bf)
            bf, off = self.carve(self.GT_t, off, f"HSO{i}", [128, TT], F32); HSOs.append(bf)
        assert off <= KF * TB, off
        scratch = [x for l_ in (XB + XC + XCB + GG) for x in l_] + RT + IT + GAW + GXW + AA + HSOs
        self.fence([self.GT.r], [x.r for x in scratch] + [self.GT.r])
        GELU_K = 2.0 * math.sqrt(2.0 / math.pi)

        def lru_block(b, state_only):
            self.load_ht(b)
            for n in range(10):
                s = self.rot("ws", 2)
                ws = self.WS[s]
                wsr = [ws.r, self.WSB[s]]
                wsa = ws.ap[:, 0:KD * 256].rearrange("p (k f) -> p k f", k=KD)
                wsu = ws.ap[:, KD * 256:KD * 512].rearrange("p (k f) -> p k f", k=KD)
                if not state_only:
                    P.op("pool", lambda e, wsa=wsa, n=n: e.dma_start(out=wsa, in_=w_in[:, :, 256 * n:256 * n + 256]),
                         writes=[ws.r], dma_key=ws.key)
                P.op("pool", lambda e, wsu=wsu, n=n: e.dma_start(out=wsu, in_=w_in[:, :, DRNN + 256 * n:DRNN + 256 * n + 256]),
                     writes=[self.WSB[s]], dma_key=ws.key + "b")
                gi = self.rot("gw", 2)
                gaw, gxw = GAW[gi], GXW[gi]
                P.op("pool", lambda e, gaw=gaw, n=n: e.dma_start(out=gaw.ap, in_=gav[:, 2 * n:2 * n + 2, :]), writes=[gaw.r], dma_key=gaw.key)
                P.op("pool", lambda e, gxw=gxw, n=n: e.dma_start(out=gxw.ap, in_=gxv[:, 2 * n:2 * n + 2, :]), writes=[gxw.r], dma_key=gxw.key)
                for t in range(TB // TT):
                    u = self.rot("lru_u", 2)
                    xbs, xcs, xcbs, ggs = XB[u], XC[u], XCB[u], GG[u]
                    tsl = slice(t * TT, (t + 1) * TT)
                    for sb_ in range(2):
                        c = 2 * n + sb_
                        xb, xc = xbs[sb_], xcs[sb_]
                        P.op("dve", lambda e, xb=xb, c=c: e.tensor_copy(out=xb.ap[:, 0:3], in_=HALO.ap[:, c, 0:3]),
                             reads=[HALO.r], writes=[xb.r], ss=True)
                        pp = self.rot("ps_f", 2)
                        pg, px = self.PS[4 + 2 * pp], self.PS[5 + 2 * pp]

                        def mmf(e, wsa=wsa, wsu=wsu, sb_=sb_, tsl=tsl, pg=pg, px=px, state_only=state_only):
                            ins = None
                            for k in range(KD if not state_only else 0):
                                ins = e.matmul(pg.ap, wsa[:, k, 128 * sb_:128 * sb_ + 128], self.HT.ap[:, k, tsl],
                                               start=(k == 0), stop=(k == KD - 1))
                            for k in range(KD):
                                ins = e.matmul(px.ap, wsu[:, k, 128 * sb_:128 * sb_ + 128], self.HT.ap[:, k, tsl],
                                               start=(k == 0), stop=(k == KD - 1))
                            return ins
                        P.op("pe", mmf, reads=wsr + [self.HT.r], writes=[pg.r, px.r])
                        P.op("act", lambda e, px=px, xb=xb: e.activation(out=xb.ap[:, 3:3 + TT], in_=px.ap, func=AF.Copy),
                             reads=[px.r], writes=[xb.r])
                        if not state_only:
                            sa = self.SA[self.rot("sa", 2)]
                            P.op("act", lambda e, sa=sa, pg=pg: e.activation(out=sa.ap, in_=pg.ap, func=AF.Square), reads=[pg.r], writes=[sa.r])
                            P.op("dve", lambda e, sa=sa: e.tensor_scalar(out=sa.ap, in0=sa.ap, scalar1=0.044715, scalar2=1.0,
                                                                        op0=ALU.mult, op1=ALU.add), writes=[sa.r])
                            P.op("dve", lambda e, sa=sa, pg=pg: e.tensor_tensor(out=sa.ap, in0=sa.ap, in1=pg.ap, op=ALU.mult),
                                 reads=[pg.r], writes=[sa.r])
                            P.op("act", lambda e, sa=sa: e.activation(out=sa.ap, in_=sa.ap, func=AF.Sigmoid, scale=GELU_K), writes=[sa.r])
                            P.op("dve", lambda e, sa=sa, pg=pg, gg=ggs[sb_]: e.tensor_tensor(out=gg.ap, in0=sa.ap, in1=pg.ap, op=ALU.mult),
                                 reads=[sa.r, pg.r], writes=[ggs[sb_].r])
                        P.op("dve", lambda e, xb=xb, xc=xc, c=c: e.tensor_scalar(out=xc.ap, in0=xb.ap[:, 3:3 + TT], scalar1=CW.ap[:, c, 3:4],
                                                                                 scalar2=LV.ap[:, 0, c:c + 1], op0=ALU.mult, op1=ALU.add),
                             reads=[xb.r, CW.r, LV.r], writes=[xc.r])
                        for kk in (2, 1, 0):
                            P.op("dve", lambda e, xb=xb, xc=xc, c=c, kk=kk: e.scalar_tensor_tensor(
                                out=xc.ap, in0=xb.ap[:, kk:kk + TT], scalar=CW.ap[:, c, kk:kk + 1], in1=xc.ap, op0=ALU.mult, op1=ALU.add),
                                reads=[xb.r, CW.r], writes=[xc.r])
                        P.op("dve", lambda e, xb=xb, c=c: e.tensor_copy(out=HALO.ap[:, c, 0:3], in_=xb.ap[:, TT:TT + 3]),
                             reads=[xb.r], writes=[HALO.r], ss=True)
                        P.op("act", lambda e, xc=xc, xcb=xcbs[sb_]: e.activation(out=xcb.ap, in_=xc.ap, func=AF.Copy),
                             reads=[xc.r], writes=[xcbs[sb_].r])
                    for ds_ in range(2):
                        d = 2 * n + ds_
                        gp = self.rot("ps_g", 2)
                        pr, pi = self.PS[2 * gp], self.PS[2 * gp + 1]

                        def gmm(e, ds_=ds_, gaw=gaw, gxw=gxw, pr=pr, pi=pi, x0=xcbs[0], x1=xcbs[1]):
                            e.matmul(pr.ap, gaw.ap[:, 0, 128 * ds_:128 * ds_ + 128], x0.ap, start=True, stop=False)
                            e.matmul(pr.ap, gaw.ap[:, 1, 128 * ds_:128 * ds_ + 128], x1.ap, start=False, stop=True)
                            e.matmul(pi.ap, gxw.ap[:, 0, 128 * ds_:128 * ds_ + 128], x0.ap, start=True, stop=False)
                            return e.matmul(pi.ap, gxw.ap[:, 1, 128 * ds_:128 * ds_ + 128], x1.ap, start=False, stop=True)
                        P.op("pe", gmm, reads=[gaw.r, gxw.r, xcbs[0].r, xcbs[1].r], writes=[pr.r, pi.r])
                        ri = self.rot("rt", 2)
                        rt, it, aa, hso = RT[ri], IT[ri], AA[ri], HSOs[ri]
                        xc = xcs[ds_]
                        P.op("act", lambda e, rt=rt, d=d, pr=pr: e.activation(out=rt.ap, in_=pr.ap, func=AF.Sigmoid, bias=LV.ap[:, 1, d:d + 1], scale=1.0),
                             reads=[pr.r, LV.r], writes=[rt.r])
                        P.op("act", lambda e, it=it, d=d, pi=pi: e.activation(out=it.ap, in_=pi.ap, func=AF.Sigmoid, bias=LV.ap[:, 2, d:d + 1], scale=1.0),
                             reads=[pi.r, LV.r], writes=[it.r])
                        P.op("act", lambda e, rt=rt, d=d, aa=aa: e.activation(out=aa.ap, in_=rt.ap, func=AF.Exp, scale=LC.ap[:, 2, d:d + 1]),
                             reads=[rt.r, LC.r], writes=[aa.r])
                        P.op("act", lambda e, rt=rt, d=d: e.activation(out=rt.ap, in_=rt.ap, func=AF.Exp, scale=LC.ap[:, 3, d:d + 1]),
                             reads=[LC.r], writes=[rt.r])
                        P.op("act", lambda e, rt=rt: e.activation(out=rt.ap, in_=rt.ap, func=AF.Sqrt, scale=-1.0, bias=1.0), writes=[rt.r])
                        P.op("dve", lambda e, it=it, xc=xc: e.tensor_tensor(out=xc.ap, in0=xc.ap, in1=it.ap, op=ALU.mult),
                             reads=[it.r], writes=[xc.r])
                        P.op("dve", lambda e, rt=rt, xc=xc: e.tensor_tensor(out=xc.ap, in0=xc.ap, in1=rt.ap, op=ALU.mult),
                             reads=[rt.r], writes=[xc.r])
                        P.op("dve", lambda e, d=d, aa=aa, xc=xc, hso=hso: e.tensor_tensor_scan(
                            out=hso.ap, data0=aa.ap, data1=xc.ap, initial=STATE.ap[:, d:d + 1], op0=ALU.mult, op1=ALU.add),
                            reads=[aa.r, xc.r, STATE.r], writes=[hso.r], ss=True)
                        P.op("dve", lambda e, d=d, hso=hso: e.tensor_copy(out=STATE.ap[:, d:d + 1], in_=hso.ap[:, TT - 1:TT]),
                             reads=[hso.r], writes=[STATE.r], ss=True)
                        if not state_only:
                            P.op("dve", lambda e, d=d, hso=hso, gg=ggs[ds_], tsl=tsl: e.tensor_tensor(
                                out=self.GT.ap[:, d, tsl], in0=hso.ap, in1=gg.ap, op=ALU.mult),
                                reads=[hso.r, ggs[ds_].r], writes=[self.GT.r])

        if self.paired:
            for b in range(NB):
                lru_block(b, True)
                self.prefetch_ht()
            P.ss = True
            LX = Buf(self.sb("LX", [128, 128], F32)[:], "LX")
            PM2 = Buf(self.sb("PM2", [128, 1], F32)[:], "PM2")
            halo_flat = HALO.ap.rearrange("p a b -> p (a b)")
            P.op("sp", lambda e: e.dma_start(out=PM2.ap, in_=self.pmask), writes=[PM2.r], dma_key="PM2")
            P.op("dve", lambda e: e.memset(LX.ap, 0.0), writes=[LX.r])
            P.op("dve", lambda e: e.tensor_copy(out=LX.ap[:, 0:KR], in_=STATE.ap), reads=[STATE.r], writes=[LX.r])
            P.op("dve", lambda e: e.tensor_copy(out=LX.ap[:, KR:KR + 4 * KR], in_=halo_flat), reads=[HALO.r], writes=[LX.r])
            P.op("sp", lambda e: e.dma_start(out=self.LIN, in_=LX.ap), reads=[LX.r], writes=[self.dres("LIN")], dma_key="LX")
            P.op("pool", lambda e: e.collective_compute("AllGather", ALU.bypass, replica_groups=[[0, 1], [2, 3], [4, 5], [6, 7]],
                                                        ins=[self.LIN], outs=[self.LOUT]),
                 reads=[self.dres("LIN")], writes=[self.dres("LOUT")], dma_key="ccL", cc=True)
            P.op("sp", lambda e: e.dma_start(out=LX.ap, in_=self.LOUT[0:128, :]), reads=[self.dres("LOUT")], writes=[LX.r], dma_key="LX")
            P.op("dve", lambda e: e.tensor_scalar(out=STATE.ap, in0=LX.ap[:, 0:KR], scalar1=PM2.ap[:, 0:1], scalar2=None, op0=ALU.mult),
                 reads=[LX.r, PM2.r], writes=[STATE.r])
            P.op("dve", lambda e: e.tensor_scalar(out=halo_flat, in0=LX.ap[:, KR:KR + 4 * KR], scalar1=PM2.ap[:, 0:1], scalar2=None, op0=ALU.mult),
                 reads=[LX.r, PM2.r], writes=[HALO.r])
        for b in range(NB):
            lru_block(b, False)
            self.prefetch_ht()
            self.out_proj(b, w_out, KR, layer, sub, x_src, x_res_fn, nxt, last)
        self.fence([x.r for x in scratch] + [self.GT.r], [self.GT.r])

    def build(self):
        self.alloc()
        self.stage_consts()
        layers = sorted(set(l for l, s in self.sub_list))
        for l in layers:
            self.stage_ada(l)
        l0, s0 = self.sub_list[0]
        self.stage_prologue(l0, s0)
        self.ht_plan = []
        for idx_, (l, s_) in enumerate(self.sub_list):
            rep = 2 if (s_ == 1 and l % 2 == 1 and self.paired) else 1
            self.ht_plan += [(idx_, b_) for b_ in range(self.NB)] * rep
        self.hs_ver = {b_: -1 for b_ in range(self.NB)}
        self.cur_idx = 0
        self.ht_pos = 0
        self.ht_loaded = 0
        for idx, (l, s) in enumerate(self.sub_list):
            self.cur_idx = idx
            first = idx == 0
            last = idx == len(self.sub_list) - 1
            nxt = None if last else self.sub_list[idx + 1]
            x_src = self.xT if first else self.XS
            x_res_fn = (lambda m, c0: self.dres("XIN")) if first else (lambda m, c0: self.dres(f"XS{m}_{c0}"))
            if s in (0, 2):
                self.stage_ffn(l, s, x_src, x_res_fn, nxt, last)
            elif l % 2 == 0:
                self.stage_attn(l, s, x_src, x_res_fn, nxt, last)
            else:
                self.stage_lru(l, s, x_src, x_res_fn, nxt, last)
        self.P.finish([r for n, r in self._dres.items() if n.startswith("OUT")])
        self.st.close()
        return self.nc


def prep_inputs(inputs, b, t0, NT, half=0, paired=False):
    f = np.float32
    x = np.asarray(inputs["x"], f)[b, t0:t0 + NT]
    m = {}
    m["xT"] = np.ascontiguousarray(x.T.reshape(KD, 128, NT))
    m["pmask"] = np.full((128, 1), float(half), f)
    m["cT"] = np.ascontiguousarray(np.asarray(inputs["c"], f)[b].reshape(KD, 128).T)
    adb = np.asarray(inputs["ada_b"], f).reshape(DEPTH, 144, 128).transpose(2, 0, 1)
    if paired:
        for l in range(DEPTH):
            m[f"ada_w_{l}"] = np.ascontiguousarray(np.asarray(inputs["ada_w"], f)[l][:, half * 9216:(half + 1) * 9216])
        m["ada_bT"] = np.ascontiguousarray(adb[:, :, half * 72:(half + 1) * 72])
    else:
        for l in range(DEPTH):
            m[f"ada_w_{l}"] = np.asarray(inputs["ada_w"], f)[l]
        m["ada_bT"] = np.ascontiguousarray(adb)
    ln = np.stack([np.asarray(inputs["ln_g"], f), np.asarray(inputs["ln_b"], f)], axis=2)
    m["lnT"] = np.ascontiguousarray(ln.reshape(DEPTH, 3, 2, KD, 128).transpose(4, 0, 1, 2, 3))
    for l in range(DEPTH):
        for wi in range(2):
            m[f"ffn_w_in_{l}_{wi}"] = np.asarray(inputs["ffn_w_in"], f)[l, wi]
            m[f"ffn_w_out_{l}_{wi}"] = np.asarray(inputs["ffn_w_out"], f)[l, wi]
    m["attn_w_qkv"] = np.asarray(inputs["attn_w_qkv"], f)[0]
    m["attn_w_o"] = np.asarray(inputs["attn_w_o"], f)[0]
    lam = np.stack([np.asarray(inputs[k], f)[0] for k in
                    ("attn_lambda_q1", "attn_lambda_k1", "attn_lambda_q2", "attn_lambda_k2")])
    m["lamv"] = np.ascontiguousarray(np.broadcast_to(lam[None], (128, 4, 128)))
    m["sublnT"] = np.ascontiguousarray(np.asarray(inputs["attn_subln_g"], f)[0].reshape(2, 128).T)
    m["lru_w_in"] = np.asarray(inputs["lru_w_in"], f)[0]
    m["lru_w_out"] = np.asarray(inputs["lru_w_out"], f)[0]
    m["lru_ga"] = np.asarray(inputs["lru_gate_a_w"], f)[0]
    m["lru_gx"] = np.asarray(inputs["lru_gate_x_w"], f)[0]
    m["cwT"] = np.ascontiguousarray(np.asarray(inputs["lru_conv_w"], f)[0].reshape(4, KR, 128).transpose(2, 1, 0))
    lv = np.stack([np.asarray(inputs[k], f)[0] for k in
                   ("lru_conv_b", "lru_gate_a_b", "lru_gate_x_b", "lru_lambda")])
    m["lruv"] = np.ascontiguousarray(lv.reshape(4, KR, 128).transpose(2, 0, 1))
    return m


FULL_SUBS = [(0, 0), (0, 1), (0, 2), (1, 0), (1, 1), (1, 2)]


def run(inputs, sub_list=FULL_SUBS, NT=2048, n_cores=8):
    mk = MK(NT, sub_list, paired=(NT == 2048))
    nc = mk.build()
    B, S = 4, 4096
    per_seq = S // NT
    in_maps = []
    for c in range(n_cores):
        b, h = c // per_seq, c % per_seq
        full = prep_inputs(inputs, b, h * NT, NT, half=h, paired=mk.paired)
        in_maps.append({k: full[k] for k in mk.in_names})
    res = run_bass_kernel_spmd(nc, in_maps, core_ids=list(range(n_cores)))
    out = np.empty((B, S, D), np.float32)
    for c in range(n_cores):
        b, h = c // per_seq, c % per_seq
        o = res.results[c]["outT"].reshape(D, NT)
        out[b, h * NT:(h + 1) * NT] = o.T
    return out


def kernel(**inputs):
    return run(inputs)
```

```python
import math
from contextlib import ExitStack

import numpy as np
import concourse.bass as bass
import concourse.mybir as mybir
from concourse.bass_utils import run_bass_kernel_spmd

F32 = mybir.dt.float32
BF16 = mybir.dt.bfloat16
AF = mybir.ActivationFunctionType
ALU = mybir.AluOpType

D = 2048
KD = 16
DFF = 5632
KF = 44
DRNN = 2560
KR = 20
DEPTH = 2
ALPHA = (2 * DEPTH) ** 0.25
LN_EPS = 1e-5
EPS_P = LN_EPS / (ALPHA * ALPHA)
RMS_EPS = 1e-5
TB = 1024
TT = 512
LRU_C = 8.0


class Res:
    __slots__ = ("name", "last_w", "readers")

    def __init__(self, name):
        self.name = name
        self.last_w = None
        self.readers = []


class Prog:
    ENGS = ("pe", "act", "dve", "pool", "sp")

    def __init__(self, nc, stack):
        self.nc = nc
        self.stack = stack
        self.e = dict(pe=nc.tensor, act=nc.scalar, dve=nc.vector, pool=nc.gpsimd, sp=nc.sync)
        self.q = {k: [] for k in self.ENGS}
        self.cnt = {k: 0 for k in self.ENGS}
        self.csem = {k: stack.enter_context(nc.semaphore("c_" + k)) for k in self.ENGS}
        self.seen = {k: {} for k in self.ENGS}
        self.dsem = {}
        self.dinc = {}
        self.ss = True

    def _dma_sem(self, key):
        if key not in self.dsem:
            self.dsem[key] = [self.stack.enter_context(self.nc.semaphore("d_" + key)), 0]
        return self.dsem[key]

    def op(self, eng, fn, reads=(), writes=(), dma_key=None, cc=False, ss=None):
        if ss is None:
            ss = self.ss
        deps = []
        for r in reads:
            if r.last_w is not None:
                deps.append(r.last_w)
        for w in writes:
            if w.last_w is not None:
                deps.append(w.last_w)
            deps.extend(w.readers)
        waits = {}
        for d in deps:
            if d[0] == "c":
                _, deng, idx = d
                if deng == eng and dma_key is None and (eng in ("pe", "sp") or not ss):
                    continue
                sem, val, skey = self.csem[deng], idx, "c_" + deng
            else:
                _, key, c = d
                sem, val, skey = self.dsem[key][0], self.dinc.get(key, 16) * c, "d_" + key
            if self.seen[eng].get(skey, 0) >= val:
                continue
            if skey not in waits or waits[skey][1] < val:
                waits[skey] = (sem, val)
        for skey, (sem, val) in waits.items():
            self.seen[eng][skey] = val
        wl = list(waits.values())
        if dma_key is None:
            self.cnt[eng] += 1
            tok = ("c", eng, self.cnt[eng])
            mysem, inc = self.csem[eng], 1
        else:
            ds = self._dma_sem(dma_key)
            ds[1] += 1
            tok = ("d", dma_key, ds[1])
            mysem, inc = ds[0], 16
            if cc:
                self.dinc[dma_key] = 1
                inc = None
        engobj = self.e[eng]

        def emit():
            for sem, val in wl:
                engobj.wait_ge(sem, val)
            ins = fn(engobj)
            if inc is None:
                ins.then_inc(mysem)
            else:
                ins.then_inc(mysem, inc)

        self.q[eng].append(emit)
        for w in writes:
            w.last_w = tok
            w.readers = []
        for r in reads:
            if r not in writes:
                r.readers.append(tok)
        return tok

    def finish(self, final_res):
        wl = []
        for r in final_res:
            for d in ([r.last_w] if r.last_w else []) + list(r.readers):
                if d[0] == "c":
                    wl.append((self.csem[d[1]], d[2]))
                else:
                    wl.append((self.dsem[d[1]][0], self.dinc.get(d[1], 16) * d[2]))
        mx = {}
        for sem, val in wl:
            k = id(sem)
            if k not in mx or mx[k][1] < val:
                mx[k] = (sem, val)
        wl = list(mx.values())
        nc, q = self.nc, self.q
        with nc.Block() as block:
            @block.tensor
            def _(eng):
                for f in q["pe"]:
                    f()

            @block.scalar
            def _(eng):
                for f in q["act"]:
                    f()

            @block.vector
            def _(eng):
                for f in q["dve"]:
                    f()

            @block.gpsimd
            def _(eng):
                for f in q["pool"]:
                    f()

            @block.sync
            def _(eng):
                for f in q["sp"]:
                    f()
                for sem, val in wl:
                    eng.wait_ge(sem, val)


class Buf:
    __slots__ = ("ap", "r", "key")

    def __init__(self, ap, name):
        self.ap = ap
        self.r = Res(name)
        self.key = name


class MK:
    def __init__(self, NT, sub_list, paired=False):
        self.paired = paired
        self.NT = NT
        self.NB = NT // TB
        self.sub_list = sub_list
        self.nc = bass.Bass("TRN2", target_bir_lowering=False)
        self.st = ExitStack()
        self.P = Prog(self.nc, self.st)
        self._rr = {}
        self.in_names = []

    def din(self, name, shape, dt=F32):
        self.in_names.append(name)
        return self.nc.dram_tensor(name, list(shape), dt, kind="ExternalInput").ap()

    def dscr(self, name, shape, dt):
        return self.nc.dram_tensor(name, list(shape), dt, kind="Internal").ap()

    def sb(self, name, shape, dt):
        t = self.st.enter_context(self.nc.sbuf_tensor(name, list(shape), dt))
        return t

    def rot(self, name, n):
        i = self._rr.get(name, 0)
        self._rr[name] = i + 1
        return i % n

    def dres(self, name):
        if not hasattr(self, "_dres"):
            self._dres = {}
        if name not in self._dres:
            self._dres[name] = Res(name)
        return self._dres[name]

    def alloc(self):
        nc, st = self.nc, self.st
        NT = self.NT
        subs = set(self.sub_list)
        layers = sorted(set(l for l, _ in subs))
        self.xT = self.din("xT", [KD, 128, NT])
        self.cT = self.din("cT", [128, KD])
        self.NADA = 72 if self.paired else 144
        self.ada_w = {l: self.din(f"ada_w_{l}", [D, self.NADA * 128]) for l in layers}
        self.ada_bT = self.din("ada_bT", [128, DEPTH, self.NADA])
        if self.paired:
            self.MIN = {l: nc.dram_tensor(f"MIN{l}", [128, 72], F32).ap() for l in layers}
            self.MOUT = {l: nc.dram_tensor(f"MOUT{l}", [256, 72], F32).ap() for l in layers}
        self.lnT = self.din("lnT", [128, DEPTH, 3, 2, KD])
        self.ffn_w_in, self.ffn_w_out = {}, {}
        for (l, sb_) in sorted(subs):
            if sb_ in (0, 2):
                wi = 0 if sb_ == 0 else 1
                self.ffn_w_in[(l, wi)] = self.din(f"ffn_w_in_{l}_{wi}", [D, 2 * DFF])
                self.ffn_w_out[(l, wi)] = self.din(f"ffn_w_out_{l}_{wi}", [DFF, D])
        if (0, 1) in subs:
            self.attn_w_qkv = self.din("attn_w_qkv", [D, 3 * D])
            self.attn_w_o = self.din("attn_w_o", [D, D])
            self.lamv = self.din("lamv", [128, 4, 128])
            self.sublnT = self.din("sublnT", [128, 2])
        if (1, 1) in subs:
            self.lru_w_in = self.din("lru_w_in", [D, 2 * DRNN])
            self.lru_w_out = self.din("lru_w_out", [DRNN, D])
            self.lru_ga = self.din("lru_ga", [10, 256, 256])
            self.lru_gx = self.din("lru_gx", [10, 256, 256])
            self.cwT = self.din("cwT", [128, KR, 4])
            self.lruv = self.din("lruv", [128, 4, KR])
        self.outT = nc.dram_tensor("outT", [KD, 128, NT], F32, kind="ExternalOutput").ap()
        self.XS = self.dscr("XS", [KD, 128, NT], F32)
        self.HS = self.dscr("HS", [KD, 128, NT], BF16)
        self.ZS = self.dscr("ZS", [KD, 128, TB], F32)
        self.QKS = self.dscr("QKS", [16, 128, NT], BF16)
        self.KIN = nc.dram_tensor("KIN", [D, NT], BF16).ap()
        self.VIN = nc.dram_tensor("VIN", [NT, D], BF16).ap()
        if self.paired:
            self.pmask = self.din("pmask", [128, 1])
            self.KOUT = [nc.dram_tensor(f"KOUT{i}", [1024, NT], BF16).ap() for i in range(D // 512)]
            self.VOUT = [nc.dram_tensor(f"VOUT{i}", [1024, D], BF16).ap() for i in range(NT // 512)]
            self.LIN = nc.dram_tensor("LIN", [128, 128], F32).ap()
            self.LOUT = nc.dram_tensor("LOUT", [256, 128], F32).ap()
        self.ONS = self.dscr("ONS", [KD, 128, NT], BF16)
        self.HT_t = self.sb("HT", [128, KD * TB], BF16)
        self.GT_t = self.sb("GT", [128, KF * TB], BF16)
        self.WS_t = [self.sb(f"WS{i}", [128, KF * 256], BF16) for i in range(2)]
        self.HT = Buf(self.HT_t[:].rearrange("p (k n) -> p k n", k=KD), "HT")
        self.GT = Buf(self.GT_t[:].rearrange("p (k n) -> p k n", k=KF), "GT")
        self.WS = [Buf(self.WS_t[i][:], f"WS{i}") for i in range(2)]
        self.WSB = [Res(f"WS{i}b") for i in range(2)]
        self.SA = [Buf(self.sb(f"SA{i}", [128, TT], F32)[:], f"SA{i}") for i in range(2)]
        self.MOD = Buf(self.sb("MOD", [128, DEPTH, 9, KD], F32)[:], "MOD")
        self.SC1P = Buf(self.sb("SC1P", [128, DEPTH, 3, KD], F32)[:], "SC1P")
        self.GATE = Buf(self.sb("GATE", [128, DEPTH, 3, KD], F32)[:], "GATE")
        self.LN = Buf(self.sb("LN", [128, DEPTH, 3, 2, KD], F32)[:], "LN")
        self.ADB = Buf(self.sb("ADB", [128, DEPTH, self.NADA], F32)[:], "ADB")
        self.CT = Buf(self.sb("CTs", [128, KD], F32)[:], "CT")
        self.CA = Buf(self.sb("CA", [128, KD], BF16)[:], "CA")
        self.ONES = Buf(self.sb("ONES", [128, 128], F32)[:], "ONES")
        self.ONESB = Buf(self.sb("ONESB", [128, 128], BF16)[:], "ONESB")
        self.EPSP = Buf(self.sb("EPSP", [128, 1], F32)[:], "EPSP")
        self.RMSE = Buf(self.sb("RMSE", [128, 1], F32)[:], "RMSE")
        self.PS = []
        for i in range(8):
            t = st.enter_context(nc.psum_tensor(f"PS{i}", [128, TT], F32))
            self.PS.append(Buf(t[:], f"PS{i}"))
        self.ov = {}
        for i in range(3):
            self.ov[f"XI{i}"] = Buf(self.sb(f"XI{i}", [128, TT], F32)[:], f"XI{i}")
        for i in range(2):
            self.ov[f"HO{i}"] = Buf(self.sb(f"HO{i}", [128, TT], BF16)[:], f"HO{i}")
        for nm in ("S", "Q", "RSTD", "NMR"):
            self.ov[nm] = Buf(self.sb(nm, [128, TB], F32)[:], nm)
        self.ov_fresh = set()

    def ovw(self, b):
        return [b.r]

    def ht_load_writes(self):
        return [self.HT.r]

    def stage_consts(self):
        P = self.P
        P.op("dve", lambda e: e.memset(self.ONES.ap, 1.0), writes=[self.ONES.r])
        P.op("dve", lambda e: e.memset(self.ONESB.ap, 1.0), writes=[self.ONESB.r])
        P.op("dve", lambda e: e.memset(self.EPSP.ap, EPS_P), writes=[self.EPSP.r])
        P.op("dve", lambda e: e.memset(self.RMSE.ap, RMS_EPS), writes=[self.RMSE.r])
        P.op("sp", lambda e: e.dma_start(out=self.CT.ap, in_=self.cT), writes=[self.CT.r], dma_key="CT")
        P.op("sp", lambda e: e.dma_start(out=self.ADB.ap, in_=self.ada_bT), writes=[self.ADB.r], dma_key="ADB")
        P.op("sp", lambda e: e.dma_start(out=self.LN.ap, in_=self.lnT), writes=[self.LN.r], dma_key="LN")
        P.op("act", lambda e: e.activation(out=self.CA.ap, in_=self.CT.ap, func=AF.Silu),
             reads=[self.CT.r], writes=[self.CA.r])

    def stage_ada(self, layer):
        P = self.P
        wv = self.ada_w[layer].rearrange("(kc p) n -> p kc n", p=128)
        ps = self.PS[7]
        for gq in range(self.NADA // 4):
            s = self.rot("ws", 2)
            ws = self.WS[s]
            wsv = ws.ap[:, 0:KD * 512].rearrange("p (k n) -> p k n", k=KD)
            P.op("pool", lambda e, wsv=wsv, gq=gq: e.dma_start(out=wsv, in_=wv[:, :, 512 * gq:512 * gq + 512]),
                 writes=[ws.r, self.WSB[s]], dma_key=ws.key)

            def mm(e, wsv=wsv, gq=gq):
                ins = None
                for c4 in range(4):
                    j = 4 * gq + c4
                    for k in range(KD):
                        ins = e.matmul(ps.ap[:, j:j + 1], wsv[:, k, 128 * c4:128 * c4 + 128],
                                       self.CA.ap[:, k:k + 1], start=(k == 0), stop=(k == KD - 1))
                return ins
            P.op("pe", mm, reads=[ws.r, self.WSB[s], self.CA.r], writes=[ps.r])
        mod_l = self.MOD.ap[:, layer].rearrange("p a k -> p (a k)")
        NA = self.NADA
        P.op("dve", lambda e: e.tensor_tensor(out=mod_l[:, 0:NA], in0=ps.ap[:, 0:NA], in1=self.ADB.ap[:, layer], op=ALU.add),
             reads=[ps.r, self.ADB.r], writes=[self.MOD.r])
        if self.paired:
            P.op("sp", lambda e: e.dma_start(out=self.MIN[layer], in_=mod_l[:, 0:NA]), reads=[self.MOD.r],
                 writes=[self.dres(f"MIN{layer}")], dma_key="MODs")
            P.op("pool", lambda e: e.collective_compute("AllGather", ALU.bypass, replica_groups=[[0, 1], [2, 3], [4, 5], [6, 7]],
                                                        ins=[self.MIN[layer]], outs=[self.MOUT[layer]]),
                 reads=[self.dres(f"MIN{layer}")], writes=[self.dres(f"MOUT{layer}")], dma_key=f"ccM{layer}", cc=True)
            P.op("sp", lambda e: e.dma_start(out=mod_l.rearrange("p (r c) -> p r c", r=2),
                                             in_=self.MOUT[layer].rearrange("(r p) c -> p r c", p=128)),
                 reads=[self.dres(f"MOUT{layer}")], writes=[self.MOD.r], dma_key="MODl")
        for s in range(3):
            w = 1.0 if s == 1 else 0.5
            P.op("dve", lambda e, s=s: e.tensor_scalar(out=self.SC1P.ap[:, layer, s], in0=self.MOD.ap[:, layer, 3 * s + 1],
                                                       scalar1=1.0, scalar2=None, op0=ALU.add),
                 reads=[self.MOD.r], writes=[self.SC1P.r])
            P.op("dve", lambda e, s=s, w=w: e.tensor_scalar(out=self.GATE.ap[:, layer, s], in0=self.MOD.ap[:, layer, 3 * s + 2],
                                                            scalar1=1.0, scalar2=w / ALPHA, op0=ALU.add, op1=ALU.mult),
                 reads=[self.MOD.r], writes=[self.GATE.r])

    def stage_prologue(self, layer, sub):
        P = self.P
        self.ht_load_writes()
        for b in range(self.NB):
            for m in range(KD):
                for t in range(TB // TT):
                    xi, ho = self.ov[f"XI{self.rot('xi', 3)}"], self.ov[f"HO{self.rot('ho', 2)}"]
                    c0 = b * TB + t * TT
                    P.op("sp", lambda e, xi=xi, m=m, c0=c0: e.dma_start(out=xi.ap, in_=self.xT[m, :, c0:c0 + TT]),
                         writes=self.ovw(xi), dma_key=xi.key)
                    P.op("act", lambda e, xi=xi, ho=ho, m=m: e.activation(
                        out=ho.ap, in_=xi.ap, func=AF.Identity,
                        scale=self.SC1P.ap[:, layer, sub, m:m + 1], bias=self.MOD.ap[:, layer, 3 * sub, m:m + 1]),
                        reads=[xi.r, self.SC1P.r, self.MOD.r], writes=self.ovw(ho))
                    P.op("sp", lambda e, ho=ho, m=m, c0=c0: e.dma_start(out=self.HS[m, :, c0:c0 + TT], in_=ho.ap),
                         reads=[ho.r], writes=[self.dres(f"HS{m}_{c0}")], dma_key=ho.key)

    def load_ht(self, b):
        idx, pb = self.ht_plan[self.ht_pos]
        assert pb == b and self.hs_ver[b] == idx - 1, (self.ht_plan, self.ht_pos, b, self.hs_ver)
        if self.ht_loaded <= self.ht_pos:
            self._load_ht(b)
            self.ht_loaded = self.ht_pos + 1
        self.ht_pos += 1

    def prefetch_ht(self):
        if self.ht_loaded == self.ht_pos and self.ht_pos < len(self.ht_plan):
            idx, b = self.ht_plan[self.ht_pos]
            if self.hs_ver[b] != idx - 1:
                return
            self._load_ht(b)
            self.ht_loaded = self.ht_pos + 1

    def _load_ht(self, b):
        P = self.P
        rl = [self.dres(f"HS{m}_{b * TB + t * TT}") for m in range(KD) for t in range(TB // TT)]
        P.op("sp", lambda e: e.dma_start(out=self.HT.ap, in_=self.HS[:, :, b * TB:(b + 1) * TB].rearrange("k p n -> p k n")),
             reads=rl, writes=self.ht_load_writes(), dma_key="HT")

    def out_proj(self, b, wv, KC, layer, sub, x_src, x_src_res, nxt, last):
        P = self.P
        S, Q, RSTD, NMR = self.ov["S"], self.ov["Q"], self.ov["RSTD"], self.ov["NMR"]
        P.op("dve", lambda e: e.memset(S.ap, 0.0), writes=self.ovw(S))
        P.op("dve", lambda e: e.memset(Q.ap, 0.0), writes=self.ovw(Q))
        steps = [(gq, t, mm) for gq in range(8) for t in range(TB // TT) for mm in range(2)]
        def load_x(step):
            gq, t, mm = step
            m = 2 * gq + mm
            i = self.rot("xi", 3)
            xi = self.ov[f"XI{i}"]
            c0 = b * TB + t * TT
            P.op("sp", lambda e: e.dma_start(out=xi.ap, in_=x_src[m, :, c0:c0 + TT]),
                 reads=[x_src_res(m, c0)], writes=self.ovw(xi), dma_key=xi.key)
            return xi
        xi_next = load_x(steps[0])
        ws = None
        for si, (gq, t, mm) in enumerate(steps):
            m = 2 * gq + mm
            if t == 0 and mm == 0:
                s = self.rot("ws", 2)
                ws = self.WS[s]
                wsv = ws.ap[:, 0:KC * 256].rearrange("p (k n) -> p k n", k=KC)
                wsr = [ws.r, self.WSB[s]]
                P.op("pool", lambda e, wsv=wsv, gq=gq: e.dma_start(out=wsv, in_=wv[:, :, 256 * gq:256 * gq + 256]),
                     writes=wsr, dma_key=ws.key)
            xi = xi_next
            if si + 1 < len(steps):
                xi_next = load_x(steps[si + 1])
            ps = self.PS[self.rot("ps_o", 2)]

            def mmf(e, wsv=wsv, mm=mm, t=t, ps=ps):
                ins = None
                for k in range(KC):
                    ins = e.matmul(ps.ap, wsv[:, k, 128 * mm:128 * mm + 128], self.GT.ap[:, k, t * TT:(t + 1) * TT],
                                   start=(k == 0), stop=(k == KC - 1))
                return ins
            P.op("pe", mmf, reads=wsr + [self.GT.r], writes=[ps.r])
            zo, sq = xi, self.SA[self.rot("sa", 2)]
            P.op("dve", lambda e, ps=ps, zo=zo, xi=xi, m=m: e.scalar_tensor_tensor(
                out=zo.ap, in0=ps.ap, scalar=self.GATE.ap[:, layer, sub, m:m + 1], in1=xi.ap, op0=ALU.mult, op1=ALU.add),
                reads=[ps.r, xi.r, self.GATE.r], writes=self.ovw(zo))
            P.op("sp", lambda e, zo=zo, m=m, t=t: e.dma_start(out=self.ZS[m, :, t * TT:(t + 1) * TT], in_=zo.ap),
                 reads=[zo.r], writes=[self.dres(f"ZS{m}_{t}")], dma_key=zo.key)
            P.op("act", lambda e, zo=zo, sq=sq: e.activation(out=sq.ap, in_=zo.ap, func=AF.Square),
                 reads=[zo.r], writes=self.ovw(sq))
            P.op("dve", lambda e, zo=zo, t=t: e.tensor_tensor(out=S.ap[:, t * TT:(t + 1) * TT], in0=S.ap[:, t * TT:(t + 1) * TT],
                                                             in1=zo.ap, op=ALU.add),
                 reads=[zo.r], writes=[S.r])
            P.op("dve", lambda e, sq=sq, t=t: e.tensor_tensor(out=Q.ap[:, t * TT:(t + 1) * TT], in0=Q.ap[:, t * TT:(t + 1) * TT],
                                                             in1=sq.ap, op=ALU.add),
                 reads=[sq.r], writes=[Q.r])
        P.ss = True
        for t in range(TB // TT):
            pa, pb = self.PS[2], self.PS[3]
            sl = slice(t * TT, (t + 1) * TT)
            P.op("pe", lambda e, sl=sl: e.matmul(pa.ap, self.ONES.ap, S.ap[:, sl], start=True, stop=True),
                 reads=[S.r, self.ONES.r], writes=[pa.r])
            P.op("pe", lambda e, sl=sl: e.matmul(pb.ap, self.ONES.ap, Q.ap[:, sl], start=True, stop=True),
                 reads=[Q.r, self.ONES.r], writes=[pb.r])
            t1 = self.SA[self.rot("sa", 2)]
            P.op("act", lambda e, sl=sl: e.activation(out=NMR.ap[:, sl], in_=pa.ap, func=AF.Copy, scale=1.0 / D),
                 reads=[pa.r], writes=self.ovw(NMR))
            P.op("dve", lambda e, sl=sl, t1=t1: e.tensor_tensor(out=t1.ap, in0=NMR.ap[:, sl], in1=NMR.ap[:, sl], op=ALU.mult),
                 reads=[NMR.r], writes=self.ovw(t1))
            P.op("dve", lambda e, sl=sl, t1=t1: e.scalar_tensor_tensor(out=RSTD.ap[:, sl], in0=pb.ap, scalar=1.0 / D, in1=t1.ap,
                                                               op0=ALU.mult, op1=ALU.subtract),
                 reads=[pb.r, t1.r], writes=self.ovw(RSTD))
            P.op("act", lambda e, sl=sl: e.activation(out=RSTD.ap[:, sl], in_=RSTD.ap[:, sl], func=AF.Sqrt,
                                                     bias=self.EPSP.ap[:, 0:1], scale=1.0),
                 reads=[self.EPSP.r], writes=[RSTD.r])
            P.op("dve", lambda e, sl=sl: e.reciprocal(out=RSTD.ap[:, sl], in_=RSTD.ap[:, sl]), writes=[RSTD.r])
            P.op("dve", lambda e, sl=sl: e.scalar_tensor_tensor(out=NMR.ap[:, sl], in0=NMR.ap[:, sl], scalar=-1.0,
                                                               in1=RSTD.ap[:, sl], op0=ALU.mult, op1=ALU.mult),
                 reads=[RSTD.r], writes=[NMR.r])
        nsteps = [(m, t) for m in range(KD) for t in range(TB // TT)]

        def load_z(step):
            m, t = step
            i = self.rot("xi", 3)
            zi = self.ov[f"XI{i}"]
            P.op("sp", lambda e: e.dma_start(out=zi.ap, in_=self.ZS[m, :, t * TT:(t + 1) * TT]),
                 reads=[self.dres(f"ZS{m}_{t}")], writes=self.ovw(zi), dma_key=zi.key)
            return zi
        zi_next = load_z(nsteps[0])
        for si, (m, t) in enumerate(nsteps):
            zi = zi_next
            if si + 1 < len(nsteps):
                zi_next = load_z(nsteps[si + 1])
            sl = slice(t * TT, (t + 1) * TT)
            t1, xo, ho = zi, zi, self.ov[f"HO{self.rot('ho', 2)}"]
            c0 = b * TB + t * TT
            P.op("dve", lambda e, zi=zi, t1=t1, sl=sl: e.tensor_tensor(out=t1.ap, in0=zi.ap, in1=RSTD.ap[:, sl], op=ALU.mult),
                 reads=[zi.r, RSTD.r], writes=self.ovw(t1))
            P.op("dve", lambda e, t1=t1, sl=sl: e.tensor_tensor(out=t1.ap, in0=t1.ap, in1=NMR.ap[:, sl], op=ALU.add),
                 reads=[NMR.r], writes=[t1.r])
            P.op("act", lambda e, t1=t1, xo=xo, m=m: e.activation(
                out=xo.ap, in_=t1.ap, func=AF.Identity, scale=self.LN.ap[:, layer, sub, 0, m:m + 1],
                bias=self.LN.ap[:, layer, sub, 1, m:m + 1]),
                reads=[t1.r, self.LN.r], writes=self.ovw(xo))
            if last:
                P.op("sp", lambda e, xo=xo, m=m, c0=c0: e.dma_start(out=self.outT[m, :, c0:c0 + TT], in_=xo.ap),
                     reads=[xo.r], writes=[self.dres(f"OUT{m}_{c0}")], dma_key=xo.key)
            else:
                P.op("sp", lambda e, xo=xo, m=m, c0=c0: e.dma_start(out=self.XS[m, :, c0:c0 + TT], in_=xo.ap),
                     reads=[xo.r], writes=[self.dres(f"XS{m}_{c0}")], dma_key=xo.key)
                nl, ns = nxt
                P.op("act", lambda e, xo=xo, ho=ho, m=m: e.activation(
                    out=ho.ap, in_=xo.ap, func=AF.Identity, scale=self.SC1P.ap[:, nl, ns, m:m + 1],
                    bias=self.MOD.ap[:, nl, 3 * ns, m:m + 1]),
                    reads=[xo.r, self.SC1P.r, self.MOD.r], writes=self.ovw(ho))
                P.op("sp", lambda e, ho=ho, m=m, c0=c0: e.dma_start(out=self.HS[m, :, c0:c0 + TT], in_=ho.ap),
                     reads=[ho.r], writes=[self.dres(f"HS{m}_{c0}")], dma_key=ho.key)

        self.hs_ver[b] = self.cur_idx

    def stage_ffn(self, layer, sub, x_src, x_res_fn, nxt, last):
        P = self.P
        P.ss = True
        wi = 0 if sub == 0 else 1
        w_in = self.ffn_w_in[(layer, wi)].rearrange("(kc p) n -> p kc n", p=128)
        w_out = self.ffn_w_out[(layer, wi)].rearrange("(kc p) n -> p kc n", p=128)
        for b in range(self.NB):
            self.load_ht(b)
            for gq in range(DFF // 256):
                s = self.rot("ws", 2)
                ws = self.WS[s]
                wsa = ws.ap[:, 0:KD * 256].rearrange("p (k f) -> p k f", k=KD)
                wsu = ws.ap[:, KD * 256:KD * 512].rearrange("p (k f) -> p k f", k=KD)
                wsr = [ws.r, self.WSB[s]]
                P.op("pool", lambda e, wsa=wsa, gq=gq: e.dma_start(out=wsa, in_=w_in[:, :, 256 * gq:256 * gq + 256]),
                     writes=[ws.r], dma_key=ws.key)
                P.op("pool", lambda e, wsu=wsu, gq=gq: e.dma_start(out=wsu, in_=w_in[:, :, DFF + 256 * gq:DFF + 256 * gq + 256]),
                     writes=[self.WSB[s]], dma_key=ws.key + "b")
                for t in range(TB // TT):
                    for sb_ in range(2):
                        j = 2 * gq + sb_
                        pp = self.rot("ps_f", 2)
                        pa, pu = self.PS[4 + 2 * pp], self.PS[5 + 2 * pp]

                        def mmf(e, wsa=wsa, wsu=wsu, sb_=sb_, t=t, pa=pa, pu=pu):
                            ins = None
                            for k in range(KD):
                                ins = e.matmul(pa.ap, wsa[:, k, 128 * sb_:128 * sb_ + 128],
                                               self.HT.ap[:, k, t * TT:(t + 1) * TT], start=(k == 0), stop=(k == KD - 1))
                            for k in range(KD):
                                ins = e.matmul(pu.ap, wsu[:, k, 128 * sb_:128 * sb_ + 128],
                                               self.HT.ap[:, k, t * TT:(t + 1) * TT], start=(k == 0), stop=(k == KD - 1))
                            return ins
                        P.op("pe", mmf, reads=wsr + [self.HT.r], writes=[pa.r, pu.r])
                        sa = self.SA[self.rot("sa", 2)]
                        P.op("act", lambda e, sa=sa, pa=pa: e.activation(out=sa.ap, in_=pa.ap, func=AF.Silu),
                             reads=[pa.r], writes=[sa.r])
                        P.op("dve", lambda e, sa=sa, pu=pu, j=j, t=t: e.tensor_tensor(
                            out=self.GT.ap[:, j, t * TT:(t + 1) * TT], in0=sa.ap, in1=pu.ap, op=ALU.mult),
                            reads=[sa.r, pu.r], writes=[self.GT.r])
            self.prefetch_ht()
            self.out_proj(b, w_out, KF, layer, sub, x_src, x_res_fn, nxt, last)

    def fence(self, old, new):
        if not hasattr(self, "FD"):
            self.FD = Buf(self.sb("FD", [128, 2], F32)[:], "FD")
        self.P.op("dve", lambda e: e.memset(self.FD.ap, 0.0), writes=[self.FD.r] + list(old) + list(new))

    def carve(self, base_t, off, name, shape, dt):
        n = int(np.prod(shape[1:]))
        units = n * (2 if dt == F32 else 1)
        ap = base_t[:, off:off + units]
        if dt == F32:
            ap = ap.bitcast(F32)
        if len(shape) == 3:
            ap = ap.rearrange("p (a b) -> p a b", a=shape[1])
        return Buf(ap, name), off + units

    def stage_attn(self, layer, sub, x_src, x_res_fn, nxt, last):
        P, NT, NB = self.P, self.NT, self.NB
        P.ss = True
        lam_init = 0.8 - 0.6 * math.exp(-0.3 * layer)
        SCALE = 128 ** -0.5
        wq = self.attn_w_qkv.rearrange("(kc p) n -> p kc n", p=128)
        wo = self.attn_w_o.rearrange("(kc p) n -> p kc n", p=128)
        NQT = NT // TT
        NKT = NT // 128
        LAMV = Buf(self.sb("LAMV", [128, 4, 128], F32)[:], "LAMV")
        ATC = Buf(self.sb("ATC", [128, 8], F32)[:], "ATC")
        SUBG = Buf(self.sb("SUBG", [128, 2], F32)[:], "SUBG")
        QS = [Buf(self.sb(f"QS{i}", [128, TT], BF16)[:], f"QS{i}") for i in range(2)]
        P.op("sp", lambda e: e.dma_start(out=LAMV.ap, in_=self.lamv), writes=[LAMV.r], dma_key="LAMV")
        P.op("sp", lambda e: e.dma_start(out=SUBG.ap, in_=self.sublnT), writes=[SUBG.r], dma_key="SUBG")
        P.op("dve", lambda e: e.tensor_scalar(out=SUBG.ap, in0=SUBG.ap, scalar1=(1.0 - lam_init), scalar2=None, op0=ALU.mult),
             writes=[SUBG.r])
        for q in range(2):
            P.op("dve", lambda e, q=q: e.tensor_tensor(out=LAMV.ap[:, 2 * q], in0=LAMV.ap[:, 2 * q], in1=LAMV.ap[:, 2 * q + 1],
                                                      op=ALU.mult), writes=[LAMV.r])
            P.op("dve", lambda e, q=q: e.reduce_sum(out=ATC.ap[:, q:q + 1], in_=LAMV.ap[:, 2 * q], axis=mybir.AxisListType.X),
                 reads=[LAMV.r], writes=[ATC.r])
        P.op("act", lambda e: e.activation(out=ATC.ap[:, 2:4], in_=ATC.ap[:, 0:2], func=AF.Exp), writes=[ATC.r])
        P.op("dve", lambda e: e.tensor_tensor(out=ATC.ap[:, 4:5], in0=ATC.ap[:, 3:4], in1=ATC.ap[:, 2:3], op=ALU.subtract),
             writes=[ATC.r])
        P.op("dve", lambda e: e.tensor_scalar(out=ATC.ap[:, 5:6], in0=ATC.ap[:, 4:5], scalar1=-lam_init, scalar2=None, op0=ALU.add),
             writes=[ATC.r])
        NEGLAM = ATC.ap[:, 5:6]

        for b in range(NB):
            self.load_ht(b)
            for g in range(16):
                s = self.rot("ws", 2)
                ws = self.WS[s]
                wsr = [ws.r, self.WSB[s]]
                wsv = ws.ap[:, 0:KD * 256].rearrange("p (k f) -> p k f", k=KD)
                P.op("pool", lambda e, wsv=wsv, g=g: e.dma_start(out=wsv, in_=wq[:, :, 256 * g:256 * g + 256]),
                     writes=wsr, dma_key=ws.key)
                for t in range(TB // TT):
                    for sb_ in range(2):
                        ps = self.PS[4 + self.rot("ps_a", 4)]

                        def mmf(e, wsv=wsv, sb_=sb_, t=t, ps=ps):
                            ins = None
                            for k in range(KD):
                                ins = e.matmul(ps.ap, wsv[:, k, 128 * sb_:128 * sb_ + 128], self.HT.ap[:, k, t * TT:(t + 1) * TT],
                                               start=(k == 0), stop=(k == KD - 1))
                            return ins
                        P.op("pe", mmf, reads=wsr + [self.HT.r], writes=[ps.r])
                        qi = self.rot("qs", 2)
                        qs = QS[qi]
                        if qi == 0:
                            P.op("act", lambda e, qs=qs, ps=ps: e.activation(out=qs.ap, in_=ps.ap, func=AF.Copy),
                                 reads=[ps.r], writes=[qs.r])
                        else:
                            P.op("dve", lambda e, qs=qs, ps=ps: e.tensor_copy(out=qs.ap, in_=ps.ap), reads=[ps.r], writes=[qs.r])
                        ch = 2 * g + sb_
                        c0 = b * TB + t * TT
                        if ch < 16:
                            dst = self.QKS[ch, :, c0:c0 + TT]
                        else:
                            dst = self.KIN[(ch - 16) * 128:(ch - 15) * 128, c0:c0 + TT]
                        P.op("sp", lambda e, qs=qs, dst=dst: e.dma_start(out=dst, in_=qs.ap),
                             reads=[qs.r], writes=[self.dres(f"QK{ch}_{c0}")], dma_key=qs.key)
            for gv in range(4):
                s = self.rot("ws", 2)
                ws = self.WS[s]
                wsr = [ws.r, self.WSB[s]]
                wsv = ws.ap[:, 0:KD * 512].rearrange("p (k f) -> p k f", k=KD)
                P.op("pool", lambda e, wsv=wsv, gv=gv: e.dma_start(out=wsv, in_=wq[:, :, 2 * D + 512 * gv:2 * D + 512 * gv + 512]),
                     writes=wsr, dma_key=ws.key)
                for tt in range(TB // 128):
                    ps = self.PS[4 + self.rot("ps_a", 4)]

                    def mmv(e, wsv=wsv, tt=tt, ps=ps):
                        ins = None
                        for k in range(KD):
                            ins = e.matmul(ps.ap, self.HT.ap[:, k, 128 * tt:128 * tt + 128], wsv[:, k, :],
                                           start=(k == 0), stop=(k == KD - 1))
                        return ins
                    P.op("pe", mmv, reads=wsr + [self.HT.r], writes=[ps.r])
                    qi = self.rot("qs", 2)
                    qs = QS[qi]
                    if qi == 0:
                        P.op("act", lambda e, qs=qs, ps=ps: e.activation(out=qs.ap, in_=ps.ap, func=AF.Copy),
                             reads=[ps.r], writes=[qs.r])
                    else:
                        P.op("dve", lambda e, qs=qs, ps=ps: e.tensor_copy(out=qs.ap, in_=ps.ap), reads=[ps.r], writes=[qs.r])
                    tk = b * (TB // 128) + tt
                    P.op("sp", lambda e, qs=qs, tk=tk, gv=gv: e.dma_start(out=self.VIN[tk * 128:(tk + 1) * 128, 512 * gv:512 * gv + 512], in_=qs.ap),
                         reads=[qs.r], writes=[self.dres(f"V{tk}_{gv}")], dma_key=qs.key)
            if b + 1 < NB:
                self.prefetch_ht()

        P.ss = True
        NPREV = NT // 128 if self.paired else 0
        if self.paired:
            NEGB = Buf(self.sb("NEGB", [128, 1], F32)[:], "NEGB")
            PM = Buf(self.sb("PM", [128, 1], F32)[:], "PM")
            P.op("sp", lambda e: e.dma_start(out=PM.ap, in_=self.pmask), writes=[PM.r], dma_key="PM")
            P.op("dve", lambda e: e.tensor_scalar(out=NEGB.ap, in0=PM.ap, scalar1=-1.0, scalar2=30000.0, op0=ALU.add, op1=ALU.mult),
                 reads=[PM.r], writes=[NEGB.r])
            grp = [[0, 1], [2, 3], [4, 5], [6, 7]]
            for i in range(D // 512):
                rk = [self.dres(f"QK{16 + 4 * i + cc_}_{c0}") for cc_ in range(4) for c0 in range(0, NT, TT)]
                P.op("pool", lambda e, i=i: e.collective_compute("AllGather", ALU.bypass, replica_groups=grp,
                                                                 ins=[self.KIN[512 * i:512 * i + 512, :]], outs=[self.KOUT[i]]),
                     reads=rk, writes=[self.dres(f"KOUT{i}")], dma_key=f"ccK{i}", cc=True)
            for i in range(NT // 512):
                rv = [self.dres(f"V{tk}_{gv}") for tk in range(4 * i, 4 * i + 4) for gv in range(4)]
                P.op("pool", lambda e, i=i: e.collective_compute("AllGather", ALU.bypass, replica_groups=grp,
                                                                 ins=[self.VIN[512 * i:512 * i + 512, :]], outs=[self.VOUT[i]]),
                     reads=rv, writes=[self.dres(f"VOUT{i}")], dma_key=f"ccV{i}", cc=True)

        NKEY = NPREV + NKT
        KH, VH, QT, PT, OST = [], [], [], [], []
        off = 0
        for i in range(2):
            bf, off = self.carve(self.HT_t, off, f"KH{i}", [128, 2, NKEY * 128], BF16)
            KH.append(bf)
        assert off <= KD * TB
        off = 0
        VP = NKEY // 4
        for i in range(2):
            parts = []
            for pp in range(VP):
                bf, off = self.carve(self.GT_t, off, f"VH{i}_{pp}", [128, 4, 256], BF16)
                parts.append(bf)
            VH.append(parts)
        for i in range(2):
            bf, off = self.carve(self.GT_t, off, f"QT{i}", [128, 2, TT], BF16)
            QT.append(bf)
        for i in range(6):
            bf, off = self.carve(self.GT_t, off, f"PT{i}", [128, TT], BF16)
            PT.append(bf)
        ACC = []
        for i in range(4):
            bf, off = self.carve(self.GT_t, off, f"ACC{i}", [128, TT], F32)
            ACC.append(bf)
        for i in range(2):
            bf, off = self.carve(self.GT_t, off, f"OST{i}", [128, 2, TT], BF16)
            OST.append(bf)
        RL, off = self.carve(self.GT_t, off, "RL", [128, TT], F32)
        ONJ = []
        for i in range(2):
            bf, off = self.carve(self.GT_t, off, f"ONJ{i}", [128, 2 * TT], F32)
            ONJ.append(bf)
        OD, off = self.carve(self.GT_t, off, "OD", [128, 2 * TT], F32)
        SQO, off = self.carve(self.GT_t, off, "SQO", [128, 2 * TT], F32)
        RS, off = self.carve(self.GT_t, off, "RS", [128, TT], F32)
        assert off <= KF * TB, off
        scratch = KH + [p_ for v_ in VH for p_ in v_] + QT + PT + OST + [RL, OD, SQO, RS] + ONJ + ACC
        old = [self.HT.r, self.GT.r] + [bb.r for bb in self.ov.values()]
        self.fence(old, [x.r for x in scratch])
        for h in range(8):
            i = h % 2
            kh, vh = KH[i], VH[i]
            rk = [self.dres(f"QK{16 + 2 * h + j}_{c0}") for j in range(2) for c0 in range(0, NT, TT)]
            if self.paired:
                r0 = 256 * (h % 2)
                P.op("sp", lambda e, kh=kh, h=h, r0=r0: e.dma_start(
                    out=kh.ap[:, :, 0:NT], in_=self.KOUT[h // 2][r0:r0 + 256, :].rearrange("(j p) n -> p j n", p=128)),
                    reads=[self.dres(f"KOUT{h // 2}")], writes=[kh.r], dma_key=kh.key + "p")
            P.op("sp", lambda e, kh=kh, h=h: e.dma_start(
                out=kh.ap[:, :, NPREV * 128:NPREV * 128 + NT],
                in_=self.KIN[256 * h:256 * h + 256, :].rearrange("(j p) n -> p j n", p=128)),
                reads=rk, writes=[kh.r], dma_key=kh.key)
            for pp in range(VP):
                vp = vh[pp]
                if pp * 4 < NPREV:
                    P.op("sp", lambda e, vp=vp, h=h, pp=pp: e.dma_start(
                        out=vp.ap, in_=self.VOUT[pp][0:512, 256 * h:256 * h + 256].rearrange("(t p) e -> p t e", p=128)),
                        reads=[self.dres(f"VOUT{pp}")], writes=[vp.r], dma_key=vp.key)
                else:
                    po_ = pp - NPREV // 4
                    rv = [self.dres(f"V{tk}_{h // 2}") for tk in range(po_ * 4, po_ * 4 + 4)]
                    P.op("sp", lambda e, vp=vp, h=h, po_=po_: e.dma_start(
                        out=vp.ap, in_=self.VIN[po_ * 512:(po_ + 1) * 512, 256 * h:256 * h + 256].rearrange("(t p) e -> p t e", p=128)),
                        reads=rv, writes=[vp.r], dma_key=vp.key)
            for t in range(NQT):
                qt = QT[self.rot("qt", 2)]
                rq = [self.dres(f"QK{2 * h + j}_{t * TT}") for j in range(2)]
                P.op("sp", lambda e, qt=qt, h=h, t=t: e.dma_start(
                    out=qt.ap, in_=self.QKS[2 * h:2 * h + 2, :, t * TT:(t + 1) * TT].rearrange("j p n -> p j n")),
                    reads=rq, writes=[qt.r], dma_key=qt.key)
                nkt = NPREV + 4 * (t + 1)
                for j in range(2):
                    po = [self.PS[3 + 2 * j], self.PS[4 + 2 * j]]
                    pl = self.PS[7]
                    acc = [ACC[2 * j], ACC[2 * j + 1]]
                    P.op("dve", lambda e, a_=acc[0]: e.memset(a_.ap, 0.0), writes=[acc[0].r])
                    pend = []

                    def emit_pv(kt, c0, pt, po=po, vh=vh, nkt=nkt, acc=acc):
                        vp = vh[kt // 4]

                        def pv(e, kt=kt, c0=c0, pt=pt, po=po, vp=vp, nkt=nkt):
                            ins = None
                            for c in range(2):
                                ins = e.matmul(po[c].ap[:, c0:TT], vp.ap[:, kt % 4, 128 * c:128 * c + 128], pt.ap[:, c0:TT],
                                               start=(kt == 0), stop=(kt == nkt - 1))
                            return ins
                        P.op("pe", pv, reads=[vp.r, pt.r], writes=[po[0].r, po[1].r])
                        a_ = acc[0]
                        eng_ = "dve"
                        P.op(eng_, lambda e, a_=a_, pt=pt, c0=c0: e.tensor_tensor(out=a_.ap[:, c0:TT], in0=a_.ap[:, c0:TT],
                                                                                 in1=pt.ap[:, c0:TT], op=ALU.add),
                             reads=[pt.r], writes=[a_.r])
                    for kt in range(nkt):
                        dpos = kt - (NPREV + 4 * t)
                        c0 = 128 * dpos if dpos > 0 else 0
                        pss = self.PS[self.rot("ps_s", 3)]
                        pt = PT[self.rot("pt", 6)]
                        P.op("pe", lambda e, pss=pss, kt=kt, c0=c0, j=j, kh=kh, qt=qt: e.matmul(
                            pss.ap[:, c0:TT], kh.ap[:, j, 128 * kt:128 * kt + 128], qt.ap[:, j, c0:TT], start=True, stop=True),
                            reads=[kh.r, qt.r], writes=[pss.r])
                        if kt < NPREV:
                            P.op("act", lambda e, pss=pss, pt=pt: e.activation(
                                out=pt.ap, in_=pss.ap, func=AF.Exp, scale=SCALE, bias=NEGB.ap[:, 0:1]),
                                reads=[pss.r, NEGB.r], writes=[pt.r])
                        else:
                            P.op("act", lambda e, pss=pss, pt=pt, c0=c0: e.activation(
                                out=pt.ap[:, c0:TT], in_=pss.ap[:, c0:TT], func=AF.Exp, scale=SCALE),
                                reads=[pss.r], writes=[pt.r])
                        if dpos >= 0:
                            P.op("pool", lambda e, pt=pt, c0=c0: e.memset(pt.ap[64:128, c0:c0 + 64], 0.0), writes=[pt.r])
                        pend.append((kt, c0, pt))
                        if len(pend) > 2:
                            emit_pv(*pend.pop(0))
                    while pend:
                        emit_pv(*pend.pop(0))

                    def lsum(e, pl=pl, acc=acc):
                        return e.matmul(pl.ap, self.ONES.ap, acc[0].ap, start=True, stop=True)
                    P.op("pe", lsum, reads=[acc[0].r, self.ONES.r], writes=[pl.r])
                    P.op("act", lambda e, pl=pl: e.activation(out=RL.ap, in_=pl.ap, func=AF.Ln), reads=[pl.r], writes=[RL.r])
                    P.op("act", lambda e: e.activation(out=RL.ap, in_=RL.ap, func=AF.Exp, scale=-1.0), writes=[RL.r])
                    for c in range(2):
                        P.op("dve", lambda e, c=c, j=j, po=po: e.tensor_tensor(out=ONJ[j].ap[:, c * TT:(c + 1) * TT], in0=po[c].ap,
                                                                              in1=RL.ap, op=ALU.mult),
                             reads=[po[c].r, RL.r], writes=[ONJ[j].r])
                P.op("dve", lambda e: e.scalar_tensor_tensor(out=OD.ap, in0=ONJ[1].ap, scalar=NEGLAM, in1=ONJ[0].ap,
                                                             op0=ALU.mult, op1=ALU.add),
                     reads=[ONJ[0].r, ONJ[1].r, ATC.r], writes=[OD.r])
                P.op("act", lambda e: e.activation(out=SQO.ap, in_=OD.ap, func=AF.Square), reads=[OD.r], writes=[SQO.r])
                pss = self.PS[self.rot("ps_s", 3)]

                def msf(e, pss=pss):
                    e.matmul(pss.ap, self.ONES.ap, SQO.ap[:, 0:TT], start=True, stop=False)
                    return e.matmul(pss.ap, self.ONES.ap, SQO.ap[:, TT:2 * TT], start=False, stop=True)
                P.op("pe", msf, reads=[SQO.r, self.ONES.r], writes=[pss.r])
                P.op("act", lambda e, pss=pss: e.activation(out=RS.ap, in_=pss.ap, func=AF.Ln, scale=1.0 / 256.0, bias=self.RMSE.ap[:, 0:1]),
                     reads=[pss.r, self.RMSE.r], writes=[RS.r])
                P.op("act", lambda e: e.activation(out=RS.ap, in_=RS.ap, func=AF.Exp, scale=-0.5), writes=[RS.r])
                ost = OST[self.rot("ost", 2)]
                for c in range(2):
                    P.op("dve", lambda e, c=c: e.tensor_tensor(out=OD.ap[:, c * TT:(c + 1) * TT], in0=OD.ap[:, c * TT:(c + 1) * TT],
                                                              in1=RS.ap, op=ALU.mult), reads=[RS.r], writes=[OD.r])
                    P.op("act", lambda e, c=c, ost=ost: e.activation(out=ost.ap[:, c], in_=OD.ap[:, c * TT:(c + 1) * TT],
                                                                    func=AF.Identity, scale=SUBG.ap[:, c:c + 1], bias=0.0),
                         reads=[OD.r, SUBG.r], writes=[ost.r])
                P.op("sp", lambda e, ost=ost, h=h, t=t: e.dma_start(
                    out=self.ONS[2 * h:2 * h + 2, :, t * TT:(t + 1) * TT].rearrange("c p n -> p c n"), in_=ost.ap),
                    reads=[ost.r], writes=[self.dres(f"ON{h}_{t}")], dma_key=ost.key)
        P.ss = True
        self.fence([x.r for x in scratch], [self.HT.r, self.GT.r] + [bb.r for bb in self.ov.values()])
        for b in range(NB):
            rl = [self.dres(f"ON{h}_{b * (TB // TT) + t}") for h in range(8) for t in range(TB // TT)]
            P.op("sp", lambda e, b=b: e.dma_start(out=self.GT.ap[:, 0:KD, :],
                                                 in_=self.ONS[:, :, b * TB:(b + 1) * TB].rearrange("k p n -> p k n")),
                 reads=rl, writes=[self.GT.r], dma_key="GT")
            self.out_proj(b, wo, KD, layer, sub, x_src, x_res_fn, nxt, last)
            self.prefetch_ht()

    def stage_lru(self, layer, sub, x_src, x_res_fn, nxt, last):
        P, NT, NB = self.P, self.NT, self.NB
        P.ss = True
        w_in = self.lru_w_in.rearrange("(kc p) n -> p kc n", p=128)
        w_out = self.lru_w_out.rearrange("(kc p) n -> p kc n", p=128)
        gav = self.lru_ga.rearrange("n c d -> (n c) d").rearrange("(q p) d -> p q d", p=128)
        gxv = self.lru_gx.rearrange("n c d -> (n c) d").rearrange("(q p) d -> p q d", p=128)
        CW = Buf(self.sb("CW", [128, KR, 4], F32)[:], "CW")
        LV = Buf(self.sb("LV", [128, 4, KR], F32)[:], "LV")
        LC = Buf(self.sb("LC", [128, 4, KR], F32)[:], "LC")
        HALO = Buf(self.sb("HALO", [128, KR, 4], F32)[:], "HALO")
        STATE = Buf(self.sb("STATE", [128, KR], F32)[:], "STATE")
        P.op("sp", lambda e: e.dma_start(out=CW.ap, in_=self.cwT), writes=[CW.r], dma_key="CW")
        P.op("sp", lambda e: e.dma_start(out=LV.ap, in_=self.lruv), writes=[LV.r], dma_key="LV")
        P.op("dve", lambda e: e.memset(HALO.ap, 0.0), writes=[HALO.r])
        P.op("dve", lambda e: e.memset(STATE.ap, 0.0), writes=[STATE.r])
        lam = LV.ap[:, 3]
        P.op("act", lambda e: e.activation(out=LC.ap[:, 0], in_=lam, func=AF.Abs), reads=[LV.r], writes=[LC.r])
        P.op("act", lambda e: e.activation(out=LC.ap[:, 0], in_=LC.ap[:, 0], func=AF.Exp, scale=-1.0), writes=[LC.r])
        P.op("act", lambda e: e.activation(out=LC.ap[:, 0], in_=LC.ap[:, 0], func=AF.Ln, bias=1.0, scale=1.0), writes=[LC.r])
        P.op("dve", lambda e: e.tensor_scalar(out=LC.ap[:, 1], in0=lam, scalar1=-1.0, scalar2=0.0, op0=ALU.mult, op1=ALU.max),
             reads=[LV.r], writes=[LC.r])
        P.op("dve", lambda e: e.tensor_tensor(out=LC.ap[:, 1], in0=LC.ap[:, 1], in1=LC.ap[:, 0], op=ALU.add), writes=[LC.r])
        P.op("dve", lambda e: e.tensor_scalar(out=LC.ap[:, 2], in0=LC.ap[:, 1], scalar1=-LRU_C, scalar2=None, op0=ALU.mult), writes=[LC.r])
        P.op("dve", lambda e: e.tensor_scalar(out=LC.ap[:, 3], in0=LC.ap[:, 1], scalar1=-2.0 * LRU_C, scalar2=None, op0=ALU.mult), writes=[LC.r])
        off = KR * TB
        XB, XC, XCB, GG, RT, IT, GAW, GXW, AA, HSOs = [], [], [], [], [], [], [], [], [], []
        for i in range(2):
            xb_, xc_, xcb_, gg_ = [], [], [], []
            for sb_ in range(2):
                bf, off = self.carve(self.GT_t, off, f"XB{i}{sb_}", [128, TT + 4], F32); xb_.append(bf)
                bf, off = self.carve(self.GT_t, off, f"XC{i}{sb_}", [128, TT], F32); xc_.append(bf)
                bf, off = self.carve(self.GT_t, off, f"XCB{i}{sb_}", [128, TT], BF16); xcb_.append(bf)
                bf, off = self.carve(self.GT_t, off, f"GG{i}{sb_}", [128, TT], BF16); gg_.append(bf)
            XB.append(xb_); XC.append(xc_); XCB.append(xcb_); GG.append(gg_)
            bf, off = self.carve(self.GT_t, off, f"RT{i}", [128, TT], F32); RT.append(bf)
            bf, off = self.carve(self.GT_t, off, f"IT{i}", [128, TT], F32); IT.append(bf)
            bf, off = self.carve(self.GT_t, off, f"GAW{i}", [128, 2, 256], BF16); GAW.append(bf)
            bf, off = self.carve(self.GT_t, off, f"GXW{i}", [128, 2, 256], BF16); GXW.append(bf)
            bf, off = self.carve(self.GT_t, off, f"AA{i}", [128, TT], F32); AA.append(bf)
            bf, off = self.carve(self.GT_t, off, f"HSO{i}", [128, TT], F32); HSOs.append(bf)
        assert off <= KF * TB, off
        scratch = [x for l_ in (XB + XC + XCB + GG) for x in l_] + RT + IT + GAW + GXW + AA + HSOs
        self.fence([self.GT.r], [x.r for x in scratch] + [self.GT.r])
        GELU_K = 2.0 * math.sqrt(2.0 / math.pi)

        def lru_block(b, state_only):
            self.load_ht(b)
            nst = {}
            units = [(n, t) for n in range(10) for t in range(TB // TT)]

            def stage1(i):
                n, t = units[i]
                u = i % 2
                if t == 0:
                    s = self.rot("ws", 2)
                    ws = self.WS[s]
                    wsr = [ws.r, self.WSB[s]]
                    wsa = ws.ap[:, 0:KD * 256].rearrange("p (k f) -> p k f", k=KD)
                    wsu = ws.ap[:, KD * 256:KD * 512].rearrange("p (k f) -> p k f", k=KD)
                    if not state_only:
                        P.op("pool", lambda e, wsa=wsa, n=n: e.dma_start(out=wsa, in_=w_in[:, :, 256 * n:256 * n + 256]),
                             writes=[ws.r], dma_key=ws.key)
                    P.op("pool", lambda e, wsu=wsu, n=n: e.dma_start(out=wsu, in_=w_in[:, :, DRNN + 256 * n:DRNN + 256 * n + 256]),
                         writes=[self.WSB[s]], dma_key=ws.key + "b")
                    gi = self.rot("gw", 2)
                    gaw, gxw = GAW[gi], GXW[gi]
                    P.op("pool", lambda e, gaw=gaw, n=n: e.dma_start(out=gaw.ap, in_=gav[:, 2 * n:2 * n + 2, :]), writes=[gaw.r], dma_key=gaw.key)
                    P.op("pool", lambda e, gxw=gxw, n=n: e.dma_start(out=gxw.ap, in_=gxv[:, 2 * n:2 * n + 2, :]), writes=[gxw.r], dma_key=gxw.key)
                    nst[n] = (wsr, wsa, wsu, gaw, gxw)
                wsr, wsa, wsu, gaw, gxw = nst[n]
                xbs, xcs, xcbs, ggs = XB[u], XC[u], XCB[u], GG[u]
                tsl = slice(t * TT, (t + 1) * TT)
                for sb_ in range(2):
                    c = 2 * n + sb_
                    xb, xc = xbs[sb_], xcs[sb_]
                    P.op("dve", lambda e, xb=xb, c=c: e.tensor_copy(out=xb.ap[:, 0:3], in_=HALO.ap[:, c, 0:3]),
                         reads=[HALO.r], writes=[xb.r])
                    pp = self.rot("ps_f", 2)
                    pg, px = self.PS[4 + 2 * pp], self.PS[5 + 2 * pp]

                    def mmf(e, wsa=wsa, wsu=wsu, sb_=sb_, tsl=tsl, pg=pg, px=px, state_only=state_only):
                        ins = None
                        for k in range(KD if not state_only else 0):
                            ins = e.matmul(pg.ap, wsa[:, k, 128 * sb_:128 * sb_ + 128], self.HT.ap[:, k, tsl],
                                           start=(k == 0), stop=(k == KD - 1))
                        for k in range(KD):
                            ins = e.matmul(px.ap, wsu[:, k, 128 * sb_:128 * sb_ + 128], self.HT.ap[:, k, tsl],
                                           start=(k == 0), stop=(k == KD - 1))
                        return ins
                    P.op("pe", mmf, reads=wsr + [self.HT.r], writes=[pg.r, px.r])
                    P.op("act", lambda e, px=px, xb=xb: e.activation(out=xb.ap[:, 3:3 + TT], in_=px.ap, func=AF.Copy),
                         reads=[px.r], writes=[xb.r])
                    if not state_only:
                        sa = self.SA[self.rot("sa", 2)]
                        P.op("act", lambda e, sa=sa, pg=pg: e.activation(out=sa.ap, in_=pg.ap, func=AF.Square), reads=[pg.r], writes=[sa.r])
                        P.op("dve", lambda e, sa=sa: e.tensor_scalar(out=sa.ap, in0=sa.ap, scalar1=0.044715, scalar2=1.0,
                                                                    op0=ALU.mult, op1=ALU.add), writes=[sa.r])
                        P.op("dve", lambda e, sa=sa, pg=pg: e.tensor_tensor(out=sa.ap, in0=sa.ap, in1=pg.ap, op=ALU.mult),
                             reads=[pg.r], writes=[sa.r])
                        P.op("act", lambda e, sa=sa: e.activation(out=sa.ap, in_=sa.ap, func=AF.Sigmoid, scale=GELU_K), writes=[sa.r])
                        P.op("dve", lambda e, sa=sa, pg=pg, gg=ggs[sb_]: e.tensor_tensor(out=gg.ap, in0=sa.ap, in1=pg.ap, op=ALU.mult),
                             reads=[sa.r, pg.r], writes=[ggs[sb_].r])
                    P.op("dve", lambda e, xb=xb, xc=xc, c=c: e.tensor_scalar(out=xc.ap, in0=xb.ap[:, 3:3 + TT], scalar1=CW.ap[:, c, 3:4],
                                                                             scalar2=LV.ap[:, 0, c:c + 1], op0=ALU.mult, op1=ALU.add),
                         reads=[xb.r, CW.r, LV.r], writes=[xc.r])
                    for kk in (2, 1, 0):
                        P.op("dve", lambda e, xb=xb, xc=xc, c=c, kk=kk: e.scalar_tensor_tensor(
                            out=xc.ap, in0=xb.ap[:, kk:kk + TT], scalar=CW.ap[:, c, kk:kk + 1], in1=xc.ap, op0=ALU.mult, op1=ALU.add),
                            reads=[xb.r, CW.r], writes=[xc.r])
                    P.op("dve", lambda e, xb=xb, c=c: e.tensor_copy(out=HALO.ap[:, c, 0:3], in_=xb.ap[:, TT:TT + 3]),
                         reads=[xb.r], writes=[HALO.r])
                    P.op("act", lambda e, xc=xc, xcb=xcbs[sb_]: e.activation(out=xcb.ap, in_=xc.ap, func=AF.Copy),
                         reads=[xc.r], writes=[xcbs[sb_].r])

            def stage2(i):
                n, t = units[i]
                u = i % 2
                wsr, wsa, wsu, gaw, gxw = nst[n]
                xcs, xcbs, ggs = XC[u], XCB[u], GG[u]
                tsl = slice(t * TT, (t + 1) * TT)
                for ds_ in range(2):
                    d = 2 * n + ds_
                    gp = self.rot("ps_g", 2)
                    pr, pi = self.PS[2 * gp], self.PS[2 * gp + 1]

                    def gmm(e, ds_=ds_, gaw=gaw, gxw=gxw, pr=pr, pi=pi, x0=xcbs[0], x1=xcbs[1]):
                        e.matmul(pr.ap, gaw.ap[:, 0, 128 * ds_:128 * ds_ + 128], x0.ap, start=True, stop=False)
                        e.matmul(pr.ap, gaw.ap[:, 1, 128 * ds_:128 * ds_ + 128], x1.ap, start=False, stop=True)
                        e.matmul(pi.ap, gxw.ap[:, 0, 128 * ds_:128 * ds_ + 128], x0.ap, start=True, stop=False)
                        return e.matmul(pi.ap, gxw.ap[:, 1, 128 * ds_:128 * ds_ + 128], x1.ap, start=False, stop=True)
                    P.op("pe", gmm, reads=[gaw.r, gxw.r, xcbs[0].r, xcbs[1].r], writes=[pr.r, pi.r])
                    ri = self.rot("rt", 2)
                    rt, it, aa, hso = RT[ri], IT[ri], AA[ri], HSOs[ri]
                    xc = xcs[ds_]
                    P.op("act", lambda e, rt=rt, d=d, pr=pr: e.activation(out=rt.ap, in_=pr.ap, func=AF.Sigmoid, bias=LV.ap[:, 1, d:d + 1], scale=1.0),
                         reads=[pr.r, LV.r], writes=[rt.r])
                    P.op("act", lambda e, it=it, d=d, pi=pi: e.activation(out=it.ap, in_=pi.ap, func=AF.Sigmoid, bias=LV.ap[:, 2, d:d + 1], scale=1.0),
                         reads=[pi.r, LV.r], writes=[it.r])
                    P.op("act", lambda e, rt=rt, d=d, aa=aa: e.activation(out=aa.ap, in_=rt.ap, func=AF.Exp, scale=LC.ap[:, 2, d:d + 1]),
                         reads=[rt.r, LC.r], writes=[aa.r])
                    P.op("act", lambda e, rt=rt, d=d: e.activation(out=rt.ap, in_=rt.ap, func=AF.Exp, scale=LC.ap[:, 3, d:d + 1]),
                         reads=[LC.r], writes=[rt.r])
                    P.op("act", lambda e, rt=rt: e.activation(out=rt.ap, in_=rt.ap, func=AF.Sqrt, scale=-1.0, bias=1.0), writes=[rt.r])
                    P.op("dve", lambda e, it=it, xc=xc: e.tensor_tensor(out=xc.ap, in0=xc.ap, in1=it.ap, op=ALU.mult),
                         reads=[it.r], writes=[xc.r])
                    P.op("dve", lambda e, rt=rt, xc=xc: e.tensor_tensor(out=xc.ap, in0=xc.ap, in1=rt.ap, op=ALU.mult),
                         reads=[rt.r], writes=[xc.r])
                    P.op("dve", lambda e, d=d, aa=aa, xc=xc, hso=hso: e.tensor_tensor_scan(
                        out=hso.ap, data0=aa.ap, data1=xc.ap, initial=STATE.ap[:, d:d + 1], op0=ALU.mult, op1=ALU.add),
                        reads=[aa.r, xc.r, STATE.r], writes=[hso.r])
                    P.op("dve", lambda e, d=d, hso=hso: e.tensor_copy(out=STATE.ap[:, d:d + 1], in_=hso.ap[:, TT - 1:TT]),
                         reads=[hso.r], writes=[STATE.r])
                    if not state_only:
                        P.op("dve", lambda e, d=d, hso=hso, gg=ggs[ds_], tsl=tsl: e.tensor_tensor(
                            out=self.GT.ap[:, d, tsl], in0=hso.ap, in1=gg.ap, op=ALU.mult),
                            reads=[hso.r, ggs[ds_].r], writes=[self.GT.r])

            stage1(0)
            for i in range(len(units)):
                if i + 1 < len(units):
                    stage1(i + 1)
                stage2(i)

        if self.paired:
            for b in range(NB):
                lru_block(b, True)
                self.prefetch_ht()
            P.ss = True
            LX = Buf(self.sb("LX", [128, 128], F32)[:], "LX")
            PM2 = Buf(self.sb("PM2", [128, 1], F32)[:], "PM2")
            halo_flat = HALO.ap.rearrange("p a b -> p (a b)")
            P.op("sp", lambda e: e.dma_start(out=PM2.ap, in_=self.pmask), writes=[PM2.r], dma_key="PM2")
            P.op("dve", lambda e: e.memset(LX.ap, 0.0), writes=[LX.r])
            P.op("dve", lambda e: e.tensor_copy(out=LX.ap[:, 0:KR], in_=STATE.ap), reads=[STATE.r], writes=[LX.r])
            P.op("dve", lambda e: e.tensor_copy(out=LX.ap[:, KR:KR + 4 * KR], in_=halo_flat), reads=[HALO.r], writes=[LX.r])
            P.op("sp", lambda e: e.dma_start(out=self.LIN, in_=LX.ap), reads=[LX.r], writes=[self.dres("LIN")], dma_key="LX")
            P.op("pool", lambda e: e.collective_compute("AllGather", ALU.bypass, replica_groups=[[0, 1], [2, 3], [4, 5], [6, 7]],
                                                        ins=[self.LIN], outs=[self.LOUT]),
                 reads=[self.dres("LIN")], writes=[self.dres("LOUT")], dma_key="ccL", cc=True)
            P.op("sp", lambda e: e.dma_start(out=LX.ap, in_=self.LOUT[0:128, :]), reads=[self.dres("LOUT")], writes=[LX.r], dma_key="LX")
            P.op("dve", lambda e: e.tensor_scalar(out=STATE.ap, in0=LX.ap[:, 0:KR], scalar1=PM2.ap[:, 0:1], scalar2=None, op0=ALU.mult),
                 reads=[LX.r, PM2.r], writes=[STATE.r])
            P.op("dve", lambda e: e.tensor_scalar(out=halo_flat, in0=LX.ap[:, KR:KR + 4 * KR], scalar1=PM2.ap[:, 0:1], scalar2=None, op0=ALU.mult),
                 reads=[LX.r, PM2.r], writes=[HALO.r])
        for b in range(NB):
            lru_block(b, False)
            self.prefetch_ht()
            self.out_proj(b, w_out, KR, layer, sub, x_src, x_res_fn, nxt, last)
        self.fence([x.r for x in scratch] + [self.GT.r], [self.GT.r])

    def build(self):
        self.alloc()
        self.stage_consts()
        layers = sorted(set(l for l, s in self.sub_list))
        for l in layers:
            self.stage_ada(l)
        l0, s0 = self.sub_list[0]
        self.stage_prologue(l0, s0)
        self.ht_plan = []
        for idx_, (l, s_) in enumerate(self.sub_list):
            rep = 2 if (s_ == 1 and l % 2 == 1 and self.paired) else 1
            self.ht_plan += [(idx_, b_) for b_ in range(self.NB)] * rep
        self.hs_ver = {b_: -1 for b_ in range(self.NB)}
        self.cur_idx = 0
        self.ht_pos = 0
        self.ht_loaded = 0
        for idx, (l, s) in enumerate(self.sub_list):
            self.cur_idx = idx
            first = idx == 0
            last = idx == len(self.sub_list) - 1
            nxt = None if last else self.sub_list[idx + 1]
            x_src = self.xT if first else self.XS
            x_res_fn = (lambda m, c0: self.dres("XIN")) if first else (lambda m, c0: self.dres(f"XS{m}_{c0}"))
            if s in (0, 2):
                self.stage_ffn(l, s, x_src, x_res_fn, nxt, last)
            elif l % 2 == 0:
                self.stage_attn(l, s, x_src, x_res_fn, nxt, last)
            else:
                self.stage_lru(l, s, x_src, x_res_fn, nxt, last)
        self.P.finish([r for n, r in self._dres.items() if n.startswith("OUT")])
        self.st.close()
        return self.nc


def prep_inputs(inputs, b, t0, NT, half=0, paired=False):
    f = np.float32
    x = np.asarray(inputs["x"], f)[b, t0:t0 + NT]
    m = {}
    m["xT"] = np.ascontiguousarray(x.T.reshape(KD, 128, NT))
    m["pmask"] = np.full((128, 1), float(half), f)
    m["cT"] = np.ascontiguousarray(np.asarray(inputs["c"], f)[b].reshape(KD, 128).T)
    adb = np.asarray(inputs["ada_b"], f).reshape(DEPTH, 144, 128).transpose(2, 0, 1)
    if paired:
        for l in range(DEPTH):
            m[f"ada_w_{l}"] = np.ascontiguousarray(np.asarray(inputs["ada_w"], f)[l][:, half * 9216:(half + 1) * 9216])
        m["ada_bT"] = np.ascontiguousarray(adb[:, :, half * 72:(half + 1) * 72])
    else:
        for l in range(DEPTH):
            m[f"ada_w_{l}"] = np.asarray(inputs["ada_w"], f)[l]
        m["ada_bT"] = np.ascontiguousarray(adb)
    ln = np.stack([np.asarray(inputs["ln_g"], f), np.asarray(inputs["ln_b"], f)], axis=2)
    m["lnT"] = np.ascontiguousarray(ln.reshape(DEPTH, 3, 2, KD, 128).transpose(4, 0, 1, 2, 3))
    for l in range(DEPTH):
        for wi in range(2):
            m[f"ffn_w_in_{l}_{wi}"] = np.asarray(inputs["ffn_w_in"], f)[l, wi]
            m[f"ffn_w_out_{l}_{wi}"] = np.asarray(inputs["ffn_w_out"], f)[l, wi]
    m["attn_w_qkv"] = np.asarray(inputs["attn_w_qkv"], f)[0]
    m["attn_w_o"] = np.asarray(inputs["attn_w_o"], f)[0]
    lam = np.stack([np.asarray(inputs[k], f)[0] for k in
                    ("attn_lambda_q1", "attn_lambda_k1", "attn_lambda_q2", "attn_lambda_k2")])
    m["lamv"] = np.ascontiguousarray(np.broadcast_to(lam[None], (128, 4, 128)))
    m["sublnT"] = np.ascontiguousarray(np.asarray(inputs["attn_subln_g"], f)[0].reshape(2, 128).T)
    m["lru_w_in"] = np.asarray(inputs["lru_w_in"], f)[0]
    m["lru_w_out"] = np.asarray(inputs["lru_w_out"], f)[0]
    m["lru_ga"] = np.asarray(inputs["lru_gate_a_w"], f)[0]
    m["lru_gx"] = np.asarray(inputs["lru_gate_x_w"], f)[0]
    m["cwT"] = np.ascontiguousarray(np.asarray(inputs["lru_conv_w"], f)[0].reshape(4, KR, 128).transpose(2, 1, 0))
    lv = np.stack([np.asarray(inputs[k], f)[0] for k in
                   ("lru_conv_b", "lru_gate_a_b", "lru_gate_x_b", "lru_lambda")])
    m["lruv"] = np.ascontiguousarray(lv.reshape(4, KR, 128).transpose(2, 0, 1))
    return m


FULL_SUBS = [(0, 0), (0, 1), (0, 2), (1, 0), (1, 1), (1, 2)]


def run(inputs, sub_list=FULL_SUBS, NT=2048, n_cores=8):
    mk = MK(NT, sub_list, paired=(NT == 2048))
    nc = mk.build()
    B, S = 4, 4096
    per_seq = S // NT
    in_maps = []
    for c in range(n_cores):
        b, h = c // per_seq, c % per_seq
        full = prep_inputs(inputs, b, h * NT, NT, half=h, paired=mk.paired)
        in_maps.append({k: full[k] for k in mk.in_names})
    res = run_bass_kernel_spmd(nc, in_maps, core_ids=list(range(n_cores)))
    out = np.empty((B, S, D), np.float32)
    for c in range(n_cores):
        b, h = c // per_seq, c % per_seq
        o = res.results[c]["outT"].reshape(D, NT)
        out[b, h * NT:(h + 1) * NT] = o.T
    return out


def kernel(**inputs):
    return run(inputs)
```

```python
import math
from contextlib import ExitStack

import numpy as np
import concourse.bass as bass
import concourse.mybir as mybir
from concourse.bass_utils import run_bass_kernel_spmd

F32 = mybir.dt.float32
BF16 = mybir.dt.bfloat16
AF = mybir.ActivationFunctionType
ALU = mybir.AluOpType

D = 2048
KD = 16
DFF = 5632
KF = 44
DRNN = 2560
KR = 20
DEPTH = 2
ALPHA = (2 * DEPTH) ** 0.25
LN_EPS = 1e-5
EPS_P = LN_EPS / (ALPHA * ALPHA)
RMS_EPS = 1e-5
TB = 1024
TT = 512
LRU_C = 8.0


class Res:
    __slots__ = ("name", "last_w", "readers")

    def __init__(self, name):
        self.name = name
        self.last_w = None
        self.readers = []


class Prog:
    ENGS = ("pe", "act", "dve", "pool", "sp")

    def __init__(self, nc, stack):
        self.nc = nc
        self.stack = stack
        self.e = dict(pe=nc.tensor, act=nc.scalar, dve=nc.vector, pool=nc.gpsimd, sp=nc.sync)
        self.q = {k: [] for k in self.ENGS}
        self.cnt = {k: 0 for k in self.ENGS}
        self.csem = {k: stack.enter_context(nc.semaphore("c_" + k)) for k in self.ENGS}
        self.seen = {k: {} for k in self.ENGS}
        self.dsem = {}
        self.dinc = {}
        self.ss = True

    def _dma_sem(self, key):
        if key not in self.dsem:
            self.dsem[key] = [self.stack.enter_context(self.nc.semaphore("d_" + key)), 0]
        return self.dsem[key]

    def op(self, eng, fn, reads=(), writes=(), dma_key=None, cc=False, ss=None):
        if ss is None:
            ss = self.ss
        deps = []
        for r in reads:
            if r.last_w is not None:
                deps.append(r.last_w)
        for w in writes:
            if w.last_w is not None:
                deps.append(w.last_w)
            deps.extend(w.readers)
        waits = {}
        for d in deps:
            if d[0] == "c":
                _, deng, idx = d
                if deng == eng and dma_key is None and (eng in ("pe", "sp") or not ss):
                    continue
                sem, val, skey = self.csem[deng], idx, "c_" + deng
            else:
                _, key, c = d
                sem, val, skey = self.dsem[key][0], self.dinc.get(key, 16) * c, "d_" + key
            if self.seen[eng].get(skey, 0) >= val:
                continue
            if skey not in waits or waits[skey][1] < val:
                waits[skey] = (sem, val)
        for skey, (sem, val) in waits.items():
            self.seen[eng][skey] = val
        wl = list(waits.values())
        if dma_key is None:
            self.cnt[eng] += 1
            tok = ("c", eng, self.cnt[eng])
            mysem, inc = self.csem[eng], 1
        else:
            ds = self._dma_sem(dma_key)
            ds[1] += 1
            tok = ("d", dma_key, ds[1])
            mysem, inc = ds[0], 16
            if cc:
                self.dinc[dma_key] = 1
                inc = None
        engobj = self.e[eng]

        def emit():
            for sem, val in wl:
                engobj.wait_ge(sem, val)
            ins = fn(engobj)
            if inc is None:
                ins.then_inc(mysem)
            else:
                ins.then_inc(mysem, inc)

        self.q[eng].append(emit)
        for w in writes:
            w.last_w = tok
            w.readers = []
        for r in reads:
            if r not in writes:
                r.readers.append(tok)
        return tok

    def finish(self, final_res):
        wl = []
        for r in final_res:
            for d in ([r.last_w] if r.last_w else []) + list(r.readers):
                if d[0] == "c":
                    wl.append((self.csem[d[1]], d[2]))
                else:
                    wl.append((self.dsem[d[1]][0], self.dinc.get(d[1], 16) * d[2]))
        mx = {}
        for sem, val in wl:
            k = id(sem)
            if k not in mx or mx[k][1] < val:
                mx[k] = (sem, val)
        wl = list(mx.values())
        nc, q = self.nc, self.q
        with nc.Block() as block:
            @block.tensor
            def _(eng):
                for f in q["pe"]:
                    f()

            @block.scalar
            def _(eng):
                for f in q["act"]:
                    f()

            @block.vector
            def _(eng):
                for f in q["dve"]:
                    f()

            @block.gpsimd
            def _(eng):
                for f in q["pool"]:
                    f()

            @block.sync
            def _(eng):
                for f in q["sp"]:
                    f()
                for sem, val in wl:
                    eng.wait_ge(sem, val)


class Buf:
    __slots__ = ("ap", "r", "key")

    def __init__(self, ap, name):
        self.ap = ap
        self.r = Res(name)
        self.key = name


class MK:
    def __init__(self, NT, sub_list, paired=False):
        self.paired = paired
        self.NT = NT
        self.NB = NT // TB
        self.sub_list = sub_list
        self.nc = bass.Bass("TRN2", target_bir_lowering=False)
        self.st = ExitStack()
        self.P = Prog(self.nc, self.st)
        self._rr = {}
        self.in_names = []
        self.deferred = []

    def din(self, name, shape, dt=F32):
        self.in_names.append(name)
        return self.nc.dram_tensor(name, list(shape), dt, kind="ExternalInput").ap()

    def dscr(self, name, shape, dt):
        return self.nc.dram_tensor(name, list(shape), dt, kind="Internal").ap()

    def sb(self, name, shape, dt):
        t = self.st.enter_context(self.nc.sbuf_tensor(name, list(shape), dt))
        return t

    def rot(self, name, n):
        i = self._rr.get(name, 0)
        self._rr[name] = i + 1
        return i % n

    def dres(self, name):
        if not hasattr(self, "_dres"):
            self._dres = {}
        if name not in self._dres:
            self._dres[name] = Res(name)
        return self._dres[name]

    def alloc(self):
        nc, st = self.nc, self.st
        NT = self.NT
        subs = set(self.sub_list)
        layers = sorted(set(l for l, _ in subs))
        self.xT = self.din("xT", [KD, 128, NT])
        self.cT = self.din("cT", [128, KD])
        self.NADA = 72 if self.paired else 144
        self.ada_w = {l: self.din(f"ada_w_{l}", [D, self.NADA * 128]) for l in layers}
        self.ada_bT = self.din("ada_bT", [128, DEPTH, self.NADA])
        if self.paired:
            self.MIN = {l: nc.dram_tensor(f"MIN{l}", [128, 72], F32).ap() for l in layers}
            self.MOUT = {l: nc.dram_tensor(f"MOUT{l}", [256, 72], F32).ap() for l in layers}
        self.lnT = self.din("lnT", [128, DEPTH, 3, 2, KD])
        self.ffn_w_in, self.ffn_w_out = {}, {}
        for (l, sb_) in sorted(subs):
            if sb_ in (0, 2):
                wi = 0 if sb_ == 0 else 1
                self.ffn_w_in[(l, wi)] = self.din(f"ffn_w_in_{l}_{wi}", [D, 2 * DFF])
                self.ffn_w_out[(l, wi)] = self.din(f"ffn_w_out_{l}_{wi}", [DFF, D])
        if (0, 1) in subs:
            self.attn_w_qkv = self.din("attn_w_qkv", [D, 3 * D])
            self.attn_w_o = self.din("attn_w_o", [D, D])
            self.lamv = self.din("lamv", [128, 4, 128])
            self.sublnT = self.din("sublnT", [128, 2])
        if (1, 1) in subs:
            self.lru_w_in = self.din("lru_w_in", [D, 2 * DRNN])
            self.lru_w_out = self.din("lru_w_out", [DRNN, D])
            self.lru_ga = self.din("lru_ga", [10, 256, 256])
            self.lru_gx = self.din("lru_gx", [10, 256, 256])
            self.cwT = self.din("cwT", [128, KR, 4])
            self.lruv = self.din("lruv", [128, 4, KR])
        self.outT = nc.dram_tensor("outT", [KD, 128, NT], F32, kind="ExternalOutput").ap()
        self.XS = self.dscr("XS", [KD, 128, NT], F32)
        self.HS = self.dscr("HS", [KD, 128, NT], BF16)
        self.ZS = self.dscr("ZS", [KD, 128, TB], F32)
        self.QKS = self.dscr("QKS", [16, 128, NT], BF16)
        self.KIN = nc.dram_tensor("KIN", [D, NT], BF16).ap()
        self.VIN = nc.dram_tensor("VIN", [NT, D], BF16).ap()
        if self.paired:
            self.pmask = self.din("pmask", [128, 1])
            self.KOUT = [nc.dram_tensor(f"KOUT{i}", [1024, NT], BF16).ap() for i in range(D // 512)]
            self.VOUT = [nc.dram_tensor(f"VOUT{i}", [1024, D], BF16).ap() for i in range(NT // 512)]
            self.LIN = nc.dram_tensor("LIN", [128, 128], F32).ap()
            self.LOUT = nc.dram_tensor("LOUT", [256, 128], F32).ap()
        self.ONS = self.dscr("ONS", [KD, 128, NT], BF16)
        self.HT_t = self.sb("HT", [128, KD * TB], BF16)
        self.GT_t = self.sb("GT", [128, KF * TB], BF16)
        self.WS_t = [self.sb(f"WS{i}", [128, KF * 256], BF16) for i in range(2)]
        self.HT = Buf(self.HT_t[:].rearrange("p (k n) -> p k n", k=KD), "HT")
        self.GT = Buf(self.GT_t[:].rearrange("p (k n) -> p k n", k=KF), "GT")
        self.WS = [Buf(self.WS_t[i][:], f"WS{i}") for i in range(2)]
        self.WSB = [Res(f"WS{i}b") for i in range(2)]
        self.SA = [Buf(self.sb(f"SA{i}", [128, TT], F32)[:], f"SA{i}") for i in range(2)]
        self.MOD = Buf(self.sb("MOD", [128, DEPTH, 9, KD], F32)[:], "MOD")
        self.SC1P = Buf(self.sb("SC1P", [128, DEPTH, 3, KD], F32)[:], "SC1P")
        self.GATE = Buf(self.sb("GATE", [128, DEPTH, 3, KD], F32)[:], "GATE")
        self.LN = Buf(self.sb("LN", [128, DEPTH, 3, 2, KD], F32)[:], "LN")
        self.ADB = Buf(self.sb("ADB", [128, DEPTH, self.NADA], F32)[:], "ADB")
        self.CT = Buf(self.sb("CTs", [128, KD], F32)[:], "CT")
        self.CA = Buf(self.sb("CA", [128, KD], BF16)[:], "CA")
        self.ONES = Buf(self.sb("ONES", [128, 128], F32)[:], "ONES")
        self.ONESB = Buf(self.sb("ONESB", [128, 128], BF16)[:], "ONESB")
        self.EPSP = Buf(self.sb("EPSP", [128, 1], F32)[:], "EPSP")
        self.RMSE = Buf(self.sb("RMSE", [128, 1], F32)[:], "RMSE")
        self.PS = []
        for i in range(8):
            t = st.enter_context(nc.psum_tensor(f"PS{i}", [128, TT], F32))
            self.PS.append(Buf(t[:], f"PS{i}"))
        self.ov = {}
        for i in range(3):
            self.ov[f"XI{i}"] = Buf(self.sb(f"XI{i}", [128, TT], F32)[:], f"XI{i}")
        for i in range(2):
            self.ov[f"HO{i}"] = Buf(self.sb(f"HO{i}", [128, TT], BF16)[:], f"HO{i}")
        for nm in ("S", "Q", "RSTD", "NMR"):
            self.ov[nm] = Buf(self.sb(nm, [128, TB], F32)[:], nm)
        self.ov_fresh = set()

    def drain(self, k=None):
        n = len(self.deferred) if k is None else min(k, len(self.deferred))
        for _ in range(n):
            self.deferred.pop(0)()

    def ovw(self, b):
        return [b.r]

    def ht_load_writes(self):
        return [self.HT.r]

    def stage_consts(self):
        P = self.P
        P.op("dve", lambda e: e.memset(self.ONES.ap, 1.0), writes=[self.ONES.r])
        P.op("dve", lambda e: e.memset(self.ONESB.ap, 1.0), writes=[self.ONESB.r])
        P.op("dve", lambda e: e.memset(self.EPSP.ap, EPS_P), writes=[self.EPSP.r])
        P.op("dve", lambda e: e.memset(self.RMSE.ap, RMS_EPS), writes=[self.RMSE.r])
        P.op("sp", lambda e: e.dma_start(out=self.CT.ap, in_=self.cT), writes=[self.CT.r], dma_key="CT")
        P.op("sp", lambda e: e.dma_start(out=self.ADB.ap, in_=self.ada_bT), writes=[self.ADB.r], dma_key="ADB")
        P.op("sp", lambda e: e.dma_start(out=self.LN.ap, in_=self.lnT), writes=[self.LN.r], dma_key="LN")
        P.op("act", lambda e: e.activation(out=self.CA.ap, in_=self.CT.ap, func=AF.Silu),
             reads=[self.CT.r], writes=[self.CA.r])

    def stage_ada(self, layer):
        P = self.P
        wv = self.ada_w[layer].rearrange("(kc p) n -> p kc n", p=128)
        ps = self.PS[7]
        for gq in range(self.NADA // 4):
            s = self.rot("ws", 2)
            ws = self.WS[s]
            wsv = ws.ap[:, 0:KD * 512].rearrange("p (k n) -> p k n", k=KD)
            P.op("pool", lambda e, wsv=wsv, gq=gq: e.dma_start(out=wsv, in_=wv[:, :, 512 * gq:512 * gq + 512]),
                 writes=[ws.r, self.WSB[s]], dma_key=ws.key)

            def mm(e, wsv=wsv, gq=gq):
                ins = None
                for c4 in range(4):
                    j = 4 * gq + c4
                    for k in range(KD):
                        ins = e.matmul(ps.ap[:, j:j + 1], wsv[:, k, 128 * c4:128 * c4 + 128],
                                       self.CA.ap[:, k:k + 1], start=(k == 0), stop=(k == KD - 1))
                return ins
            P.op("pe", mm, reads=[ws.r, self.WSB[s], self.CA.r], writes=[ps.r])
        mod_l = self.MOD.ap[:, layer].rearrange("p a k -> p (a k)")
        NA = self.NADA
        P.op("dve", lambda e: e.tensor_tensor(out=mod_l[:, 0:NA], in0=ps.ap[:, 0:NA], in1=self.ADB.ap[:, layer], op=ALU.add),
             reads=[ps.r, self.ADB.r], writes=[self.MOD.r])
        if self.paired:
            P.op("sp", lambda e: e.dma_start(out=self.MIN[layer], in_=mod_l[:, 0:NA]), reads=[self.MOD.r],
                 writes=[self.dres(f"MIN{layer}")], dma_key="MODs")
            P.op("pool", lambda e: e.collective_compute("AllGather", ALU.bypass, replica_groups=[[0, 1], [2, 3], [4, 5], [6, 7]],
                                                        ins=[self.MIN[layer]], outs=[self.MOUT[layer]]),
                 reads=[self.dres(f"MIN{layer}")], writes=[self.dres(f"MOUT{layer}")], dma_key=f"ccM{layer}", cc=True)
            P.op("sp", lambda e: e.dma_start(out=mod_l.rearrange("p (r c) -> p r c", r=2),
                                             in_=self.MOUT[layer].rearrange("(r p) c -> p r c", p=128)),
                 reads=[self.dres(f"MOUT{layer}")], writes=[self.MOD.r], dma_key="MODl")
        for s in range(3):
            w = 1.0 if s == 1 else 0.5
            P.op("dve", lambda e, s=s: e.tensor_scalar(out=self.SC1P.ap[:, layer, s], in0=self.MOD.ap[:, layer, 3 * s + 1],
                                                       scalar1=1.0, scalar2=None, op0=ALU.add),
                 reads=[self.MOD.r], writes=[self.SC1P.r])
            P.op("dve", lambda e, s=s, w=w: e.tensor_scalar(out=self.GATE.ap[:, layer, s], in0=self.MOD.ap[:, layer, 3 * s + 2],
                                                            scalar1=1.0, scalar2=w / ALPHA, op0=ALU.add, op1=ALU.mult),
                 reads=[self.MOD.r], writes=[self.GATE.r])

    def stage_prologue(self, layer, sub):
        P = self.P
        self.ht_load_writes()
        for b in range(self.NB):
            for m in range(KD):
                for t in range(TB // TT):
                    xi, ho = self.ov[f"XI{self.rot('xi', 3)}"], self.ov[f"HO{self.rot('ho', 2)}"]
                    c0 = b * TB + t * TT
                    P.op("sp", lambda e, xi=xi, m=m, c0=c0: e.dma_start(out=xi.ap, in_=self.xT[m, :, c0:c0 + TT]),
                         writes=self.ovw(xi), dma_key=xi.key)
                    P.op("act", lambda e, xi=xi, ho=ho, m=m: e.activation(
                        out=ho.ap, in_=xi.ap, func=AF.Identity,
                        scale=self.SC1P.ap[:, layer, sub, m:m + 1], bias=self.MOD.ap[:, layer, 3 * sub, m:m + 1]),
                        reads=[xi.r, self.SC1P.r, self.MOD.r], writes=self.ovw(ho))
                    P.op("sp", lambda e, ho=ho, m=m, c0=c0: e.dma_start(out=self.HS[m, :, c0:c0 + TT], in_=ho.ap),
                         reads=[ho.r], writes=[self.dres(f"HS{m}_{c0}")], dma_key=ho.key)

    def load_ht(self, b):
        idx, pb = self.ht_plan[self.ht_pos]
        if self.hs_ver[b] != idx - 1:
            self.drain()
        assert pb == b and self.hs_ver[b] == idx - 1, (self.ht_plan, self.ht_pos, b, self.hs_ver)
        if self.ht_loaded <= self.ht_pos:
            self._load_ht(b)
            self.ht_loaded = self.ht_pos + 1
        self.ht_pos += 1

    def prefetch_ht(self):
        if self.ht_loaded == self.ht_pos and self.ht_pos < len(self.ht_plan):
            idx, b = self.ht_plan[self.ht_pos]
            if self.hs_ver[b] != idx - 1:
                return
            self._load_ht(b)
            self.ht_loaded = self.ht_pos + 1

    def _load_ht(self, b):
        P = self.P
        rl = [self.dres(f"HS{m}_{b * TB + t * TT}") for m in range(KD) for t in range(TB // TT)]
        P.op("sp", lambda e: e.dma_start(out=self.HT.ap, in_=self.HS[:, :, b * TB:(b + 1) * TB].rearrange("k p n -> p k n")),
             reads=rl, writes=self.ht_load_writes(), dma_key="HT")

    def out_proj(self, b, wv, KC, layer, sub, x_src, x_src_res, nxt, last):
        P = self.P
        S, Q, RSTD, NMR = self.ov["S"], self.ov["Q"], self.ov["RSTD"], self.ov["NMR"]
        P.op("dve", lambda e: e.memset(S.ap, 0.0), writes=self.ovw(S))
        P.op("dve", lambda e: e.memset(Q.ap, 0.0), writes=self.ovw(Q))
        steps = [(gq, t, mm) for gq in range(8) for t in range(TB // TT) for mm in range(2)]
        def load_x(step):
            gq, t, mm = step
            m = 2 * gq + mm
            i = self.rot("xi", 3)
            xi = self.ov[f"XI{i}"]
            c0 = b * TB + t * TT
            P.op("sp", lambda e: e.dma_start(out=xi.ap, in_=x_src[m, :, c0:c0 + TT]),
                 reads=[x_src_res(m, c0)], writes=self.ovw(xi), dma_key=xi.key)
            return xi
        xi_next = load_x(steps[0])
        ws = None
        self.drain()
        pend_before = 0
        for si, (gq, t, mm) in enumerate(steps):
            m = 2 * gq + mm
            if pend_before > 0:
                self.drain(1)
                pend_before -= 1
            if t == 0 and mm == 0:
                s = self.rot("ws", 2)
                ws = self.WS[s]
                wsv = ws.ap[:, 0:KC * 256].rearrange("p (k n) -> p k n", k=KC)
                wsr = [ws.r, self.WSB[s]]
                P.op("pool", lambda e, wsv=wsv, gq=gq: e.dma_start(out=wsv, in_=wv[:, :, 256 * gq:256 * gq + 256]),
                     writes=wsr, dma_key=ws.key)
            xi = xi_next
            if si + 1 < len(steps):
                xi_next = load_x(steps[si + 1])
            ps = self.PS[self.rot("ps_o", 2)]

            def mmf(e, wsv=wsv, mm=mm, t=t, ps=ps):
                ins = None
                for k in range(KC):
                    ins = e.matmul(ps.ap, wsv[:, k, 128 * mm:128 * mm + 128], self.GT.ap[:, k, t * TT:(t + 1) * TT],
                                   start=(k == 0), stop=(k == KC - 1))
                return ins
            P.op("pe", mmf, reads=wsr + [self.GT.r], writes=[ps.r])
            zo, sq = xi, self.SA[self.rot("sa", 2)]
            P.op("dve", lambda e, ps=ps, zo=zo, xi=xi, m=m: e.scalar_tensor_tensor(
                out=zo.ap, in0=ps.ap, scalar=self.GATE.ap[:, layer, sub, m:m + 1], in1=xi.ap, op0=ALU.mult, op1=ALU.add),
                reads=[ps.r, xi.r, self.GATE.r], writes=self.ovw(zo))
            P.op("sp", lambda e, zo=zo, m=m, t=t: e.dma_start(out=self.ZS[m, :, t * TT:(t + 1) * TT], in_=zo.ap),
                 reads=[zo.r], writes=[self.dres(f"ZS{m}_{t}")], dma_key=zo.key)
            P.op("act", lambda e, zo=zo, sq=sq: e.activation(out=sq.ap, in_=zo.ap, func=AF.Square),
                 reads=[zo.r], writes=self.ovw(sq))
            P.op("dve", lambda e, zo=zo, t=t: e.tensor_tensor(out=S.ap[:, t * TT:(t + 1) * TT], in0=S.ap[:, t * TT:(t + 1) * TT],
                                                             in1=zo.ap, op=ALU.add),
                 reads=[zo.r], writes=[S.r])
            P.op("dve", lambda e, sq=sq, t=t: e.tensor_tensor(out=Q.ap[:, t * TT:(t + 1) * TT], in0=Q.ap[:, t * TT:(t + 1) * TT],
                                                             in1=sq.ap, op=ALU.add),
                 reads=[sq.r], writes=[Q.r])
        self.drain()
        P.ss = True
        for t in range(TB // TT):
            pa, pb = self.PS[2], self.PS[3]
            sl = slice(t * TT, (t + 1) * TT)
            P.op("pe", lambda e, sl=sl: e.matmul(pa.ap, self.ONES.ap, S.ap[:, sl], start=True, stop=True),
                 reads=[S.r, self.ONES.r], writes=[pa.r])
            P.op("pe", lambda e, sl=sl: e.matmul(pb.ap, self.ONES.ap, Q.ap[:, sl], start=True, stop=True),
                 reads=[Q.r, self.ONES.r], writes=[pb.r])
            t1 = self.SA[self.rot("sa", 2)]
            P.op("act", lambda e, sl=sl: e.activation(out=NMR.ap[:, sl], in_=pa.ap, func=AF.Copy, scale=1.0 / D),
                 reads=[pa.r], writes=self.ovw(NMR))
            P.op("dve", lambda e, sl=sl, t1=t1: e.tensor_tensor(out=t1.ap, in0=NMR.ap[:, sl], in1=NMR.ap[:, sl], op=ALU.mult),
                 reads=[NMR.r], writes=self.ovw(t1))
            P.op("dve", lambda e, sl=sl, t1=t1: e.scalar_tensor_tensor(out=RSTD.ap[:, sl], in0=pb.ap, scalar=1.0 / D, in1=t1.ap,
                                                               op0=ALU.mult, op1=ALU.subtract),
                 reads=[pb.r, t1.r], writes=self.ovw(RSTD))
            P.op("act", lambda e, sl=sl: e.activation(out=RSTD.ap[:, sl], in_=RSTD.ap[:, sl], func=AF.Sqrt,
                                                     bias=self.EPSP.ap[:, 0:1], scale=1.0),
                 reads=[self.EPSP.r], writes=[RSTD.r])
            P.op("dve", lambda e, sl=sl: e.reciprocal(out=RSTD.ap[:, sl], in_=RSTD.ap[:, sl]), writes=[RSTD.r])
            P.op("dve", lambda e, sl=sl: e.scalar_tensor_tensor(out=NMR.ap[:, sl], in0=NMR.ap[:, sl], scalar=-1.0,
                                                               in1=RSTD.ap[:, sl], op0=ALU.mult, op1=ALU.mult),
                 reads=[RSTD.r], writes=[NMR.r])
        nsteps = [(m, t) for m in range(KD) for t in range(TB // TT)]

        def load_z(step):
            m, t = step
            i = self.rot("xi", 3)
            zi = self.ov[f"XI{i}"]
            P.op("sp", lambda e: e.dma_start(out=zi.ap, in_=self.ZS[m, :, t * TT:(t + 1) * TT]),
                 reads=[self.dres(f"ZS{m}_{t}")], writes=self.ovw(zi), dma_key=zi.key)
            return zi
        cur_idx = self.cur_idx

        def make_step(si, m, t):
            def step():
                zi = load_z((m, t))
                sl = slice(t * TT, (t + 1) * TT)
                t1, xo, ho = zi, zi, self.ov[f"HO{self.rot('ho', 2)}"]
                c0 = b * TB + t * TT
                P.op("dve", lambda e: e.tensor_tensor(out=t1.ap, in0=zi.ap, in1=RSTD.ap[:, sl], op=ALU.mult),
                     reads=[zi.r, RSTD.r], writes=[t1.r])
                P.op("dve", lambda e: e.tensor_tensor(out=t1.ap, in0=t1.ap, in1=NMR.ap[:, sl], op=ALU.add),
                     reads=[NMR.r], writes=[t1.r])
                P.op("act", lambda e: e.activation(
                    out=xo.ap, in_=t1.ap, func=AF.Identity, scale=self.LN.ap[:, layer, sub, 0, m:m + 1],
                    bias=self.LN.ap[:, layer, sub, 1, m:m + 1]),
                    reads=[t1.r, self.LN.r], writes=[xo.r])
                if last:
                    P.op("sp", lambda e: e.dma_start(out=self.outT[m, :, c0:c0 + TT], in_=xo.ap),
                         reads=[xo.r], writes=[self.dres(f"OUT{m}_{c0}")], dma_key=xo.key)
                else:
                    P.op("sp", lambda e: e.dma_start(out=self.XS[m, :, c0:c0 + TT], in_=xo.ap),
                         reads=[xo.r], writes=[self.dres(f"XS{m}_{c0}")], dma_key=xo.key)
                    nl, ns = nxt
                    P.op("act", lambda e: e.activation(
                        out=ho.ap, in_=xo.ap, func=AF.Identity, scale=self.SC1P.ap[:, nl, ns, m:m + 1],
                        bias=self.MOD.ap[:, nl, 3 * ns, m:m + 1]),
                        reads=[xo.r, self.SC1P.r, self.MOD.r], writes=[ho.r])
                    P.op("sp", lambda e: e.dma_start(out=self.HS[m, :, c0:c0 + TT], in_=ho.ap),
                         reads=[ho.r], writes=[self.dres(f"HS{m}_{c0}")], dma_key=ho.key)
                if si == len(nsteps) - 1:
                    self.hs_ver[b] = cur_idx
            return step
        for si, (m, t) in enumerate(nsteps):
            self.deferred.append(make_step(si, m, t))
        return

        self.hs_ver[b] = self.cur_idx

    def stage_ffn(self, layer, sub, x_src, x_res_fn, nxt, last):
        P = self.P
        P.ss = True
        wi = 0 if sub == 0 else 1
        w_in = self.ffn_w_in[(layer, wi)].rearrange("(kc p) n -> p kc n", p=128)
        w_out = self.ffn_w_out[(layer, wi)].rearrange("(kc p) n -> p kc n", p=128)
        for b in range(self.NB):
            self.load_ht(b)
            for gq in range(DFF // 256):
                s = self.rot("ws", 2)
                ws = self.WS[s]
                wsa = ws.ap[:, 0:KD * 256].rearrange("p (k f) -> p k f", k=KD)
                wsu = ws.ap[:, KD * 256:KD * 512].rearrange("p (k f) -> p k f", k=KD)
                wsr = [ws.r, self.WSB[s]]
                P.op("pool", lambda e, wsa=wsa, gq=gq: e.dma_start(out=wsa, in_=w_in[:, :, 256 * gq:256 * gq + 256]),
                     writes=[ws.r], dma_key=ws.key)
                P.op("pool", lambda e, wsu=wsu, gq=gq: e.dma_start(out=wsu, in_=w_in[:, :, DFF + 256 * gq:DFF + 256 * gq + 256]),
                     writes=[self.WSB[s]], dma_key=ws.key + "b")
                for t in range(TB // TT):
                    for sb_ in range(2):
                        j = 2 * gq + sb_
                        pp = self.rot("ps_f", 2)
                        pa, pu = self.PS[4 + 2 * pp], self.PS[5 + 2 * pp]

                        def mmf(e, wsa=wsa, wsu=wsu, sb_=sb_, t=t, pa=pa, pu=pu):
                            ins = None
                            for k in range(KD):
                                ins = e.matmul(pa.ap, wsa[:, k, 128 * sb_:128 * sb_ + 128],
                                               self.HT.ap[:, k, t * TT:(t + 1) * TT], start=(k == 0), stop=(k == KD - 1))
                            for k in range(KD):
                                ins = e.matmul(pu.ap, wsu[:, k, 128 * sb_:128 * sb_ + 128],
                                               self.HT.ap[:, k, t * TT:(t + 1) * TT], start=(k == 0), stop=(k == KD - 1))
                            return ins
                        P.op("pe", mmf, reads=wsr + [self.HT.r], writes=[pa.r, pu.r])
                        sa = self.SA[self.rot("sa", 2)]
                        P.op("act", lambda e, sa=sa, pa=pa: e.activation(out=sa.ap, in_=pa.ap, func=AF.Silu),
                             reads=[pa.r], writes=[sa.r])
                        P.op("dve", lambda e, sa=sa, pu=pu, j=j, t=t: e.tensor_tensor(
                            out=self.GT.ap[:, j, t * TT:(t + 1) * TT], in0=sa.ap, in1=pu.ap, op=ALU.mult),
                            reads=[sa.r, pu.r], writes=[self.GT.r])
                        if (t + sb_) % 2 == 1:
                            self.drain(1)
            self.drain()
            self.prefetch_ht()
            self.out_proj(b, w_out, KF, layer, sub, x_src, x_res_fn, nxt, last)

    def fence(self, old, new):
        self.drain()
        if not hasattr(self, "FD"):
            self.FD = Buf(self.sb("FD", [128, 2], F32)[:], "FD")
        self.P.op("dve", lambda e: e.memset(self.FD.ap, 0.0), writes=[self.FD.r] + list(old) + list(new))

    def carve(self, base_t, off, name, shape, dt):
        n = int(np.prod(shape[1:]))
        units = n * (2 if dt == F32 else 1)
        ap = base_t[:, off:off + units]
        if dt == F32:
            ap = ap.bitcast(F32)
        if len(shape) == 3:
            ap = ap.rearrange("p (a b) -> p a b", a=shape[1])
        return Buf(ap, name), off + units

    def stage_attn(self, layer, sub, x_src, x_res_fn, nxt, last):
        P, NT, NB = self.P, self.NT, self.NB
        P.ss = True
        lam_init = 0.8 - 0.6 * math.exp(-0.3 * layer)
        SCALE = 128 ** -0.5
        wq = self.attn_w_qkv.rearrange("(kc p) n -> p kc n", p=128)
        wo = self.attn_w_o.rearrange("(kc p) n -> p kc n", p=128)
        NQT = NT // TT
        NKT = NT // 128
        LAMV = Buf(self.sb("LAMV", [128, 4, 128], F32)[:], "LAMV")
        ATC = Buf(self.sb("ATC", [128, 8], F32)[:], "ATC")
        SUBG = Buf(self.sb("SUBG", [128, 2], F32)[:], "SUBG")
        QS = [Buf(self.sb(f"QS{i}", [128, TT], BF16)[:], f"QS{i}") for i in range(2)]
        P.op("sp", lambda e: e.dma_start(out=LAMV.ap, in_=self.lamv), writes=[LAMV.r], dma_key="LAMV")
        P.op("sp", lambda e: e.dma_start(out=SUBG.ap, in_=self.sublnT), writes=[SUBG.r], dma_key="SUBG")
        P.op("dve", lambda e: e.tensor_scalar(out=SUBG.ap, in0=SUBG.ap, scalar1=(1.0 - lam_init), scalar2=None, op0=ALU.mult),
             writes=[SUBG.r])
        for q in range(2):
            P.op("dve", lambda e, q=q: e.tensor_tensor(out=LAMV.ap[:, 2 * q], in0=LAMV.ap[:, 2 * q], in1=LAMV.ap[:, 2 * q + 1],
                                                      op=ALU.mult), writes=[LAMV.r])
            P.op("dve", lambda e, q=q: e.reduce_sum(out=ATC.ap[:, q:q + 1], in_=LAMV.ap[:, 2 * q], axis=mybir.AxisListType.X),
                 reads=[LAMV.r], writes=[ATC.r])
        P.op("act", lambda e: e.activation(out=ATC.ap[:, 2:4], in_=ATC.ap[:, 0:2], func=AF.Exp), writes=[ATC.r])
        P.op("dve", lambda e: e.tensor_tensor(out=ATC.ap[:, 4:5], in0=ATC.ap[:, 3:4], in1=ATC.ap[:, 2:3], op=ALU.subtract),
             writes=[ATC.r])
        P.op("dve", lambda e: e.tensor_scalar(out=ATC.ap[:, 5:6], in0=ATC.ap[:, 4:5], scalar1=-lam_init, scalar2=None, op0=ALU.add),
             writes=[ATC.r])
        NEGLAM = ATC.ap[:, 5:6]

        for b in range(NB):
            self.load_ht(b)
            for g in range(16):
                s = self.rot("ws", 2)
                ws = self.WS[s]
                wsr = [ws.r, self.WSB[s]]
                wsv = ws.ap[:, 0:KD * 256].rearrange("p (k f) -> p k f", k=KD)
                P.op("pool", lambda e, wsv=wsv, g=g: e.dma_start(out=wsv, in_=wq[:, :, 256 * g:256 * g + 256]),
                     writes=wsr, dma_key=ws.key)
                for t in range(TB // TT):
                    for sb_ in range(2):
                        ps = self.PS[4 + self.rot("ps_a", 4)]

                        def mmf(e, wsv=wsv, sb_=sb_, t=t, ps=ps):
                            ins = None
                            for k in range(KD):
                                ins = e.matmul(ps.ap, wsv[:, k, 128 * sb_:128 * sb_ + 128], self.HT.ap[:, k, t * TT:(t + 1) * TT],
                                               start=(k == 0), stop=(k == KD - 1))
                            return ins
                        P.op("pe", mmf, reads=wsr + [self.HT.r], writes=[ps.r])
                        qi = self.rot("qs", 2)
                        qs = QS[qi]
                        if qi == 0:
                            P.op("act", lambda e, qs=qs, ps=ps: e.activation(out=qs.ap, in_=ps.ap, func=AF.Copy),
                                 reads=[ps.r], writes=[qs.r])
                        else:
                            P.op("dve", lambda e, qs=qs, ps=ps: e.tensor_copy(out=qs.ap, in_=ps.ap), reads=[ps.r], writes=[qs.r])
                        ch = 2 * g + sb_
                        c0 = b * TB + t * TT
                        if ch < 16:
                            dst = self.QKS[ch, :, c0:c0 + TT]
                        else:
                            dst = self.KIN[(ch - 16) * 128:(ch - 15) * 128, c0:c0 + TT]
                        P.op("sp", lambda e, qs=qs, dst=dst: e.dma_start(out=dst, in_=qs.ap),
                             reads=[qs.r], writes=[self.dres(f"QK{ch}_{c0}")], dma_key=qs.key)
                        self.drain(1)
            for gv in range(4):
                s = self.rot("ws", 2)
                ws = self.WS[s]
                wsr = [ws.r, self.WSB[s]]
                wsv = ws.ap[:, 0:KD * 512].rearrange("p (k f) -> p k f", k=KD)
                P.op("pool", lambda e, wsv=wsv, gv=gv: e.dma_start(out=wsv, in_=wq[:, :, 2 * D + 512 * gv:2 * D + 512 * gv + 512]),
                     writes=wsr, dma_key=ws.key)
                for tt in range(TB // 128):
                    ps = self.PS[4 + self.rot("ps_a", 4)]

                    def mmv(e, wsv=wsv, tt=tt, ps=ps):
                        ins = None
                        for k in range(KD):
                            ins = e.matmul(ps.ap, self.HT.ap[:, k, 128 * tt:128 * tt + 128], wsv[:, k, :],
                                           start=(k == 0), stop=(k == KD - 1))
                        return ins
                    P.op("pe", mmv, reads=wsr + [self.HT.r], writes=[ps.r])
                    qi = self.rot("qs", 2)
                    qs = QS[qi]
                    if qi == 0:
                        P.op("act", lambda e, qs=qs, ps=ps: e.activation(out=qs.ap, in_=ps.ap, func=AF.Copy),
                             reads=[ps.r], writes=[qs.r])
                    else:
                        P.op("dve", lambda e, qs=qs, ps=ps: e.tensor_copy(out=qs.ap, in_=ps.ap), reads=[ps.r], writes=[qs.r])
                    tk = b * (TB // 128) + tt
                    P.op("sp", lambda e, qs=qs, tk=tk, gv=gv: e.dma_start(out=self.VIN[tk * 128:(tk + 1) * 128, 512 * gv:512 * gv + 512], in_=qs.ap),
                         reads=[qs.r], writes=[self.dres(f"V{tk}_{gv}")], dma_key=qs.key)
            self.drain()
            if b + 1 < NB:
                self.prefetch_ht()

        P.ss = True
        NPREV = NT // 128 if self.paired else 0
        if self.paired:
            NEGB = Buf(self.sb("NEGB", [128, 1], F32)[:], "NEGB")
            PM = Buf(self.sb("PM", [128, 1], F32)[:], "PM")
            P.op("sp", lambda e: e.dma_start(out=PM.ap, in_=self.pmask), writes=[PM.r], dma_key="PM")
            P.op("dve", lambda e: e.tensor_scalar(out=NEGB.ap, in0=PM.ap, scalar1=-1.0, scalar2=30000.0, op0=ALU.add, op1=ALU.mult),
                 reads=[PM.r], writes=[NEGB.r])
            grp = [[0, 1], [2, 3], [4, 5], [6, 7]]
            for i in range(D // 512):
                rk = [self.dres(f"QK{16 + 4 * i + cc_}_{c0}") for cc_ in range(4) for c0 in range(0, NT, TT)]
                P.op("pool", lambda e, i=i: e.collective_compute("AllGather", ALU.bypass, replica_groups=grp,
                                                                 ins=[self.KIN[512 * i:512 * i + 512, :]], outs=[self.KOUT[i]]),
                     reads=rk, writes=[self.dres(f"KOUT{i}")], dma_key=f"ccK{i}", cc=True)
            for i in range(NT // 512):
                rv = [self.dres(f"V{tk}_{gv}") for tk in range(4 * i, 4 * i + 4) for gv in range(4)]
                P.op("pool", lambda e, i=i: e.collective_compute("AllGather", ALU.bypass, replica_groups=grp,
                                                                 ins=[self.VIN[512 * i:512 * i + 512, :]], outs=[self.VOUT[i]]),
                     reads=rv, writes=[self.dres(f"VOUT{i}")], dma_key=f"ccV{i}", cc=True)

        NKEY = NPREV + NKT
        KH, VH, QT, PT, OST = [], [], [], [], []
        off = 0
        for i in range(2):
            bf, off = self.carve(self.HT_t, off, f"KH{i}", [128, 2, NKEY * 128], BF16)
            KH.append(bf)
        assert off <= KD * TB
        off = 0
        VP = NKEY // 4
        for i in range(2):
            parts = []
            for pp in range(VP):
                bf, off = self.carve(self.GT_t, off, f"VH{i}_{pp}", [128, 4, 256], BF16)
                parts.append(bf)
            VH.append(parts)
        for i in range(2):
            bf, off = self.carve(self.GT_t, off, f"QT{i}", [128, 2, TT], BF16)
            QT.append(bf)
        for i in range(6):
            bf, off = self.carve(self.GT_t, off, f"PT{i}", [128, TT], BF16)
            PT.append(bf)
        ACC = []
        for i in range(4):
            bf, off = self.carve(self.GT_t, off, f"ACC{i}", [128, TT], F32)
            ACC.append(bf)
        for i in range(2):
            bf, off = self.carve(self.GT_t, off, f"OST{i}", [128, 2, TT], BF16)
            OST.append(bf)
        RL, off = self.carve(self.GT_t, off, "RL", [128, TT], F32)
        ONJ = []
        for i in range(2):
            bf, off = self.carve(self.GT_t, off, f"ONJ{i}", [128, 2 * TT], F32)
            ONJ.append(bf)
        OD, off = self.carve(self.GT_t, off, "OD", [128, 2 * TT], F32)
        SQO, off = self.carve(self.GT_t, off, "SQO", [128, 2 * TT], F32)
        RS, off = self.carve(self.GT_t, off, "RS", [128, TT], F32)
        assert off <= KF * TB, off
        scratch = KH + [p_ for v_ in VH for p_ in v_] + QT + PT + OST + [RL, OD, SQO, RS] + ONJ + ACC
        old = [self.HT.r, self.GT.r] + [bb.r for bb in self.ov.values()]
        self.fence(old, [x.r for x in scratch])
        for h in range(8):
            i = h % 2
            kh, vh = KH[i], VH[i]
            rk = [self.dres(f"QK{16 + 2 * h + j}_{c0}") for j in range(2) for c0 in range(0, NT, TT)]
            if self.paired:
                r0 = 256 * (h % 2)
                P.op("sp", lambda e, kh=kh, h=h, r0=r0: e.dma_start(
                    out=kh.ap[:, :, 0:NT], in_=self.KOUT[h // 2][r0:r0 + 256, :].rearrange("(j p) n -> p j n", p=128)),
                    reads=[self.dres(f"KOUT{h // 2}")], writes=[kh.r], dma_key=kh.key + "p")
            P.op("sp", lambda e, kh=kh, h=h: e.dma_start(
                out=kh.ap[:, :, NPREV * 128:NPREV * 128 + NT],
                in_=self.KIN[256 * h:256 * h + 256, :].rearrange("(j p) n -> p j n", p=128)),
                reads=rk, writes=[kh.r], dma_key=kh.key)
            for pp in range(VP):
                vp = vh[pp]
                if pp * 4 < NPREV:
                    P.op("sp", lambda e, vp=vp, h=h, pp=pp: e.dma_start(
                        out=vp.ap, in_=self.VOUT[pp][0:512, 256 * h:256 * h + 256].rearrange("(t p) e -> p t e", p=128)),
                        reads=[self.dres(f"VOUT{pp}")], writes=[vp.r], dma_key=vp.key)
                else:
                    po_ = pp - NPREV // 4
                    rv = [self.dres(f"V{tk}_{h // 2}") for tk in range(po_ * 4, po_ * 4 + 4)]
                    P.op("sp", lambda e, vp=vp, h=h, po_=po_: e.dma_start(
                        out=vp.ap, in_=self.VIN[po_ * 512:(po_ + 1) * 512, 256 * h:256 * h + 256].rearrange("(t p) e -> p t e", p=128)),
                        reads=rv, writes=[vp.r], dma_key=vp.key)
            for t in range(NQT):
                qt = QT[self.rot("qt", 2)]
                rq = [self.dres(f"QK{2 * h + j}_{t * TT}") for j in range(2)]
                P.op("sp", lambda e, qt=qt, h=h, t=t: e.dma_start(
                    out=qt.ap, in_=self.QKS[2 * h:2 * h + 2, :, t * TT:(t + 1) * TT].rearrange("j p n -> p j n")),
                    reads=rq, writes=[qt.r], dma_key=qt.key)
                nkt = NPREV + 4 * (t + 1)
                for j in range(2):
                    po = [self.PS[3 + 2 * j], self.PS[4 + 2 * j]]
                    pl = self.PS[7]
                    acc = [ACC[2 * j], ACC[2 * j + 1]]
                    P.op("dve", lambda e, a_=acc[0]: e.memset(a_.ap, 0.0), writes=[acc[0].r])
                    pend = []

                    def emit_pv(kt, c0, pt, po=po, vh=vh, nkt=nkt, acc=acc):
                        vp = vh[kt // 4]

                        def pv(e, kt=kt, c0=c0, pt=pt, po=po, vp=vp, nkt=nkt):
                            ins = None
                            for c in range(2):
                                ins = e.matmul(po[c].ap[:, c0:TT], vp.ap[:, kt % 4, 128 * c:128 * c + 128], pt.ap[:, c0:TT],
                                               start=(kt == 0), stop=(kt == nkt - 1))
                            return ins
                        P.op("pe", pv, reads=[vp.r, pt.r], writes=[po[0].r, po[1].r])
                        a_ = acc[0]
                        eng_ = "dve"
                        P.op(eng_, lambda e, a_=a_, pt=pt, c0=c0: e.tensor_tensor(out=a_.ap[:, c0:TT], in0=a_.ap[:, c0:TT],
                                                                                 in1=pt.ap[:, c0:TT], op=ALU.add),
                             reads=[pt.r], writes=[a_.r])
                    for kt in range(nkt):
                        dpos = kt - (NPREV + 4 * t)
                        c0 = 128 * dpos if dpos > 0 else 0
                        pss = self.PS[self.rot("ps_s", 3)]
                        pt = PT[self.rot("pt", 6)]
                        P.op("pe", lambda e, pss=pss, kt=kt, c0=c0, j=j, kh=kh, qt=qt: e.matmul(
                            pss.ap[:, c0:TT], kh.ap[:, j, 128 * kt:128 * kt + 128], qt.ap[:, j, c0:TT], start=True, stop=True),
                            reads=[kh.r, qt.r], writes=[pss.r])
                        if kt < NPREV:
                            P.op("act", lambda e, pss=pss, pt=pt: e.activation(
                                out=pt.ap, in_=pss.ap, func=AF.Exp, scale=SCALE, bias=NEGB.ap[:, 0:1]),
                                reads=[pss.r, NEGB.r], writes=[pt.r])
                        else:
                            P.op("act", lambda e, pss=pss, pt=pt, c0=c0: e.activation(
                                out=pt.ap[:, c0:TT], in_=pss.ap[:, c0:TT], func=AF.Exp, scale=SCALE),
                                reads=[pss.r], writes=[pt.r])
                        if dpos >= 0:
                            P.op("pool", lambda e, pt=pt, c0=c0: e.memset(pt.ap[64:128, c0:c0 + 64], 0.0), writes=[pt.r])
                        pend.append((kt, c0, pt))
                        if len(pend) > 2:
                            emit_pv(*pend.pop(0))
                    while pend:
                        emit_pv(*pend.pop(0))

                    def lsum(e, pl=pl, acc=acc):
                        return e.matmul(pl.ap, self.ONES.ap, acc[0].ap, start=True, stop=True)
                    P.op("pe", lsum, reads=[acc[0].r, self.ONES.r], writes=[pl.r])
                    P.op("act", lambda e, pl=pl: e.activation(out=RL.ap, in_=pl.ap, func=AF.Ln), reads=[pl.r], writes=[RL.r])
                    P.op("act", lambda e: e.activation(out=RL.ap, in_=RL.ap, func=AF.Exp, scale=-1.0), writes=[RL.r])
                    for c in range(2):
                        P.op("dve", lambda e, c=c, j=j, po=po: e.tensor_tensor(out=ONJ[j].ap[:, c * TT:(c + 1) * TT], in0=po[c].ap,
                                                                              in1=RL.ap, op=ALU.mult),
                             reads=[po[c].r, RL.r], writes=[ONJ[j].r])
                P.op("dve", lambda e: e.scalar_tensor_tensor(out=OD.ap, in0=ONJ[1].ap, scalar=NEGLAM, in1=ONJ[0].ap,
                                                             op0=ALU.mult, op1=ALU.add),
                     reads=[ONJ[0].r, ONJ[1].r, ATC.r], writes=[OD.r])
                P.op("act", lambda e: e.activation(out=SQO.ap, in_=OD.ap, func=AF.Square), reads=[OD.r], writes=[SQO.r])
                pss = self.PS[self.rot("ps_s", 3)]

                def msf(e, pss=pss):
                    e.matmul(pss.ap, self.ONES.ap, SQO.ap[:, 0:TT], start=True, stop=False)
                    return e.matmul(pss.ap, self.ONES.ap, SQO.ap[:, TT:2 * TT], start=False, stop=True)
                P.op("pe", msf, reads=[SQO.r, self.ONES.r], writes=[pss.r])
                P.op("act", lambda e, pss=pss: e.activation(out=RS.ap, in_=pss.ap, func=AF.Ln, scale=1.0 / 256.0, bias=self.RMSE.ap[:, 0:1]),
                     reads=[pss.r, self.RMSE.r], writes=[RS.r])
                P.op("act", lambda e: e.activation(out=RS.ap, in_=RS.ap, func=AF.Exp, scale=-0.5), writes=[RS.r])
                ost = OST[self.rot("ost", 2)]
                for c in range(2):
                    P.op("dve", lambda e, c=c: e.tensor_tensor(out=OD.ap[:, c * TT:(c + 1) * TT], in0=OD.ap[:, c * TT:(c + 1) * TT],
                                                              in1=RS.ap, op=ALU.mult), reads=[RS.r], writes=[OD.r])
                    P.op("act", lambda e, c=c, ost=ost: e.activation(out=ost.ap[:, c], in_=OD.ap[:, c * TT:(c + 1) * TT],
                                                                    func=AF.Identity, scale=SUBG.ap[:, c:c + 1], bias=0.0),
                         reads=[OD.r, SUBG.r], writes=[ost.r])
                P.op("sp", lambda e, ost=ost, h=h, t=t: e.dma_start(
                    out=self.ONS[2 * h:2 * h + 2, :, t * TT:(t + 1) * TT].rearrange("c p n -> p c n"), in_=ost.ap),
                    reads=[ost.r], writes=[self.dres(f"ON{h}_{t}")], dma_key=ost.key)
        P.ss = True
        self.fence([x.r for x in scratch], [self.HT.r, self.GT.r] + [bb.r for bb in self.ov.values()])
        for b in range(NB):
            rl = [self.dres(f"ON{h}_{b * (TB // TT) + t}") for h in range(8) for t in range(TB // TT)]
            P.op("sp", lambda e, b=b: e.dma_start(out=self.GT.ap[:, 0:KD, :],
                                                 in_=self.ONS[:, :, b * TB:(b + 1) * TB].rearrange("k p n -> p k n")),
                 reads=rl, writes=[self.GT.r], dma_key="GT")
            self.drain()
            self.out_proj(b, wo, KD, layer, sub, x_src, x_res_fn, nxt, last)
            self.prefetch_ht()

    def stage_lru(self, layer, sub, x_src, x_res_fn, nxt, last):
        P, NT, NB = self.P, self.NT, self.NB
        P.ss = True
        w_in = self.lru_w_in.rearrange("(kc p) n -> p kc n", p=128)
        w_out = self.lru_w_out.rearrange("(kc p) n -> p kc n", p=128)
        gav = self.lru_ga.rearrange("n c d -> (n c) d").rearrange("(q p) d -> p q d", p=128)
        gxv = self.lru_gx.rearrange("n c d -> (n c) d").rearrange("(q p) d -> p q d", p=128)
        CW = Buf(self.sb("CW", [128, KR, 4], F32)[:], "CW")
        LV = Buf(self.sb("LV", [128, 4, KR], F32)[:], "LV")
        LC = Buf(self.sb("LC", [128, 4, KR], F32)[:], "LC")
        HALO = Buf(self.sb("HALO", [128, KR, 4], F32)[:], "HALO")
        STATE = Buf(self.sb("STATE", [128, KR], F32)[:], "STATE")
        P.op("sp", lambda e: e.dma_start(out=CW.ap, in_=self.cwT), writes=[CW.r], dma_key="CW")
        P.op("sp", lambda e: e.dma_start(out=LV.ap, in_=self.lruv), writes=[LV.r], dma_key="LV")
        P.op("dve", lambda e: e.memset(HALO.ap, 0.0), writes=[HALO.r])
        P.op("dve", lambda e: e.memset(STATE.ap, 0.0), writes=[STATE.r])
        lam = LV.ap[:, 3]
        P.op("act", lambda e: e.activation(out=LC.ap[:, 0], in_=lam, func=AF.Abs), reads=[LV.r], writes=[LC.r])
        P.op("act", lambda e: e.activation(out=LC.ap[:, 0], in_=LC.ap[:, 0], func=AF.Exp, scale=-1.0), writes=[LC.r])
        P.op("act", lambda e: e.activation(out=LC.ap[:, 0], in_=LC.ap[:, 0], func=AF.Ln, bias=1.0, scale=1.0), writes=[LC.r])
        P.op("dve", lambda e: e.tensor_scalar(out=LC.ap[:, 1], in0=lam, scalar1=-1.0, scalar2=0.0, op0=ALU.mult, op1=ALU.max),
             reads=[LV.r], writes=[LC.r])
        P.op("dve", lambda e: e.tensor_tensor(out=LC.ap[:, 1], in0=LC.ap[:, 1], in1=LC.ap[:, 0], op=ALU.add), writes=[LC.r])
        P.op("dve", lambda e: e.tensor_scalar(out=LC.ap[:, 2], in0=LC.ap[:, 1], scalar1=-LRU_C, scalar2=None, op0=ALU.mult), writes=[LC.r])
        P.op("dve", lambda e: e.tensor_scalar(out=LC.ap[:, 3], in0=LC.ap[:, 1], scalar1=-2.0 * LRU_C, scalar2=None, op0=ALU.mult), writes=[LC.r])
        off = KR * TB
        XB, XC, XCB, GG, RT, IT, GAW, GXW, AA, HSOs = [], [], [], [], [], [], [], [], [], []
        for i in range(2):
            xb_, xc_, xcb_, gg_ = [], [], [], []
            for sb_ in range(2):
                bf, off = self.carve(self.GT_t, off, f"XB{i}{sb_}", [128, TT + 4], F32); xb_.append(bf)
                bf, off = self.carve(self.GT_t, off, f"XC{i}{sb_}", [128, TT], F32); xc_.append(bf)
                bf, off = self.carve(self.GT_t, off, f"XCB{i}{sb_}", [128, TT], BF16); xcb_.append(bf)
                bf, off = self.carve(self.GT_t, off, f"GG{i}{sb_}", [128, TT], BF16); gg_.append(bf)
            XB.append(xb_); XC.append(xc_); XCB.append(xcb_); GG.append(gg_)
            bf, off = self.carve(self.GT_t, off, f"RT{i}", [128, TT], F32); RT.append(bf)
            bf, off = self.carve(self.GT_t, off, f"IT{i}", [128, TT], F32); IT.append(bf)
            bf, off = self.carve(self.GT_t, off, f"GAW{i}", [128, 2, 256], BF16); GAW.append(bf)
            bf, off = self.carve(self.GT_t, off, f"GXW{i}", [128, 2, 256], BF16); GXW.append(bf)
            bf, off = self.carve(self.GT_t, off, f"AA{i}", [128, TT], F32); AA.append(bf)
            bf, off = self.carve(self.GT_t, off, f"HSO{i}", [128, TT], F32); HSOs.append(bf)
        assert off <= KF * TB, off
        scratch = [x for l_ in (XB + XC + XCB + GG) for x in l_] + RT + IT + GAW + GXW + AA + HSOs
        self.fence([self.GT.r], [x.r for x in scratch] + [self.GT.r])
        GELU_K = 2.0 * math.sqrt(2.0 / math.pi)

        def lru_block(b, state_only):
            self.load_ht(b)
            nst = {}
            units = [(n, t) for n in range(10) for t in range(TB // TT)]

            def stage1(i):
                n, t = units[i]
                u = i % 2
                if t == 0:
                    s = self.rot("ws", 2)
                    ws = self.WS[s]
                    wsr = [ws.r, self.WSB[s]]
                    wsa = ws.ap[:, 0:KD * 256].rearrange("p (k f) -> p k f", k=KD)
                    wsu = ws.ap[:, KD * 256:KD * 512].rearrange("p (k f) -> p k f", k=KD)
                    if not state_only:
                        P.op("pool", lambda e, wsa=wsa, n=n: e.dma_start(out=wsa, in_=w_in[:, :, 256 * n:256 * n + 256]),
                             writes=[ws.r], dma_key=ws.key)
                    P.op("pool", lambda e, wsu=wsu, n=n: e.dma_start(out=wsu, in_=w_in[:, :, DRNN + 256 * n:DRNN + 256 * n + 256]),
                         writes=[self.WSB[s]], dma_key=ws.key + "b")
                    gi = self.rot("gw", 2)
                    gaw, gxw = GAW[gi], GXW[gi]
                    P.op("pool", lambda e, gaw=gaw, n=n: e.dma_start(out=gaw.ap, in_=gav[:, 2 * n:2 * n + 2, :]), writes=[gaw.r], dma_key=gaw.key)
                    P.op("pool", lambda e, gxw=gxw, n=n: e.dma_start(out=gxw.ap, in_=gxv[:, 2 * n:2 * n + 2, :]), writes=[gxw.r], dma_key=gxw.key)
                    nst[n] = (wsr, wsa, wsu, gaw, gxw)
                wsr, wsa, wsu, gaw, gxw = nst[n]
                xbs, xcs, xcbs, ggs = XB[u], XC[u], XCB[u], GG[u]
                tsl = slice(t * TT, (t + 1) * TT)
                for sb_ in range(2):
                    c = 2 * n + sb_
                    xb, xc = xbs[sb_], xcs[sb_]
                    P.op("dve", lambda e, xb=xb, c=c: e.tensor_copy(out=xb.ap[:, 0:3], in_=HALO.ap[:, c, 0:3]),
                         reads=[HALO.r], writes=[xb.r])
                    pp = self.rot("ps_f", 2)
                    pg, px = self.PS[4 + 2 * pp], self.PS[5 + 2 * pp]

                    def mmf(e, wsa=wsa, wsu=wsu, sb_=sb_, tsl=tsl, pg=pg, px=px, state_only=state_only):
                        ins = None
                        for k in range(KD if not state_only else 0):
                            ins = e.matmul(pg.ap, wsa[:, k, 128 * sb_:128 * sb_ + 128], self.HT.ap[:, k, tsl],
                                           start=(k == 0), stop=(k == KD - 1))
                        for k in range(KD):
                            ins = e.matmul(px.ap, wsu[:, k, 128 * sb_:128 * sb_ + 128], self.HT.ap[:, k, tsl],
                                           start=(k == 0), stop=(k == KD - 1))
                        return ins
                    P.op("pe", mmf, reads=wsr + [self.HT.r], writes=[pg.r, px.r])
                    P.op("act", lambda e, px=px, xb=xb: e.activation(out=xb.ap[:, 3:3 + TT], in_=px.ap, func=AF.Copy),
                         reads=[px.r], writes=[xb.r])
                    if not state_only:
                        sa = self.SA[self.rot("sa", 2)]
                        P.op("act", lambda e, sa=sa, pg=pg: e.activation(out=sa.ap, in_=pg.ap, func=AF.Square), reads=[pg.r], writes=[sa.r])
                        P.op("dve", lambda e, sa=sa: e.tensor_scalar(out=sa.ap, in0=sa.ap, scalar1=0.044715, scalar2=1.0,
                                                                    op0=ALU.mult, op1=ALU.add), writes=[sa.r])
                        P.op("dve", lambda e, sa=sa, pg=pg: e.tensor_tensor(out=sa.ap, in0=sa.ap, in1=pg.ap, op=ALU.mult),
                             reads=[pg.r], writes=[sa.r])
                        P.op("act", lambda e, sa=sa: e.activation(out=sa.ap, in_=sa.ap, func=AF.Sigmoid, scale=GELU_K), writes=[sa.r])
                        P.op("dve", lambda e, sa=sa, pg=pg, gg=ggs[sb_]: e.tensor_tensor(out=gg.ap, in0=sa.ap, in1=pg.ap, op=ALU.mult),
                             reads=[sa.r, pg.r], writes=[ggs[sb_].r])
                    P.op("dve", lambda e, xb=xb, xc=xc, c=c: e.tensor_scalar(out=xc.ap, in0=xb.ap[:, 3:3 + TT], scalar1=CW.ap[:, c, 3:4],
                                                                             scalar2=LV.ap[:, 0, c:c + 1], op0=ALU.mult, op1=ALU.add),
                         reads=[xb.r, CW.r, LV.r], writes=[xc.r])
                    for kk in (2, 1, 0):
                        P.op("dve", lambda e, xb=xb, xc=xc, c=c, kk=kk: e.scalar_tensor_tensor(
                            out=xc.ap, in0=xb.ap[:, kk:kk + TT], scalar=CW.ap[:, c, kk:kk + 1], in1=xc.ap, op0=ALU.mult, op1=ALU.add),
                            reads=[xb.r, CW.r], writes=[xc.r])
                    P.op("dve", lambda e, xb=xb, c=c: e.tensor_copy(out=HALO.ap[:, c, 0:3], in_=xb.ap[:, TT:TT + 3]),
                         reads=[xb.r], writes=[HALO.r])
                    P.op("act", lambda e, xc=xc, xcb=xcbs[sb_]: e.activation(out=xcb.ap, in_=xc.ap, func=AF.Copy),
                         reads=[xc.r], writes=[xcbs[sb_].r])

            def stage2(i):
                n, t = units[i]
                u = i % 2
                wsr, wsa, wsu, gaw, gxw = nst[n]
                xcs, xcbs, ggs = XC[u], XCB[u], GG[u]
                tsl = slice(t * TT, (t + 1) * TT)
                for ds_ in range(2):
                    d = 2 * n + ds_
                    gp = self.rot("ps_g", 2)
                    pr, pi = self.PS[2 * gp], self.PS[2 * gp + 1]

                    def gmm(e, ds_=ds_, gaw=gaw, gxw=gxw, pr=pr, pi=pi, x0=xcbs[0], x1=xcbs[1]):
                        e.matmul(pr.ap, gaw.ap[:, 0, 128 * ds_:128 * ds_ + 128], x0.ap, start=True, stop=False)
                        e.matmul(pr.ap, gaw.ap[:, 1, 128 * ds_:128 * ds_ + 128], x1.ap, start=False, stop=True)
                        e.matmul(pi.ap, gxw.ap[:, 0, 128 * ds_:128 * ds_ + 128], x0.ap, start=True, stop=False)
                        return e.matmul(pi.ap, gxw.ap[:, 1, 128 * ds_:128 * ds_ + 128], x1.ap, start=False, stop=True)
                    P.op("pe", gmm, reads=[gaw.r, gxw.r, xcbs[0].r, xcbs[1].r], writes=[pr.r, pi.r])
                    ri = self.rot("rt", 2)
                    rt, it, aa, hso = RT[ri], IT[ri], AA[ri], HSOs[ri]
                    xc = xcs[ds_]
                    P.op("act", lambda e, rt=rt, d=d, pr=pr: e.activation(out=rt.ap, in_=pr.ap, func=AF.Sigmoid, bias=LV.ap[:, 1, d:d + 1], scale=1.0),
                         reads=[pr.r, LV.r], writes=[rt.r])
                    P.op("act", lambda e, it=it, d=d, pi=pi: e.activation(out=it.ap, in_=pi.ap, func=AF.Sigmoid, bias=LV.ap[:, 2, d:d + 1], scale=1.0),
                         reads=[pi.r, LV.r], writes=[it.r])
                    P.op("act", lambda e, rt=rt, d=d, aa=aa: e.activation(out=aa.ap, in_=rt.ap, func=AF.Exp, scale=LC.ap[:, 2, d:d + 1]),
                         reads=[rt.r, LC.r], writes=[aa.r])
                    P.op("act", lambda e, rt=rt, d=d: e.activation(out=rt.ap, in_=rt.ap, func=AF.Exp, scale=LC.ap[:, 3, d:d + 1]),
                         reads=[LC.r], writes=[rt.r])
                    P.op("act", lambda e, rt=rt: e.activation(out=rt.ap, in_=rt.ap, func=AF.Sqrt, scale=-1.0, bias=1.0), writes=[rt.r])
                    P.op("dve", lambda e, it=it, xc=xc: e.tensor_tensor(out=xc.ap, in0=xc.ap, in1=it.ap, op=ALU.mult),
                         reads=[it.r], writes=[xc.r])
                    P.op("dve", lambda e, rt=rt, xc=xc: e.tensor_tensor(out=xc.ap, in0=xc.ap, in1=rt.ap, op=ALU.mult),
                         reads=[rt.r], writes=[xc.r])
                    P.op("dve", lambda e, d=d, aa=aa, xc=xc, hso=hso: e.tensor_tensor_scan(
                        out=hso.ap, data0=aa.ap, data1=xc.ap, initial=STATE.ap[:, d:d + 1], op0=ALU.mult, op1=ALU.add),
                        reads=[aa.r, xc.r, STATE.r], writes=[hso.r])
                    P.op("dve", lambda e, d=d, hso=hso: e.tensor_copy(out=STATE.ap[:, d:d + 1], in_=hso.ap[:, TT - 1:TT]),
                         reads=[hso.r], writes=[STATE.r])
                    if not state_only:
                        P.op("dve", lambda e, d=d, hso=hso, gg=ggs[ds_], tsl=tsl: e.tensor_tensor(
                            out=self.GT.ap[:, d, tsl], in0=hso.ap, in1=gg.ap, op=ALU.mult),
                            reads=[hso.r, ggs[ds_].r], writes=[self.GT.r])

            stage1(0)
            for i in range(len(units)):
                if i + 1 < len(units):
                    stage1(i + 1)
                self.drain(1)
                stage2(i)
            self.drain()

        if self.paired:
            for b in range(NB):
                lru_block(b, True)
                self.prefetch_ht()
            P.ss = True
            LX = Buf(self.sb("LX", [128, 128], F32)[:], "LX")
            PM2 = Buf(self.sb("PM2", [128, 1], F32)[:], "PM2")
            halo_flat = HALO.ap.rearrange("p a b -> p (a b)")
            P.op("sp", lambda e: e.dma_start(out=PM2.ap, in_=self.pmask), writes=[PM2.r], dma_key="PM2")
            P.op("dve", lambda e: e.memset(LX.ap, 0.0), writes=[LX.r])
            P.op("dve", lambda e: e.tensor_copy(out=LX.ap[:, 0:KR], in_=STATE.ap), reads=[STATE.r], writes=[LX.r])
            P.op("dve", lambda e: e.tensor_copy(out=LX.ap[:, KR:KR + 4 * KR], in_=halo_flat), reads=[HALO.r], writes=[LX.r])
            P.op("sp", lambda e: e.dma_start(out=self.LIN, in_=LX.ap), reads=[LX.r], writes=[self.dres("LIN")], dma_key="LX")
            P.op("pool", lambda e: e.collective_compute("AllGather", ALU.bypass, replica_groups=[[0, 1], [2, 3], [4, 5], [6, 7]],
                                                        ins=[self.LIN], outs=[self.LOUT]),
                 reads=[self.dres("LIN")], writes=[self.dres("LOUT")], dma_key="ccL", cc=True)
            P.op("sp", lambda e: e.dma_start(out=LX.ap, in_=self.LOUT[0:128, :]), reads=[self.dres("LOUT")], writes=[LX.r], dma_key="LX")
            P.op("dve", lambda e: e.tensor_scalar(out=STATE.ap, in0=LX.ap[:, 0:KR], scalar1=PM2.ap[:, 0:1], scalar2=None, op0=ALU.mult),
                 reads=[LX.r, PM2.r], writes=[STATE.r])
            P.op("dve", lambda e: e.tensor_scalar(out=halo_flat, in0=LX.ap[:, KR:KR + 4 * KR], scalar1=PM2.ap[:, 0:1], scalar2=None, op0=ALU.mult),
                 reads=[LX.r, PM2.r], writes=[HALO.r])
        for b in range(NB):
            lru_block(b, False)
            self.prefetch_ht()
            self.out_proj(b, w_out, KR, layer, sub, x_src, x_res_fn, nxt, last)
        self.fence([x.r for x in scratch] + [self.GT.r], [self.GT.r])

    def build(self):
        self.alloc()
        self.stage_consts()
        layers = sorted(set(l for l, s in self.sub_list))
        for l in layers:
            self.stage_ada(l)
        l0, s0 = self.sub_list[0]
        self.stage_prologue(l0, s0)
        self.ht_plan = []
        for idx_, (l, s_) in enumerate(self.sub_list):
            rep = 2 if (s_ == 1 and l % 2 == 1 and self.paired) else 1
            self.ht_plan += [(idx_, b_) for b_ in range(self.NB)] * rep
        self.hs_ver = {b_: -1 for b_ in range(self.NB)}
        self.cur_idx = 0
        self.ht_pos = 0
        self.ht_loaded = 0
        for idx, (l, s) in enumerate(self.sub_list):
            self.cur_idx = idx
            first = idx == 0
            last = idx == len(self.sub_list) - 1
            nxt = None if last else self.sub_list[idx + 1]
            x_src = self.xT if first else self.XS
            x_res_fn = (lambda m, c0: self.dres("XIN")) if first else (lambda m, c0: self.dres(f"XS{m}_{c0}"))
            if s in (0, 2):
                self.stage_ffn(l, s, x_src, x_res_fn, nxt, last)
            elif l % 2 == 0:
                self.stage_attn(l, s, x_src, x_res_fn, nxt, last)
            else:
                self.stage_lru(l, s, x_src, x_res_fn, nxt, last)
        self.drain()
        self.P.finish([r for n, r in self._dres.items() if n.startswith("OUT")])
        self.st.close()
        return self.nc


def prep_inputs(inputs, b, t0, NT, half=0, paired=False):
    f = np.float32
    x = np.asarray(inputs["x"], f)[b, t0:t0 + NT]
    m = {}
    m["xT"] = np.ascontiguousarray(x.T.reshape(KD, 128, NT))
    m["pmask"] = np.full((128, 1), float(half), f)
    m["cT"] = np.ascontiguousarray(np.asarray(inputs["c"], f)[b].reshape(KD, 128).T)
    adb = np.asarray(inputs["ada_b"], f).reshape(DEPTH, 144, 128).transpose(2, 0, 1)
    if paired:
        for l in range(DEPTH):
            m[f"ada_w_{l}"] = np.ascontiguousarray(np.asarray(inputs["ada_w"], f)[l][:, half * 9216:(half + 1) * 9216])
        m["ada_bT"] = np.ascontiguousarray(adb[:, :, half * 72:(half + 1) * 72])
    else:
        for l in range(DEPTH):
            m[f"ada_w_{l}"] = np.asarray(inputs["ada_w"], f)[l]
        m["ada_bT"] = np.ascontiguousarray(adb)
    ln = np.stack([np.asarray(inputs["ln_g"], f), np.asarray(inputs["ln_b"], f)], axis=2)
    m["lnT"] = np.ascontiguousarray(ln.reshape(DEPTH, 3, 2, KD, 128).transpose(4, 0, 1, 2, 3))
    for l in range(DEPTH):
        for wi in range(2):
            m[f"ffn_w_in_{l}_{wi}"] = np.asarray(inputs["ffn_w_in"], f)[l, wi]
            m[f"ffn_w_out_{l}_{wi}"] = np.asarray(inputs["ffn_w_out"], f)[l, wi]
    m["attn_w_qkv"] = np.asarray(inputs["attn_w_qkv"], f)[0]
    m["attn_w_o"] = np.asarray(inputs["attn_w_o"], f)[0]
    lam = np.stack([np.asarray(inputs[k], f)[0] for k in
                    ("attn_lambda_q1", "attn_lambda_k1", "attn_lambda_q2", "attn_lambda_k2")])
    m["lamv"] = np.ascontiguousarray(np.broadcast_to(lam[None], (128, 4, 128)))
    m["sublnT"] = np.ascontiguousarray(np.asarray(inputs["attn_subln_g"], f)[0].reshape(2, 128).T)
    m["lru_w_in"] = np.asarray(inputs["lru_w_in"], f)[0]
    m["lru_w_out"] = np.asarray(inputs["lru_w_out"], f)[0]
    m["lru_ga"] = np.asarray(inputs["lru_gate_a_w"], f)[0]
    m["lru_gx"] = np.asarray(inputs["lru_gate_x_w"], f)[0]
    m["cwT"] = np.ascontiguousarray(np.asarray(inputs["lru_conv_w"], f)[0].reshape(4, KR, 128).transpose(2, 1, 0))
    lv = np.stack([np.asarray(inputs[k], f)[0] for k in
                   ("lru_conv_b", "lru_gate_a_b", "lru_gate_x_b", "lru_lambda")])
    m["lruv"] = np.ascontiguousarray(lv.reshape(4, KR, 128).transpose(2, 0, 1))
    return m


FULL_SUBS = [(0, 0), (0, 1), (0, 2), (1, 0), (1, 1), (1, 2)]


def run(inputs, sub_list=FULL_SUBS, NT=2048, n_cores=8):
    mk = MK(NT, sub_list, paired=(NT == 2048))
    nc = mk.build()
    B, S = 4, 4096
    per_seq = S // NT
    in_maps = []
    for c in range(n_cores):
        b, h = c // per_seq, c % per_seq
        full = prep_inputs(inputs, b, h * NT, NT, half=h, paired=mk.paired)
        in_maps.append({k: full[k] for k in mk.in_names})
    res = run_bass_kernel_spmd(nc, in_maps, core_ids=list(range(n_cores)))
    out = np.empty((B, S, D), np.float32)
    for c in range(n_cores):
        b, h = c // per_seq, c % per_seq
        o = res.results[c]["outT"].reshape(D, NT)
        out[b, h * NT:(h + 1) * NT] = o.T
    return out


def kernel(**inputs):
    return run(inputs)
```
